# Optimizing a Trainium2 kernel written in Bass

```python
import jax
import jax.numpy as jnp
from jax import lax
import numpy as np

D_MODEL = 4096
BATCH = 32
SEQ = 256
DEPTH = 2
DEC_BATCH = 4
DEC_SEQ = 2048
PAST_LEN = 256

GRID_W = 64
D_INNER = D_MODEL
N_EVEN = (DEPTH + 1) // 2
N_ODD = DEPTH // 2
D_FOURIER = D_INNER // 2
FOURIER_GROUPS = 4
FOURIER_CG = D_FOURIER // FOURIER_GROUPS
D_POOL = D_INNER - D_FOURIER
POOL_WINDOWS = (2, 4, 8, 16)
POOL_GROUPS = len(POOL_WINDOWS)
POOL_CG = D_POOL // POOL_GROUPS
EVEN_IN = 2 * D_INNER
D_ATTN = D_INNER // 2
HEAD_DIM = 128
N_HEADS = D_ATTN // HEAD_DIM
D_CONV = D_INNER - D_ATTN
CONV_WIDTH = 31
NA_KH = 8
NA_KW = 16
NA_QC = 16
NA_KSPAN = 2 * NA_KW
ODD_IN = 3 * D_ATTN + 2 * D_CONV + D_INNER
EPS = 1e-6
NEG_INF = -1e30

kernel_name = "hybrid_flow_backbone_step"


def rmsnorm(x, g):
    xf = x.astype(jnp.float32)
    y = xf * lax.rsqrt(jnp.mean(xf * xf, axis=-1, keepdims=True) + EPS)
    return (y * g.astype(jnp.float32)).astype(x.dtype)


def layernorm(x, g, b):
    xf = x.astype(jnp.float32)
    mu = jnp.mean(xf, axis=-1, keepdims=True)
    xc = xf - mu
    y = xc * lax.rsqrt(jnp.mean(xc * xc, axis=-1, keepdims=True) + EPS)
    return (y * g.astype(jnp.float32) + b.astype(jnp.float32)).astype(x.dtype)


def adaln(cond, w, b):
    mod = (jax.nn.silu(cond) @ w + b)[:, None, :]
    return jnp.split(mod, 3, axis=-1)


def fourier_mixer(u, w):
    B, L, _ = u.shape
    ug = u.reshape(B, L, FOURIER_GROUPS, FOURIER_CG).astype(jnp.float32)
    f = jnp.fft.fftn(ug, axes=(1, 3), norm="ortho").real.astype(u.dtype)
    return jnp.einsum("blgc,gcd->blgd", f, w).reshape(B, L, D_FOURIER)


def pool_mixer(u, w, scale):
    B, L, _ = u.shape
    ug = u.reshape(B, L, POOL_GROUPS, POOL_CG)
    csum = jnp.pad(jnp.cumsum(ug.astype(jnp.float32), axis=1), ((0, 0), (1, 0), (0, 0), (0, 0)))
    t = np.arange(L)
    means = []
    for g, win in enumerate(POOL_WINDOWS):
        lo = np.maximum(t - win // 2, 0)
        hi = np.minimum(t + win // 2, L)
        cnt = (hi - lo).astype(np.float32)
        means.append((csum[:, hi, g] - csum[:, lo, g]) / cnt[None, :, None])
    d = (jnp.stack(means, axis=2) - ug.astype(jnp.float32)).astype(u.dtype)
    y = jnp.einsum("blgc,gcd->blgd", d, w).reshape(B, L, D_POOL)
    return y * scale


def conv_module(u, dw, dw_b, ln_g, ln_b, w_pw):
    a, b = jnp.split(u, 2, axis=-1)
    h = a * jax.nn.sigmoid(b)
    h = lax.conv_general_dilated(h, dw[:, None, :], (1,), [(CONV_WIDTH // 2, CONV_WIDTH // 2)],
                                 dimension_numbers=("NWC", "WIO", "NWC"),
                                 feature_group_count=D_CONV) + dw_b
    h = jax.nn.silu(layernorm(h, ln_g, ln_b))
    return h @ w_pw


def split_heads(u):
    B, L, _ = u.shape
    return u.reshape(B, L, N_HEADS, HEAD_DIM).transpose(0, 2, 1, 3)


def merge_heads(o):
    B, H, L, HD = o.shape
    return o.transpose(0, 2, 1, 3).reshape(B, L, H * HD)


def context_attention(q, k, v):
    s = jnp.einsum("bhqd,bhkd->bhqk", q, k).astype(jnp.float32) * HEAD_DIM ** -0.5
    p = jax.nn.softmax(s, axis=-1).astype(v.dtype)
    return jnp.einsum("bhqk,bhkd->bhqd", p, v)


def neighbourhood_attention(q, k, v, k_ctx, v_ctx, rel_bias):
    B, H, L, HD = q.shape
    rows = L // GRID_W
    kh = min(NA_KH, rows)
    nb = GRID_W // NA_QC
    qg = q.reshape(B, H, rows, GRID_W, HD)
    kg = k.reshape(B, H, rows, GRID_W, HD)
    vg = v.reshape(B, H, rows, GRID_W, HD)
    q_cols = np.arange(GRID_W).reshape(nb, NA_QC)
    q_start = np.clip(q_cols - NA_KW // 2, 0, GRID_W - NA_KW)
    blk_start = np.minimum(q_start[:, 0], GRID_W - NA_KSPAN)
    key_cols = blk_start[:, None] + np.arange(NA_KSPAN)[None, :]
    kc = key_cols[:, None, :]
    col_ok = (kc >= q_start[..., None]) & (kc < q_start[..., None] + NA_KW)
    dc_idx = np.clip(kc - q_cols[..., None] + NA_KW - 1, 0, 2 * NA_KW - 2)
    bias_c = rel_bias[:, :, dc_idx].astype(jnp.float32)
    scale = HEAD_DIM ** -0.5
    n_loc = kh * NA_KSPAN

    def one_row(r):
        start = jnp.clip(r - kh // 2, 0, rows - kh)
        qr = lax.dynamic_index_in_dim(qg, r, axis=2, keepdims=False).reshape(B, H, nb, NA_QC, HD)
        kr = lax.dynamic_slice_in_dim(kg, start, kh, axis=2)[:, :, :, key_cols]
        vr = lax.dynamic_slice_in_dim(vg, start, kh, axis=2)[:, :, :, key_cols]
        dr_idx = start + jnp.arange(kh) - r + NA_KH - 1
        bias = jnp.take(bias_c, dr_idx, axis=1).transpose(0, 2, 3, 1, 4)
        s_loc = jnp.einsum("bhnqd,bhrnkd->bhnqrk", qr, kr).astype(jnp.float32) * scale + bias[None]
        s_loc = jnp.where(col_ok[:, :, None, :], s_loc, NEG_INF).reshape(B, H, nb, NA_QC, n_loc)
        s_ctx = jnp.einsum("bhnqd,bhkd->bhnqk", qr, k_ctx).astype(jnp.float32) * scale
        p = jax.nn.softmax(jnp.concatenate([s_loc, s_ctx], axis=-1), axis=-1).astype(v.dtype)
        p_loc = p[..., :n_loc].reshape(B, H, nb, NA_QC, kh, NA_KSPAN)
        p_ctx = p[..., n_loc:]
        o = (jnp.einsum("bhnqrk,bhrnkd->bhnqd", p_loc, vr)
             + jnp.einsum("bhnqk,bhkd->bhnqd", p_ctx, v_ctx))
        return o.reshape(B, H, GRID_W, HD)

    out = lax.map(one_row, jnp.arange(rows))
    return out.transpose(1, 2, 0, 3, 4).reshape(B, H, L, HD)


def run_trunk(x, cond, cache_k, cache_v, norm_g, w_ada, b_ada, w_in_even, w_out_even, w_fourier,
              w_pool, pool_scale, w_in_odd, w_out_odd, q_norm_g, k_norm_g, rel_bias, conv_dw,
              conv_dw_b, conv_ln_g, conv_ln_b, w_conv_pw):
    is_context = cache_k is None
    new_k, new_v = [], []
    for i in range(DEPTH):
        shift, scale, gate = adaln(cond, w_ada[i], b_ada[i])
        h = rmsnorm(x, norm_g[i]) * (1 + scale) + shift
        j = i // 2
        if i % 2 == 0:
            u = h @ w_in_even[j]
            a_in, b_in, gp = jnp.split(u, [D_FOURIER, D_INNER], axis=-1)
            y = jnp.concatenate([fourier_mixer(a_in, w_fourier[j]),
                                 pool_mixer(b_in, w_pool[j], pool_scale[j])], axis=-1)
            y = (y * jax.nn.silu(gp)) @ w_out_even[j]
        else:
            u = h @ w_in_odd[j]
            q_in, k_in, v_in, c_in, gp = jnp.split(
                u, [D_ATTN, 2 * D_ATTN, 3 * D_ATTN, 3 * D_ATTN + 2 * D_CONV], axis=-1)
            q = rmsnorm(split_heads(q_in), q_norm_g[j])
            k = rmsnorm(split_heads(k_in), k_norm_g[j])
            v = split_heads(v_in)
            if is_context:
                o = context_attention(q, k, v)
                new_k.append(k)
                new_v.append(v)
            else:
                o = neighbourhood_attention(q, k, v, cache_k[:, j], cache_v[:, j], rel_bias[j])
            yd = conv_module(c_in, conv_dw[j], conv_dw_b[j], conv_ln_g[j], conv_ln_b[j], w_conv_pw[j])
            y = jnp.concatenate([merge_heads(o), yd], axis=-1)
            y = (y * jax.nn.silu(gp)) @ w_out_odd[j]
        x = x + gate * y
    return x, new_k, new_v


def setup_inputs(seed: int = 0) -> dict:
    key = jax.random.key(seed)
    ks = jax.random.split(key, 24)

    def nrm(k, shape, s):
        return jax.random.normal(k, shape, jnp.float32) * s

    return {
        "x_prompt": nrm(ks[0], (BATCH, SEQ, D_MODEL), 1.0),
        "x_sample": nrm(ks[1], (DEC_BATCH, DEC_SEQ, D_MODEL), 1.0),
        "cache_k": nrm(ks[2], (DEC_BATCH, N_ODD, N_HEADS, PAST_LEN, HEAD_DIM), 1.0),
        "cache_v": nrm(ks[3], (DEC_BATCH, N_ODD, N_HEADS, PAST_LEN, HEAD_DIM), 1.0),
        "c": nrm(ks[4], (DEC_BATCH, D_MODEL), 1.0),
        "c_ctx": nrm(ks[5], (D_MODEL,), 1.0),
        "norm_g": 1.0 + nrm(ks[6], (DEPTH, D_MODEL), 0.05),
        "w_ada": nrm(ks[7], (DEPTH, D_MODEL, 3 * D_MODEL), 0.5 * D_MODEL ** -0.5),
        "b_ada": nrm(ks[8], (DEPTH, 3 * D_MODEL), 0.02),
        "w_in_even": nrm(ks[9], (N_EVEN, D_MODEL, EVEN_IN), D_MODEL ** -0.5),
        "w_out_even": nrm(ks[10], (N_EVEN, D_INNER, D_MODEL), D_INNER ** -0.5),
        "w_fourier": nrm(ks[11], (N_EVEN, FOURIER_GROUPS, FOURIER_CG, FOURIER_CG), FOURIER_CG ** -0.5),
        "w_pool": nrm(ks[12], (N_EVEN, POOL_GROUPS, POOL_CG, POOL_CG), POOL_CG ** -0.5),
        "pool_scale": 1.0 + nrm(ks[13], (N_EVEN, D_POOL), 0.05),
        "w_in_odd": nrm(ks[14], (N_ODD, D_MODEL, ODD_IN), D_MODEL ** -0.5),
        "w_out_odd": nrm(ks[15], (N_ODD, D_INNER, D_MODEL), D_INNER ** -0.5),
        "q_norm_g": 1.0 + nrm(ks[16], (N_ODD, HEAD_DIM), 0.05),
        "k_norm_g": 1.0 + nrm(ks[17], (N_ODD, HEAD_DIM), 0.05),
        "rel_bias": nrm(ks[18], (N_ODD, N_HEADS, 2 * NA_KH - 1, 2 * NA_KW - 1), 0.1),
        "conv_dw": nrm(ks[19], (N_ODD, CONV_WIDTH, D_CONV), CONV_WIDTH ** -0.5),
        "conv_dw_b": nrm(ks[20], (N_ODD, D_CONV), 0.02),
        "conv_ln_g": 1.0 + nrm(ks[21], (N_ODD, D_CONV), 0.05),
        "conv_ln_b": nrm(ks[22], (N_ODD, D_CONV), 0.02),
        "w_conv_pw": nrm(ks[23], (N_ODD, D_CONV, D_CONV), D_CONV ** -0.5),
    }


def reference(x_prompt, x_sample, cache_k, cache_v, c, c_ctx, norm_g, w_ada, b_ada, w_in_even,
              w_out_even, w_fourier, w_pool, pool_scale, w_in_odd, w_out_odd, q_norm_g, k_norm_g,
              rel_bias, conv_dw, conv_dw_b, conv_ln_g, conv_ln_b, w_conv_pw):
    weights = (norm_g, w_ada, b_ada, w_in_even, w_out_even, w_fourier, w_pool, pool_scale,
               w_in_odd, w_out_odd, q_norm_g, k_norm_g, rel_bias, conv_dw, conv_dw_b,
               conv_ln_g, conv_ln_b, w_conv_pw)
    y_prompt, ks_new, vs_new = run_trunk(x_prompt, c_ctx[None, :], None, None, *weights)
    state_k = jnp.stack(ks_new, axis=1)
    state_v = jnp.stack(vs_new, axis=1)
    y_sample, _, _ = run_trunk(x_sample, c, cache_k, cache_v, *weights)
    return (y_prompt, y_sample, state_k, state_v)
```

```python
import numpy as np
import ml_dtypes
import concourse.bass as bass
import concourse.mybir as mybir
from concourse.bass_utils import run_bass_kernel_spmd

F32 = mybir.dt.float32
BF16 = mybir.dt.bfloat16
AF = mybir.ActivationFunctionType
ALU = mybir.AluOpType
NPBF = ml_dtypes.bfloat16

T = 2048
D = 4096
NT = 16
EPS = 1e-6
NEG = -30000.0
ENG = ("sp", "act", "pool", "dve", "pe")


class Prog:
    def __init__(self, nc, eng_sems, dma_sems):
        self.nc = nc
        self.streams = {e: [] for e in ENG}
        self.eng_sem = eng_sems
        self.dma_pool = list(dma_sems)
        self.cnt = {}
        self.semobj = {}
        for s in list(eng_sems.values()) + list(dma_sems):
            self.cnt[id(s)] = 0
            self.semobj[id(s)] = s
        self.waited = {e: {} for e in ENG}
        self.last_w = {}
        self.readers = {}
        self.dma_map = {}
        self.dma_next = 0

    def _dma_sem(self, key):
        if key not in self.dma_map:
            assert self.dma_next < len(self.dma_pool), "out of dma sems"
            self.dma_map[key] = self.dma_pool[self.dma_next]
            self.dma_next += 1
        return self.dma_map[key]

    def _wait(self, eng, ev):
        sid, val = ev
        if self.waited[eng].get(sid, 0) < val:
            self.streams[eng].append(("wait", sid, val))
            self.waited[eng][sid] = val

    def op(self, eng, fn, reads=(), writes=(), dma_key=None):
        reads = list(reads)
        writes = list(writes)
        deps = set()
        for k in reads:
            if k in self.last_w:
                deps.add(self.last_w[k])
            if isinstance(k, tuple) and k[0] == "ps" and k not in writes:
                for (e2, r) in self.readers.get(k, ()):
                    if e2 != eng:
                        deps.add(r)
        for k in writes:
            if k in self.last_w:
                deps.add(self.last_w[k])
            for (e2, r) in self.readers.get(k, ()):
                deps.add(r)
        mx = {}
        for sid, val in deps:
            mx[sid] = max(mx.get(sid, 0), val)
        for sid in sorted(mx):
            self._wait(eng, (sid, mx[sid]))
        if dma_key is not None:
            sem = self._dma_sem(dma_key)
            inc = 16
        else:
            sem = self.eng_sem[eng]
            inc = 1
        self.cnt[id(sem)] += inc
        ev = (id(sem), self.cnt[id(sem)])
        self.streams[eng].append(("op", fn, id(sem), inc))
        for k in writes:
            self.last_w[k] = ev
            self.readers[k] = []
        for k in reads:
            if k not in writes:
                self.readers.setdefault(k, []).append((eng, ev))
        return ev

    def barrier(self):
        for e in ENG:
            for sid, c in self.cnt.items():
                if c > 0:
                    self._wait(e, (sid, c))
        self.last_w = {}
        self.readers = {}
        self.dma_map = {}
        self.dma_next = 0

    def final_wait(self, eng="sp"):
        for sid, c in self.cnt.items():
            if c > 0:
                self._wait(eng, (sid, c))

    def emit(self):
        nc = self.nc
        with nc.Block() as block:
            def replay(name, e):
                for it in self.streams[name]:
                    if it[0] == "wait":
                        e.wait_ge(self.semobj[it[1]], it[2])
                    else:
                        ins = it[1](e)
                        ins.then_inc(self.semobj[it[2]], it[3])

            @block.sync
            def _(e):
                replay("sp", e)

            @block.scalar
            def _(e):
                replay("act", e)

            @block.gpsimd
            def _(e):
                replay("pool", e)

            @block.vector
            def _(e):
                replay("dve", e)

            @block.tensor
            def _(e):
                replay("pe", e)


class Arena:
    def __init__(self, big, nbytes):
        self.big = big
        self.n = nbytes
        self.off = 0
        self.marks = []

    def mark(self):
        self.marks.append(self.off)

    def release(self):
        self.off = self.marks.pop()

    def alloc(self, shape_free, dtype):
        esz = 4 if dtype == F32 else 2
        n = int(np.prod(shape_free))
        nb = n * esz
        self.off = (self.off + 63) // 64 * 64
        assert self.off + nb <= self.n, f"SBUF arena overflow {self.off}+{nb}>{self.n}"
        ap = self.big[:, self.off // 2:(self.off + nb) // 2]
        self.off += nb
        if dtype == F32:
            ap = ap.bitcast(F32)
        if len(shape_free) == 2:
            ap = ap.rearrange("p (a b) -> p a b", a=shape_free[0])
        elif len(shape_free) == 3:
            ap = ap.rearrange("p (a b c) -> p a b c", a=shape_free[0], b=shape_free[1])
        elif len(shape_free) == 4:
            ap = ap.rearrange("p (a b c d) -> p a b c d", a=shape_free[0], b=shape_free[1], c=shape_free[2])
        return ap


DBG = {}
ALL_PHASES = ("adaln", "l0_norm", "l0_in", "l0_mix", "l0_out",
              "l1_norm", "l1_in", "l1_attn", "l1_conv", "l1_out")


def build(phases=ALL_PHASES, ext_in=(), ext_out=()):
    nc = bass.Bass("TRN2", target_bir_lowering=False)

    def din(name, shape, dt=F32):
        return nc.dram_tensor(name, list(shape), dt, kind="ExternalInput").ap()

    def dout(name, shape, dt=F32):
        return nc.dram_tensor(name, list(shape), dt, kind="ExternalOutput").ap()

    def dscr(name, shape, dt):
        kind = "ExternalInput" if name in ext_in else ("ExternalOutput" if name in ext_out else "Internal")
        return nc.dram_tensor(name, list(shape), dt, kind=kind).ap()

    has = lambda p: p in phases
    L0 = any(p.startswith("l0") for p in phases)
    L1 = any(p.startswith("l1") for p in phases)

    ident_d = din("ident", [128, 128], BF16)
    ng_l = din("ng_l", [2, 128, 32])
    if has("adaln"):
        cond_l = din("cond_l", [128, 32])
        w_ada = din("w_ada", [2, D, 3 * D])
        b_ada = din("b_ada", [2, 3 * D])
    if has("l0_norm") or has("l0_out"):
        x_d = din("x", [T, D])
    if has("l0_in"):
        w_in0 = din("w_in_even", [D, 2 * D])
    if has("l0_mix"):
        w_fou = din("w_fourier", [4, 512, 512])
        w_pool = din("w_pool", [4, 512, 512])
        pscale_l = din("pscale_l", [128, 16])
        cc_m = din("cc_m", [512, 512], BF16)
        sc_m = din("sc_m", [512, 512], BF16)
        cl_m = din("cl_m", [T, T], BF16)
        sl_m = din("sl_m", [T, T], BF16)
        band_m = din("band_m", [4, 4, 6, 128, 512], BF16)
    if has("l0_out"):
        w_out0 = din("w_out_even", [D, D])
    if has("l1_in"):
        w_in1 = din("w_in_odd", [D, 14336])
        qg_d = din("q_norm_g", [128])
        kg_d = din("k_norm_g", [128])
    if has("l1_attn"):
        bias_t = din("bias_t", [16, 128, 1152])
        maskadd = din("maskadd", [128, 6, 640])
        ctxbias_d = din("ctxbias", [128, 1])
        cache_k = din("cache_k", [16, 256, 128])
        cache_v = din("cache_v", [16, 256, 128])
    if has("l1_conv"):
        flag_d = din("flag", [128, 1])
        dw_l = din("dw_l", [128, 16, 31])
        dwb_l = din("dwb_l", [128, 16])
        lng_l = din("lng_l", [128, 16])
        lnb_l = din("lnb_l", [128, 16])
        w_pw = din("w_conv_pw", [2048, 2048])
    if has("l1_out"):
        w_out1 = din("w_out_odd", [D, D])

    if has("l1_out"):
        y_d = dout("y", [T, D])
    if has("l1_in"):
        sk_d = dout("sk", [16, T, 128])
        sv_d = dout("sv", [16, T, 128])

    mod_d = dscr("mod", [2, 3 * D], F32)
    if L0:
        A0 = dscr("A0", [4, 512, T], BF16)
        B0 = dscr("B0", [4, T, 512], BF16)
    SG = dscr("SG", [D, T], BF16)
    YG = dscr("YG", [D, T], BF16)
    X1 = dscr("X1", [T, D], F32)
    if L1:
        Qs = dscr("Qs", [T, 2048], BF16)
        Ks = dscr("Ks", [T, 2048], BF16)
        Vs = dscr("Vs", [T, 2048], BF16)
        Gs = dscr("Gs", [2048, T + 30], BF16)

    ARENA = 206 * 1024
    big = nc.alloc_sbuf_tensor("big", [128, ARENA // 2], BF16)
    ps = nc.alloc_psum_tensor("ps", [128, 8, 512], F32)
    sems = {e: nc.alloc_semaphore(f"s_{e}") for e in ENG}
    dsems = [nc.alloc_semaphore(f"d_{i}") for i in range(40)]
    P = Prog(nc, sems, dsems)
    A = Arena(big, ARENA)

    def psb(bank):
        return ps[:, bank, :].bitcast(BF16)

    ident = A.alloc([128], BF16)
    onesb = A.alloc([128], BF16)
    onesf = A.alloc([128], F32)
    s_bf = A.alloc([32], BF16)
    P.op("sp", lambda e: e.dma_start(out=ident, in_=ident_d), writes=["ident"], dma_key="ident")
    P.op("dve", lambda e: e.memset(onesb, 1.0), writes=["onesb"])
    P.op("dve", lambda e: e.memset(onesf, 1.0), writes=["onesf"])
    P.barrier()

    def wblock(W, col, n=256):
        return W[:, col:col + n].rearrange("(c p) n -> p c n", p=128)

    ADA_LAYERS = (0, 1) if DBG.get("ada_serial") or not has("l0_in") else (0,)

    def phase_adaln():
        A.mark()
        condt = A.alloc([32], F32)
        modrow = A.alloc([3 * D], F32)
        wb = [A.alloc([32, 512], BF16) for _ in range(3)]
        P.op("sp", lambda e: e.dma_start(out=condt, in_=cond_l), writes=["cond"], dma_key="cond")
        P.op("act", lambda e: e.activation(out=s_bf, in_=condt, func=AF.Silu), reads=["cond"], writes=["s"])
        blk = 0
        for layer in ADA_LAYERS:
            mkeys = [("modrow", nb) for nb in range(24)]
            P.op("sp", lambda e, layer=layer: e.dma_start(out=modrow[0:1, :], in_=b_ada[layer:layer + 1, :]),
                 writes=mkeys, dma_key="modrow")
            for nb in range(24):
                slot = blk % 3
                bank = blk % 8
                blk += 1
                P.op("pool", lambda e, slot=slot, layer=layer, nb=nb: e.dma_start(
                    out=wb[slot], in_=wblock(w_ada[layer], nb * 512, 512)),
                    writes=[("wb", slot)], dma_key=("wb", slot))

                def mm(e, slot=slot, bank=bank):
                    for c in range(32):
                        ins = e.matmul(ps[0:1, bank, :], s_bf[:, c:c + 1], wb[slot][:, c, :],
                                       start=(c == 0), stop=(c == 31))
                    return ins
                P.op("pe", mm, reads=[("wb", slot), "s"], writes=[("ps", bank)])
                P.op("dve", lambda e, bank=bank, nb=nb: e.tensor_tensor(
                    out=modrow[0:1, nb * 512:(nb + 1) * 512], in0=ps[0:1, bank, :],
                    in1=modrow[0:1, nb * 512:(nb + 1) * 512], op=ALU.add),
                    reads=[("ps", bank)], writes=[("modrow", nb)])
            P.op("sp", lambda e, layer=layer: e.dma_start(out=mod_d[layer:layer + 1, :], in_=modrow[0:1, :]),
                 reads=mkeys, dma_key="modst")
        P.barrier()
        A.release()

    def phase_norm(layer, xsrc, hT):
        A.mark()
        sc = A.alloc([32], F32)
        sh = A.alloc([32], F32)
        g = A.alloc([32], F32)
        sc1 = A.alloc([32], F32)
        xt = [A.alloc([D], F32) for _ in range(2)]
        junk = A.alloc([D], BF16)
        xn = [A.alloc([D], BF16) for _ in range(2)]
        ss = [A.alloc([1], F32) for _ in range(2)]
        rstd = [A.alloc([1], F32) for _ in range(2)]
        col = lambda off: bass.AP(mod_d.tensor, layer * 3 * D + off, [[1, 128], [128, 32]])
        P.op("sp", lambda e: e.dma_start(out=sh, in_=col(0), allow_slow_non_contiguous=True), writes=["sh"], dma_key="sh")
        P.op("sp", lambda e: e.dma_start(out=sc, in_=col(D), allow_slow_non_contiguous=True), writes=["sc"], dma_key="sc")
        P.op("sp", lambda e: e.dma_start(out=g, in_=ng_l[layer]), writes=["g"], dma_key="g")
        P.op("dve", lambda e: e.scalar_tensor_tensor(out=sc1, in0=sc, scalar=1.0, in1=g, op0=ALU.add, op1=ALU.mult),
             reads=["sc", "g"], writes=["sc1"])
        def stage_a(i):
            s = i % 2
            P.op("sp", lambda e, s=s, i=i: e.dma_start(out=xt[s], in_=xsrc[i * 128:(i + 1) * 128, :]),
                 writes=[("xt", s)], dma_key=("xt", s))
            P.op("act", lambda e, s=s: e.activation(out=junk, in_=xt[s], func=AF.Square, accum_out=ss[s]),
                 reads=[("xt", s)], writes=["junk", ("ss", s)])
            P.op("act", lambda e, s=s: e.activation(out=ss[s], in_=ss[s], func=AF.Sqrt, scale=1.0 / D, bias=EPS),
                 reads=[("ss", s)], writes=[("ss", s)])
            P.op("dve", lambda e, s=s: e.reciprocal(out=rstd[s], in_=ss[s]), reads=[("ss", s)], writes=[("rstd", s)])
            P.op("act", lambda e, s=s: e.activation(out=xn[s], in_=xt[s], func=AF.Copy, scale=rstd[s]),
                 reads=[("xt", s), ("rstd", s)], writes=[("xn", s)])

        def stage_b(i):
            s = i % 2
            for grp in range(8):
                bank = (i * 8 + grp) % 8

                def tr(e, s=s, grp=grp, bank=bank):
                    for j in range(4):
                        c = grp * 4 + j
                        ins = e.transpose(out=psb(bank)[:, j * 128:(j + 1) * 128],
                                          in_=xn[s][:, c * 128:(c + 1) * 128], identity=ident)
                    return ins
                P.op("pe", tr, reads=[("xn", s), "ident"], writes=[("ps", bank)])
                for j in range(4):
                    c = grp * 4 + j
                    P.op("dve", lambda e, bank=bank, j=j, c=c, i=i: e.tensor_scalar(
                        out=hT[:, c, i * 128:(i + 1) * 128], in0=psb(bank)[:, j * 128:(j + 1) * 128],
                        scalar1=sc1[:, c:c + 1], scalar2=sh[:, c:c + 1], op0=ALU.mult, op1=ALU.add),
                        reads=[("ps", bank), "sc1", "sh"])
        stage_a(0)
        for i in range(NT):
            if i + 1 < NT:
                stage_a(i + 1)
            stage_b(i)
        P.barrier()
        A.release()

    class G:
        bset = 0
        bank = 0

    def gemm_phase(W, jobs, hT, ncol=256, nslots=3, side=None):
        A.mark()
        wb = [A.alloc([32, ncol], BF16) for _ in range(nslots)]
        for ji, (col, mode, epi) in enumerate(jobs):
            slot = ji % nslots
            P.op("pool", lambda e, slot=slot, col=col: e.dma_start(out=wb[slot], in_=wblock(W, col, ncol)),
                 writes=[("wb", slot)], dma_key=("wb", slot))
            if mode == "fm":
                for half in range(ncol // 128):
                    banks = [G.bset * 4 + tt for tt in range(4)]
                    G.bset ^= 1

                    def mm(e, slot=slot, half=half, banks=banks):
                        for c in range(32):
                            for tt in range(4):
                                ins = e.matmul(ps[:, banks[tt], :], wb[slot][:, c, half * 128:(half + 1) * 128],
                                               hT[:, c, tt * 512:(tt + 1) * 512], start=(c == 0), stop=(c == 31))
                        return ins
                    P.op("pe", mm, reads=[("wb", slot)], writes=[("ps", b) for b in banks])
                    epi(half, banks)
                    if side is not None:
                        side(G.bset * 4 + 3)
            else:
                for i in range(NT):
                    bank = G.bank
                    G.bank = (G.bank + 1) % 8

                    def mm(e, slot=slot, i=i, bank=bank):
                        for c in range(32):
                            ins = e.matmul(ps[:, bank, 0:ncol], hT[:, c, i * 128:(i + 1) * 128], wb[slot][:, c, :],
                                           start=(c == 0), stop=(c == 31))
                        return ins
                    P.op("pe", mm, reads=[("wb", slot)], writes=[("ps", bank)])
                    epi(i, bank)
        P.barrier()
        A.release()

    class St:
        pass

    def make_fm_store(stages, dst_rows, func, eng_copy="act", in1=None):
        state = {"n": 0}

        def epi(half, banks):
            s = state["n"] % len(stages)
            state["n"] += 1
            for tt in range(4):
                o = stages[s][:, tt * 512:(tt + 1) * 512]
                if in1 is not None:
                    src, key = in1(half)
                    P.op("dve", lambda e, o=o, b=banks[tt], tt=tt, src=src: e.tensor_tensor(
                        out=o, in0=ps[:, b, :], in1=src[:, tt * 512:(tt + 1) * 512], op=ALU.mult),
                        reads=[("ps", banks[tt]), key + (tt,)], writes=[("stg", id(stages), s, tt)])
                elif func is None:
                    P.op("dve", lambda e, o=o, b=banks[tt]: e.tensor_copy(out=o, in_=ps[:, b, :]),
                         reads=[("ps", banks[tt])], writes=[("stg", id(stages), s, tt)])
                else:
                    P.op("act", lambda e, o=o, b=banks[tt]: e.activation(out=o, in_=ps[:, b, :], func=func),
                         reads=[("ps", banks[tt])], writes=[("stg", id(stages), s, tt)])
            d = dst_rows(half)
            if d is not None:
                P.op("act", lambda e, s=s, d=d: e.dma_start(out=d, in_=stages[s]),
                     reads=[("stg", id(stages), s, tt) for tt in range(4)], dma_key=("stgst", id(stages), s))
        return epi

    rot = {"b": 0}

    def nbank():
        b = rot["b"]
        rot["b"] = (b + 1) % 8
        return b

    alt = {"n": 0}

    def evac(out, bank, writes, src=None):
        use_dve = src is not None
        src = ps[:, bank, :] if src is None else src
        alt["n"] += 1
        if alt["n"] % 2 or use_dve:
            P.op("dve", lambda e: e.tensor_copy(out=out, in_=src), reads=[("ps", bank)], writes=writes)
        else:
            P.op("act", lambda e: e.activation(out=out, in_=src, func=AF.Copy), reads=[("ps", bank)], writes=writes)

    def phase_l0_in(hT):
        A.mark()
        stA = [A.alloc([T], BF16) for _ in range(2)]
        stS = [A.alloc([T], BF16) for _ in range(2)]
        stB = [A.alloc([4, 256], BF16) for _ in range(2)]
        jobs = []
        for g in range(4):
            for blk in range(2):
                jobs.append((g * 512 + blk * 256, "fm", make_fm_store(
                    stA, lambda half, g=g, blk=blk: A0[g, blk * 256 + half * 128: blk * 256 + (half + 1) * 128, :], None)))
                for jj in range(2):
                    j = (g * 2 + blk) * 2 + jj
                    jobs.append((D + j * 256, "fm", make_fm_store(
                        stS, lambda half, j=j: SG[j * 256 + half * 128: j * 256 + (half + 1) * 128, :], AF.Silu)))
                stt = {"n": 0}

                def epi_b(i, bank, g=g, blk=blk, stt=stt):
                    s = (i // 4) % 2
                    P.op("dve", lambda e, s=s, i=i, bank=bank: e.tensor_copy(out=stB[s][:, i % 4, :], in_=ps[:, bank, 0:256]),
                         reads=[("ps", bank)], writes=[("stB", s, i % 4)])
                    if i % 4 == 3:
                        q = i // 4
                        d = B0[g, q * 512:(q + 1) * 512, blk * 256:(blk + 1) * 256].rearrange("(t p) c -> p t c", p=128)
                        P.op("act", lambda e, s=s, d=d: e.dma_start(out=d, in_=stB[s]),
                             reads=[("stB", s, k) for k in range(4)], dma_key=("stBst", s))
                jobs.append((2048 + g * 512 + blk * 256, "tm", epi_b))
        if DBG.get("l0_jobs"):
            jobs = [j for j in jobs if j[1] == DBG["l0_jobs"][0]][DBG["l0_jobs"][1]:DBG["l0_jobs"][2]]
        side = None
        if has("adaln") and 1 not in ADA_LAYERS:
            wa = [A.alloc([32, 256], BF16)]
            brow = [A.alloc([256], F32) for _ in range(2)]
            res = [A.alloc([256], F32) for _ in range(2)]
            sk_ = {"k": 0}

            def side(bank):
                for _ in range(1):
                    k = sk_["k"]
                    if k >= 48:
                        return
                    sk_["k"] += 1
                    q = k % 2
                    P.op("pool", lambda e, k=k: e.dma_start(out=wa[0], in_=wblock(w_ada[1], k * 256, 256)),
                         writes=[("wa", 0)], dma_key=("wa", 0))
                    P.op("sp", lambda e, q=q, k=k: e.dma_start(out=brow[q][0:1, :], in_=b_ada[1:2, k * 256:(k + 1) * 256]),
                         writes=[("brow", q)], dma_key=("brow", q))

                    def mm(e, bank=bank):
                        for c in range(32):
                            ins = e.matmul(ps[0:1, bank, 0:256], s_bf[:, c:c + 1], wa[0][:, c, :], start=(c == 0), stop=(c == 31))
                        return ins
                    P.op("pe", mm, reads=[("wa", 0)], writes=[("ps", bank)])
                    P.op("dve", lambda e, q=q, bank=bank: e.tensor_tensor(out=res[q][0:1, :], in0=ps[0:1, bank, 0:256],
                                                                        in1=brow[q][0:1, :], op=ALU.add),
                         reads=[("ps", bank), ("brow", q)], writes=[("res", q)])
                    P.op("act", lambda e, q=q, k=k: e.dma_start(out=mod_d[1:2, k * 256:(k + 1) * 256], in_=res[q][0:1, :]),
                         reads=[("res", q)], dma_key=("resst", q))
        gemm_phase(w_in0, jobs, hT, nslots=2 if side is not None else 3, side=side)
        A.release()

    def phase_l0_mix():
        A.mark()
        ccs = A.alloc([4, 512], BF16)
        scs = A.alloc([4, 512], BF16)
        psc = A.alloc([16], F32)
        P.op("sp", lambda e: e.dma_start(out=ccs, in_=cc_m.rearrange("(c p) n -> p c n", p=128)), writes=["ccs"], dma_key="ccs")
        P.op("sp", lambda e: e.dma_start(out=scs, in_=sc_m.rearrange("(c p) n -> p c n", p=128)), writes=["scs"], dma_key="scs")
        P.op("sp", lambda e: e.dma_start(out=psc, in_=pscale_l), writes=["psc"], dma_key="psc")
        sgt = [A.alloc([4, 512], BF16) for _ in range(2)]
        ygs = [A.alloc([4, 512], BF16) for _ in range(2)]
        wf = [A.alloc([4, 512], BF16) for _ in range(2)]
        fT = [A.alloc([4, 512], BF16) for _ in range(2)]
        A.mark()
        aT = [A.alloc([4, T], BF16) for _ in range(2)]
        ucs = A.alloc([2, 16, 512], BF16)
        clt = [A.alloc([2, 16, 512], BF16) for _ in range(2)]
        ucs_keys = [("ucs", m, lt) for m in range(2) for lt in range(16)]
        for g in range(4):
            sa = g % 2
            P.op("sp", lambda e, sa=sa, g=g: e.dma_start(out=aT[sa], in_=A0[g].rearrange("(c p) t -> p c t", p=128)),
                 writes=[("aT", sa)], dma_key=("aT", sa))
            P.op("pool", lambda e, sa=sa, g=g: e.dma_start(out=wf[sa], in_=w_fou[g].rearrange("(c p) n -> p c n", p=128)),
                 writes=[("wf", sa)], dma_key=("wf", sa))
            for lt in range(16):
                for m in range(2):
                    bank = nbank()
                    tw = ccs if m == 0 else scs

                    def mm(e, sa=sa, lt=lt, tw=tw, bank=bank):
                        for cc in range(4):
                            ins = e.matmul(ps[:, bank, :], aT[sa][:, cc, lt * 128:(lt + 1) * 128], tw[:, cc, :],
                                           start=(cc == 0), stop=(cc == 3))
                        return ins
                    P.op("pe", mm, reads=[("aT", sa), "ccs", "scs"], writes=[("ps", bank)])
                    evac(ucs[:, m, lt, :], bank, [("ucs", m, lt)])
            for tt in range(4):
                sc_ = (g * 4 + tt) % 2
                P.op("sp", lambda e, sc_=sc_, tt=tt: e.dma_start(
                    out=clt[sc_][:, 0], in_=cl_m[:, tt * 512:(tt + 1) * 512].rearrange("(c p) t -> p c t", p=128)),
                    writes=[("clt", sc_, 0)], dma_key=("clt", sc_, 0))
                P.op("sp", lambda e, sc_=sc_, tt=tt: e.dma_start(
                    out=clt[sc_][:, 1], in_=sl_m[:, tt * 512:(tt + 1) * 512].rearrange("(c p) t -> p c t", p=128)),
                    writes=[("clt", sc_, 1)], dma_key=("clt", sc_, 1))
                P.op("sp", lambda e, sc_=sc_, tt=tt, g=g: e.dma_start(
                    out=sgt[sc_], in_=SG[g * 512:(g + 1) * 512, tt * 512:(tt + 1) * 512].rearrange("(c p) t -> p c t", p=128)),
                    writes=[("sgt", sc_)], dma_key=("sgt", sc_))
                for ck in range(4):
                    bank = nbank()

                    def mm(e, sc_=sc_, ck=ck, bank=bank):
                        n = 0
                        for m in range(2):
                            for lc in range(16):
                                ins = e.matmul(ps[:, bank, :], ucs[:, m, lc, ck * 128:(ck + 1) * 128], clt[sc_][:, m, lc, :],
                                               start=(n == 0), stop=(n == 31))
                                n += 1
                        return ins
                    P.op("pe", mm, reads=ucs_keys + [("clt", sc_, 0), ("clt", sc_, 1)], writes=[("ps", bank)])
                    evac(fT[sc_][:, ck, :], bank, [("fT", sc_, ck)])
                for dk in range(4):
                    bank = nbank()

                    def mm(e, sa=sa, sc_=sc_, dk=dk, bank=bank):
                        for ck in range(4):
                            ins = e.matmul(ps[:, bank, :], wf[sa][:, ck, dk * 128:(dk + 1) * 128], fT[sc_][:, ck, :],
                                           start=(ck == 0), stop=(ck == 3))
                        return ins
                    P.op("pe", mm, reads=[("fT", sc_, ck) for ck in range(4)] + [("wf", sa)], writes=[("ps", bank)])
                    P.op("dve", lambda e, sc_=sc_, dk=dk, bank=bank: e.tensor_tensor(
                        out=ygs[sc_][:, dk, :], in0=ps[:, bank, :], in1=sgt[sc_][:, dk, :], op=ALU.mult),
                        reads=[("ps", bank), ("sgt", sc_)], writes=[("ygs", sc_, dk)])
                P.op("act", lambda e, sc_=sc_, g=g, tt=tt: e.dma_start(
                    out=YG[g * 512:(g + 1) * 512, tt * 512:(tt + 1) * 512].rearrange("(c p) t -> p c t", p=128), in_=ygs[sc_]),
                    reads=[("ygs", sc_, dk) for dk in range(4)], dma_key=("ygst", sc_))
        P.barrier()
        A.release()
        A.mark()
        bT = [A.alloc([16, 512], BF16) for _ in range(2)]
        bnd = [A.alloc([4, 6, 512], BF16) for _ in range(2)]
        for g in range(4):
            sa = g % 2
            P.op("sp", lambda e, sa=sa, g=g: e.dma_start(out=bT[sa], in_=B0[g].rearrange("(t p) c -> p t c", p=128)),
                 writes=[("bT", sa)], dma_key=("bT", sa))
            for a in range(4):
                P.op("sp", lambda e, sa=sa, g=g, a=a: e.dma_start(out=bnd[sa][:, a], in_=band_m[g, a].rearrange("j p t -> p j t")),
                     writes=[("bnd", sa, a)], dma_key=("bnd", sa, a))
            P.op("pool", lambda e, sa=sa, g=g: e.dma_start(out=wf[sa], in_=w_pool[g].rearrange("(c p) n -> p c n", p=128)),
                 writes=[("wf", sa)], dma_key=("wf", sa))
            for tt in range(4):
                sc_ = (g * 4 + tt) % 2
                P.op("sp", lambda e, sc_=sc_, tt=tt, g=g: e.dma_start(
                    out=sgt[sc_], in_=SG[2048 + g * 512:2048 + (g + 1) * 512, tt * 512:(tt + 1) * 512].rearrange("(c p) t -> p c t", p=128)),
                    writes=[("sgt", sc_)], dma_key=("sgt", sc_))
                for ck in range(4):
                    bank = nbank()

                    def mm(e, sa=sa, tt=tt, ck=ck, bank=bank):
                        for j in range(6):
                            lt = min(max(4 * tt - 1 + j, 0), 15)
                            ins = e.matmul(ps[:, bank, :], bT[sa][:, lt, ck * 128:(ck + 1) * 128], bnd[sa][:, tt, j, :],
                                           start=(j == 0), stop=(j == 5))
                        return ins
                    P.op("pe", mm, reads=[("bT", sa), ("bnd", sa, tt)], writes=[("ps", bank)])
                    evac(fT[sc_][:, ck, :], bank, [("fT", sc_, ck)])
                for dk in range(4):
                    bank = nbank()

                    def mm(e, sa=sa, sc_=sc_, dk=dk, bank=bank):
                        for ck in range(4):
                            ins = e.matmul(ps[:, bank, :], wf[sa][:, ck, dk * 128:(dk + 1) * 128], fT[sc_][:, ck, :],
                                           start=(ck == 0), stop=(ck == 3))
                        return ins
                    P.op("pe", mm, reads=[("fT", sc_, ck) for ck in range(4)] + [("wf", sa)], writes=[("ps", bank)])
                    P.op("dve", lambda e, sc_=sc_, dk=dk, bank=bank, g=g: e.scalar_tensor_tensor(
                        out=ygs[sc_][:, dk, :], in0=ps[:, bank, :], scalar=psc[:, g * 4 + dk:g * 4 + dk + 1],
                        in1=sgt[sc_][:, dk, :], op0=ALU.mult, op1=ALU.mult),
                        reads=[("ps", bank), ("sgt", sc_), "psc"], writes=[("ygs", sc_, dk)])
                P.op("act", lambda e, sc_=sc_, g=g, tt=tt: e.dma_start(
                    out=YG[2048 + g * 512:2048 + (g + 1) * 512, tt * 512:(tt + 1) * 512].rearrange("(c p) t -> p c t", p=128), in_=ygs[sc_]),
                    reads=[("ygs", sc_, dk) for dk in range(4)], dma_key=("ygst", sc_))
        P.barrier()
        A.release()
        A.release()

    def phase_out(W, xsrc, dst, layer, yg):
        A.mark()
        gate_b = A.alloc([D], F32)
        xs = [A.alloc([4, 256], F32) for _ in range(2)]
        os_ = [A.alloc([4, 256], F32) for _ in range(2)]
        for q in range(4):
            P.op("sp", lambda e, q=q: e.dma_start(
                out=yg[:, q * 8:(q + 1) * 8, :], in_=YG[q * 1024:(q + 1) * 1024, :].rearrange("(c p) t -> p c t", p=128)),
                dma_key=("ygld", q))
        P.op("sp", lambda e: e.dma_start(out=gate_b, in_=mod_d[layer, 2 * D:3 * D].partition_broadcast(128)), dma_key="gate_b")
        P.barrier()
        jobs = []
        cnt = {"n": 0}
        for j in range(16):
            col = j * 256

            def epi(i, bank, col=col):
                q = i // 4
                if i % 4 == 0:
                    cnt["n"] += 1
                s = cnt["n"] % 2
                if i % 4 == 0:
                    P.op("sp", lambda e, s=s, q=q, col=col: e.dma_start(
                        out=xs[s], in_=xsrc[q * 512:(q + 1) * 512, col:col + 256].rearrange("(t p) c -> p t c", p=128)),
                        writes=[("xs", s)], dma_key=("xs", s))
                P.op("dve", lambda e, s=s, i=i, bank=bank, col=col: e.tensor_tensor(
                    out=os_[s][:, i % 4, :], in0=ps[:, bank, 0:256], in1=gate_b[:, col:col + 256], op=ALU.mult),
                    reads=[("ps", bank)], writes=[("os", s, i % 4)])
                P.op("dve", lambda e, s=s, i=i: e.tensor_tensor(
                    out=os_[s][:, i % 4, :], in0=os_[s][:, i % 4, :], in1=xs[s][:, i % 4, :], op=ALU.add),
                    reads=[("xs", s), ("os", s, i % 4)], writes=[("os", s, i % 4)])
                if i % 4 == 3:
                    P.op("act", lambda e, s=s, q=q, col=col: e.dma_start(
                        out=dst[q * 512:(q + 1) * 512, col:col + 256].rearrange("(t p) c -> p t c", p=128), in_=os_[s]),
                        reads=[("os", s, k) for k in range(4)], dma_key=("osst", s))
            jobs.append((col, "tm", epi))
        gemm_phase(W, jobs, yg, nslots=2)
        A.release()

    def phase_l1_in(hT):
        A.mark()
        gq = A.alloc([128], F32)
        gk = A.alloc([128], F32)
        P.op("sp", lambda e: e.dma_start(out=gq, in_=qg_d.partition_broadcast(128)), dma_key="gq")
        P.op("sp", lambda e: e.dma_start(out=gk, in_=kg_d.partition_broadcast(128)), dma_key="gk")
        P.barrier()
        junkq = A.alloc([128], BF16)
        ssq = [A.alloc([2], F32) for _ in range(4)]
        rq = [A.alloc([2], F32) for _ in range(4)]
        qst = [A.alloc([4, 256], BF16) for _ in range(2)]
        kst = [A.alloc([4, 256], F32) for _ in range(2)]
        cnt = {"n": 0, "g": 0}

        def make_qk(jq, is_k):
            col_out = jq * 256
            gb = gk if is_k else gq

            def epi(i, bank):
                s2 = cnt["n"] % 4
                cnt["n"] += 1
                if i % 4 == 0:
                    cnt["g"] += 1
                s = cnt["g"] % 2
                q4 = i // 4
                for hd in range(2):
                    P.op("act", lambda e, hd=hd, s2=s2, bank=bank: e.activation(
                        out=junkq, in_=ps[:, bank, hd * 128:(hd + 1) * 128], func=AF.Square, accum_out=ssq[s2][:, hd:hd + 1]),
                        reads=[("ps", bank)], writes=["junkq", ("ssq", s2, hd)])
                P.op("act", lambda e, s2=s2: e.activation(out=ssq[s2], in_=ssq[s2], func=AF.Sqrt, scale=1.0 / 128, bias=EPS),
                     reads=[("ssq", s2, 0), ("ssq", s2, 1)], writes=[("ssq", s2, 0), ("ssq", s2, 1)])
                P.op("dve", lambda e, s2=s2: e.reciprocal(out=rq[s2], in_=ssq[s2]),
                     reads=[("ssq", s2, 0), ("ssq", s2, 1)], writes=[("rq", s2)])
                for hd in range(2):
                    if is_k:
                        o = kst[s][:, i % 4, hd * 128:(hd + 1) * 128]
                        wk = ("kst", s, i % 4, hd)
                    else:
                        o = qst[s][:, i % 4, hd * 128:(hd + 1) * 128]
                        wk = ("qst", s, i % 4, hd)
                    P.op("dve", lambda e, o=o, hd=hd, s2=s2, bank=bank: e.scalar_tensor_tensor(
                        out=o, in0=ps[:, bank, hd * 128:(hd + 1) * 128], scalar=rq[s2][:, hd:hd + 1], in1=gb,
                        op0=ALU.mult, op1=ALU.mult), reads=[("ps", bank), ("rq", s2)], writes=[wk])
                if is_k:
                    P.op("act", lambda e, s=s, i=i: e.activation(out=qst[s][:, i % 4, :], in_=kst[s][:, i % 4, :], func=AF.Copy),
                         reads=[("kst", s, i % 4, 0), ("kst", s, i % 4, 1)], writes=[("qst", s, i % 4, 0), ("qst", s, i % 4, 1)])
                if i % 4 == 3:
                    dstb = (Ks if is_k else Qs)[q4 * 512:(q4 + 1) * 512, col_out:col_out + 256].rearrange("(t p) c -> p t c", p=128)
                    P.op("act", lambda e, s=s, dstb=dstb: e.dma_start(out=dstb, in_=qst[s]),
                         reads=[("qst", s, k4, hd) for k4 in range(4) for hd in range(2)], dma_key=("qstst", s))
                    if is_k:
                        for hd in range(2):
                            h = jq * 2 + hd
                            d = sk_d[h, q4 * 512:(q4 + 1) * 512, :].rearrange("(t p) d -> p t d", p=128)
                            P.op("act", lambda e, s=s, d=d, hd=hd: e.dma_start(out=d, in_=kst[s][:, :, hd * 128:(hd + 1) * 128]),
                                 reads=[("kst", s, k4, hd) for k4 in range(4)], dma_key=("kstst", s, hd))
            return epi
        jobs = []
        PARTS = DBG.get("l1_parts", ["q", "k", "v", "glu", "gate"])
        NJ = DBG.get("l1_nj", 8)
        for jq in range(NJ if "q" in PARTS else 0):
            jobs.append((jq * 256, "tm", make_qk(jq, False)))
        for jq in range(NJ if "k" in PARTS else 0):
            jobs.append((2048 + jq * 256, "tm", make_qk(jq, True)))

        def make_v(jq):
            def epi(i, bank):
                if i % 4 == 0:
                    cnt["g"] += 1
                s = cnt["g"] % 2
                q4 = i // 4
                P.op("dve", lambda e, s=s, i=i, bank=bank: e.tensor_copy(out=kst[s][:, i % 4, :], in_=ps[:, bank, 0:256]),
                     reads=[("ps", bank)], writes=[("kst", s, i % 4, 0), ("kst", s, i % 4, 1)])
                P.op("act", lambda e, s=s, i=i, bank=bank: e.activation(out=qst[s][:, i % 4, :], in_=ps[:, bank, 0:256], func=AF.Copy),
                     reads=[("ps", bank)], writes=[("qst", s, i % 4, 0), ("qst", s, i % 4, 1)])
                if i % 4 == 3:
                    dstb = Vs[q4 * 512:(q4 + 1) * 512, jq * 256:(jq + 1) * 256].rearrange("(t p) c -> p t c", p=128)
                    P.op("act", lambda e, s=s, dstb=dstb: e.dma_start(out=dstb, in_=qst[s]),
                         reads=[("qst", s, k4, hd) for k4 in range(4) for hd in range(2)], dma_key=("qstst", s))
                    for hd in range(2):
                        h = jq * 2 + hd
                        d = sv_d[h, q4 * 512:(q4 + 1) * 512, :].rearrange("(t p) d -> p t d", p=128)
                        P.op("act", lambda e, s=s, d=d, hd=hd: e.dma_start(out=d, in_=kst[s][:, :, hd * 128:(hd + 1) * 128]),
                             reads=[("kst", s, k4, hd) for k4 in range(4)], dma_key=("kstst", s, hd))
            return epi
        for jq in range(NJ if "v" in PARTS else 0):
            jobs.append((4096 + jq * 256, "tm", make_v(jq)))
        gemm_phase(w_in1, jobs, hT)
        A.release()

        A.mark()
        zt = A.alloc([16], BF16)
        P.op("dve", lambda e: e.memset(zt, 0.0), writes=["zt"])
        for cq in range(16):
            P.op("sp", lambda e, cq=cq: e.dma_start(out=Gs[cq * 128:(cq + 1) * 128, 0:15], in_=zt[:, 0:15]),
                 reads=["zt"], dma_key="zt0")
            P.op("sp", lambda e, cq=cq: e.dma_start(out=Gs[cq * 128:(cq + 1) * 128, T + 15:T + 30], in_=zt[:, 0:15]),
                 reads=["zt"], dma_key="zt1")
        sig = [A.alloc([T], BF16) for _ in range(2)]
        stG = [A.alloc([T], BF16) for _ in range(2)]
        jobs = []
        for jb in range(NJ if "glu" in PARTS else 0):
            jobs.append((6144 + 2048 + jb * 256, "fm", make_fm_store(sig, lambda half: None, AF.Sigmoid)))
            jobs.append((6144 + jb * 256, "fm", make_fm_store(
                stG, lambda half, jb=jb: Gs[jb * 256 + half * 128: jb * 256 + (half + 1) * 128, 15:T + 15], None,
                in1=lambda half: (sig[half], ("stg", id(sig), half)))))
        gemm_phase(w_in1, jobs, hT)
        A.release()

        A.mark()
        stS = [A.alloc([T], BF16) for _ in range(3)]
        jobs = []
        for j in range(2 * NJ if "gate" in PARTS else 0):
            jobs.append((10240 + j * 256, "fm", make_fm_store(
                stS, lambda half, j=j: SG[j * 256 + half * 128: j * 256 + (half + 1) * 128, :], AF.Silu)))
        gemm_phase(w_in1, jobs, hT)
        A.release()

    def phase_l1_attn():
        A.mark()
        mk = A.alloc([6, 640], F32)
        ctxb = A.alloc([1], F32)
        P.op("sp", lambda e: e.dma_start(out=mk, in_=maskadd), dma_key="mk")
        P.op("sp", lambda e: e.dma_start(out=ctxb, in_=ctxbias_d), dma_key="ctxb")
        P.barrier()
        qtm = [A.alloc([16, 128], BF16) for _ in range(2)]
        ktm = [A.alloc([16, 128], BF16) for _ in range(2)]
        vtm = [A.alloc([16, 128], BF16) for _ in range(2)]
        kctm = [A.alloc([2, 128], BF16) for _ in range(2)]
        vctm = [A.alloc([2, 128], BF16) for _ in range(2)]
        bia = [A.alloc([1152], F32) for _ in range(2)]
        sgh = [A.alloc([T], BF16) for _ in range(2)]
        qT = [A.alloc([T], BF16) for _ in range(2)]
        kT = [A.alloc([T], BF16) for _ in range(2)]
        kcT = [A.alloc([256], BF16) for _ in range(2)]
        bm = [A.alloc([6, 640], BF16) for _ in range(2)]
        ctxm = A.alloc([256], BF16)
        pT = [A.alloc([896], BF16) for _ in range(3)]
        rs = [A.alloc([512], F32) for _ in range(2)]
        on = [A.alloc([512], F32) for _ in range(2)]
        ygst = [A.alloc([T], BF16) for _ in range(2)]
        scale = 128.0 ** -0.5
        P.op("dve", lambda e: e.tensor_scalar(out=mk, in0=mk, scalar1=1.0 / scale, scalar2=None, op0=ALU.mult), writes=["mk"])
        zc = A.alloc([256], F32)
        P.op("dve", lambda e: e.memset(zc, 0.0), writes=["zc"])
        P.op("dve", lambda e: e.tensor_scalar(out=ctxm, in0=zc, scalar1=ctxb[:, 0:1], scalar2=1.0 / scale, op0=ALU.add, op1=ALU.mult),
             reads=["zc"], writes=["ctxm"])
        P.barrier()
        cn = {"q": 0}

        def preamble(h):
            s = h % 2
            tmrows = lambda Z: Z[:, h * 128:(h + 1) * 128].rearrange("(t p) d -> p t d", p=128)
            P.op("sp", lambda e, a=tmrows(Qs): e.dma_start(out=qtm[s], in_=a), writes=[("qtm", s)], dma_key=("qtm", s))
            P.op("sp", lambda e, a=tmrows(Ks): e.dma_start(out=ktm[s], in_=a), writes=[("ktm", s)], dma_key=("ktm", s))
            P.op("sp", lambda e, a=tmrows(Vs): e.dma_start(out=vtm[s], in_=a), writes=[("vtm", s)], dma_key=("vtm", s))
            P.op("pool", lambda e: e.dma_start(out=kctm[s], in_=cache_k[h].rearrange("(t p) d -> p t d", p=128)),
                 writes=[("kctm", s)], dma_key=("kctm", s))
            P.op("pool", lambda e: e.dma_start(out=vctm[s], in_=cache_v[h].rearrange("(t p) d -> p t d", p=128)),
                 writes=[("vctm", s)], dma_key=("vctm", s))
            P.op("sp", lambda e: e.dma_start(out=bia[s], in_=bias_t[h]), writes=[("bia", s)], dma_key=("bia", s))
            P.op("sp", lambda e: e.dma_start(out=sgh[s], in_=SG[h * 128:(h + 1) * 128, :]), writes=[("sgh", s)], dma_key=("sgh", s))
            for (src, dst, nt, rk, wkk) in ((qtm, qT, 16, "qtm", "qT"), (ktm, kT, 16, "ktm", "kT"), (kctm, kcT, 2, "kctm", "kcT")):
                for g8 in range((nt + 3) // 4):
                    bank = 6 + (cn["q"] % 2)
                    cn["q"] += 1
                    n8 = min(4, nt - g8 * 4)

                    def tr(e, src=src, g8=g8, n8=n8, bank=bank):
                        for j in range(n8):
                            ins = e.transpose(out=psb(bank)[:, j * 128:(j + 1) * 128], in_=src[s][:, g8 * 4 + j, :], identity=ident)
                        return ins
                    P.op("pe", tr, reads=[(rk, s)], writes=[("ps", bank)])
                    evac(dst[s][:, g8 * 512:g8 * 512 + n8 * 128], bank, [(wkk, s, g8)], src=psb(bank)[:, 0:n8 * 128])
            for v in range(6):
                o_v = _VREP[v] - _base(_VREP[v])
                P.op("dve", lambda e, v=v, o_v=o_v: e.scalar_tensor_tensor(
                    out=bm[s][:, v, :], in0=bia[s][:, (4 - o_v) * 128:(9 - o_v) * 128], scalar=1.0 / scale, in1=mk[:, v, :],
                    op0=ALU.mult, op1=ALU.add),
                    reads=[("bia", s)], writes=[("bm", s, v)])

        def scores(n, h, i):
            s = h % 2
            b = _base(i)
            sp2 = n % 2
            bA, bB = sp2 * 2, sp2 * 2 + 1
            psA = ps[:, bA, :].rearrange("p (j q) -> p j q", j=4)
            psB = ps[:, bB, :].rearrange("p (j q) -> p j q", j=4)

            v = _variant(i)

            def sc_mm(e):
                for j in range(5):
                    o = psA[:, j, :] if j < 4 else psB[:, 0, :]
                    e.matmul(o, kT[s][:, (b + j) * 128:(b + j + 1) * 128], qT[s][:, i * 128:(i + 1) * 128],
                             start=True, stop=False)
                    ins = e.matmul(o, ident, bm[s][:, v, j * 128:(j + 1) * 128], start=False, stop=True)
                for j in range(2):
                    e.matmul(psB[:, 1 + j, :], kcT[s][:, j * 128:(j + 1) * 128], qT[s][:, i * 128:(i + 1) * 128],
                             start=True, stop=False)
                    ins = e.matmul(psB[:, 1 + j, :], ident, ctxm[:, j * 128:(j + 1) * 128], start=False, stop=True)
                return ins
            P.op("pe", sc_mm, reads=[("qT", s, g) for g in range(4)] + [("kT", s, g) for g in range(4)] + [("kcT", s, 0), ("bm", s, v)],
                 writes=[("ps", bA), ("ps", bB)])

        def rest(n, h, i):
            s = h % 2
            v = _variant(i)
            b = _base(i)
            sp2 = n % 2
            sp3 = n % 3
            bA, bB = sp2 * 2, sp2 * 2 + 1
            i4, ii = i // 4, i % 4
            so = (h * 4 + i4) % 2
            P.op("act", lambda e: e.activation(out=pT[sp3][:, 0:512], in_=ps[:, bA, :], func=AF.Exp, scale=scale),
                 reads=[("ps", bA)], writes=[("pT", sp3, 0)])
            P.op("act", lambda e: e.activation(out=pT[sp3][:, 512:896], in_=ps[:, bB, 0:384], func=AF.Exp, scale=scale),
                 reads=[("ps", bB)], writes=[("pT", sp3, 1)])

            def pv_mm(e):
                for j in range(7):
                    lhs = vtm[s][:, b + j, :] if j < 5 else vctm[s][:, j - 5, :]
                    ins = e.matmul(ps[:, 4, ii * 128:(ii + 1) * 128], lhs, pT[sp3][:, j * 128:(j + 1) * 128], start=(j == 0), stop=(j == 6))
                for j in range(7):
                    ins = e.matmul(ps[:, 5, ii * 128:(ii + 1) * 128], onesb, pT[sp3][:, j * 128:(j + 1) * 128], start=(j == 0), stop=(j == 6))
                return ins
            P.op("pe", pv_mm, reads=[("pT", sp3, 0), ("pT", sp3, 1), ("vtm", s), ("vctm", s)],
                 writes=[("ps", 4, ii), ("ps", 5, ii)])
            if ii == 3:
                okeys = [("ps", 4, k) for k in range(4)]
                skeys = [("ps", 5, k) for k in range(4)]
                P.op("dve", lambda e: e.reciprocal(out=rs[so], in_=ps[:, 5, :]), reads=skeys, writes=[("rs", so)])
                P.op("dve", lambda e: e.tensor_tensor(out=on[so], in0=ps[:, 4, :], in1=rs[so], op=ALU.mult),
                     reads=okeys + [("rs", so)], writes=[("on", so)])
                P.op("dve", lambda e: e.tensor_tensor(
                    out=ygst[s][:, i4 * 512:(i4 + 1) * 512], in0=on[so], in1=sgh[s][:, i4 * 512:(i4 + 1) * 512], op=ALU.mult),
                    reads=[("on", so), ("sgh", s)], writes=[("ygst", s, i4)])
            if i == 15:
                P.op("act", lambda e: e.dma_start(out=YG[h * 128:(h + 1) * 128, :], in_=ygst[s]),
                     reads=[("ygst", s, k) for k in range(4)], dma_key=("ygstst", s))

        tiles = [(h, i) for h in range(16) for i in range(16)]
        for n, (h, i) in enumerate(tiles):
            if i == 0:
                preamble(h)
            scores(n, h, i)
            if n >= 1:
                rest(n - 1, *tiles[n - 1])
        rest(len(tiles) - 1, *tiles[-1])
        P.barrier()
        A.release()

    def phase_l1_conv():
        A.mark()
        flag = A.alloc([1], F32)
        dwt = A.alloc([16, 31], F32)
        dwb = A.alloc([16], F32)
        lng = A.alloc([16], F32)
        lnb = A.alloc([16], F32)
        wpw = A.alloc([16, 2048], BF16)
        P.op("sp", lambda e: e.dma_start(out=flag, in_=flag_d), dma_key="c0")
        P.op("sp", lambda e: e.dma_start(out=dwt, in_=dw_l), dma_key="c1")
        P.op("sp", lambda e: e.dma_start(out=dwb, in_=dwb_l), dma_key="c2")
        P.op("sp", lambda e: e.dma_start(out=lng, in_=lng_l), dma_key="c3")
        P.op("sp", lambda e: e.dma_start(out=lnb, in_=lnb_l), dma_key="c4")
        for q in range(4):
            P.op("pool", lambda e, q=q: e.dma_start(
                out=wpw[:, q * 4:(q + 1) * 4, :], in_=w_pw[q * 512:(q + 1) * 512, :].rearrange("(c p) n -> p c n", p=128)),
                dma_key=("wpw", q))
        P.barrier()
        Gp = [A.alloc([2, 286], BF16) for _ in range(3)]
        cv = A.alloc([16, 512], F32)
        dg = [A.alloc([31, 128], BF16) for _ in range(2)]
        sq = [A.alloc([512], F32) for _ in range(2)]
        mean = A.alloc([512], F32)
        rstd = A.alloc([512], F32)
        tmp = A.alloc([512], F32)
        t1 = [A.alloc([512], F32) for _ in range(2)]
        zT = A.alloc([16, 512], BF16)
        sgt = A.alloc([16, 512], BF16)
        ygs = A.alloc([16, 512], BF16)
        W2 = T + 30
        n = 0
        for tt in range(4):
            P.op("sp", lambda e, tt=tt: e.dma_start(
                out=sgt, in_=SG[2048:4096, tt * 512:(tt + 1) * 512].rearrange("(c p) t -> p c t", p=128)),
                writes=["sgt"], dma_key="sgt")
            pend = []
            for cc in range(16):
                n += 1
                s3 = n % 3
                s2 = n % 2
                src = bass.AP(Gs.tensor, cc * 128 * W2 + tt * 512, [[W2, 128], [256, 2], [1, 286]])
                P.op("sp", lambda e, s3=s3, src=src: e.dma_start(out=Gp[s3], in_=src), writes=[("Gp", s3)], dma_key=("Gp", s3))
                P.op("dve", lambda e, s3=s3: e.tensor_scalar(out=Gp[s3][:, :, 0:15], in0=Gp[s3][:, :, 0:15],
                                                           scalar1=flag[:, 0:1], scalar2=None, op0=ALU.mult),
                     reads=[("Gp", s3)], writes=[("Gp", s3)])
                P.op("dve", lambda e, s3=s3: e.tensor_scalar(out=Gp[s3][:, :, 271:286], in0=Gp[s3][:, :, 271:286],
                                                           scalar1=flag[:, 0:1], scalar2=None, op0=ALU.mult),
                     reads=[("Gp", s3)], writes=[("Gp", s3)])
                dgs = dg[n % 2]
                in0b = bass.AP(ident.tensor, ident.offset, [list(ident.ap[0]), [0, 31], [1, 128]])
                dsl = dwt[:, cc, :]
                in1b = bass.AP(dsl.tensor, dsl.offset, [list(dsl.ap[0]), [1, 31], [0, 128]])
                P.op("dve", lambda e, dgs=dgs, in0b=in0b, in1b=in1b: e.tensor_tensor(out=dgs, in0=in0b, in1=in1b, op=ALU.mult),
                     reads=["ident"], writes=[("dg", n % 2)])
                cbank = 2 + (n % 2)

                def cv_mm(e, s3=s3, dgs=dgs, cbank=cbank):
                    for k in range(31):
                        ins = e.matmul(ps[:, cbank, :].rearrange("p (s t) -> p s t", s=2), dgs[:, k, :], Gp[s3][:, :, k:k + 256],
                                       start=(k == 0), stop=(k == 30))
                    return ins
                P.op("pe", cv_mm, reads=[("Gp", s3), ("dg", n % 2)], writes=[("ps", cbank)])
                P.op("act", lambda e, cc=cc, cbank=cbank: e.activation(out=cv[:, cc, :], in_=ps[:, cbank, :], func=AF.Identity,
                                                                     bias=dwb[:, cc:cc + 1]),
                     reads=[("ps", cbank)], writes=[("cv", cc)])
                P.op("act", lambda e, s2=s2, cc=cc: e.activation(out=sq[s2], in_=cv[:, cc, :], func=AF.Square),
                     reads=[("cv", cc)], writes=[("sq", s2)])

                def st_mm(e, cc=cc, s2=s2):
                    e.matmul(ps[:, 0, :], onesf, cv[:, cc, :], start=(cc == 0), stop=(cc == 15))
                    return e.matmul(ps[:, 1, :], onesf, sq[s2], start=(cc == 0), stop=(cc == 15))
                pend.append((st_mm, [("cv", cc), ("sq", s2)]))
                if len(pend) > 1:
                    f, rd = pend.pop(0)
                    P.op("pe", f, reads=rd, writes=[("ps", 0), ("ps", 1)])
            while pend:
                f, rd = pend.pop(0)
                P.op("pe", f, reads=rd, writes=[("ps", 0), ("ps", 1)])
            P.op("dve", lambda e: e.tensor_scalar(out=mean, in0=ps[:, 0, :], scalar1=1.0 / 2048, scalar2=None, op0=ALU.mult),
                 reads=[("ps", 0)], writes=["mean"])
            P.op("dve", lambda e: e.tensor_tensor(out=tmp, in0=mean, in1=mean, op=ALU.mult), reads=["mean"], writes=["tmp"])
            P.op("dve", lambda e: e.scalar_tensor_tensor(out=rstd, in0=ps[:, 1, :], scalar=1.0 / 2048, in1=tmp,
                                                         op0=ALU.mult, op1=ALU.subtract),
                 reads=[("ps", 1), "tmp"], writes=["rstd"])
            P.op("act", lambda e: e.activation(out=rstd, in_=rstd, func=AF.Sqrt, scale=1.0, bias=EPS), reads=["rstd"], writes=["rstd"])
            P.op("dve", lambda e: e.reciprocal(out=rstd, in_=rstd), reads=["rstd"], writes=["rstd"])
            for cc in range(16):
                s2 = cc % 2
                P.op("dve", lambda e, s2=s2, cc=cc: e.tensor_tensor(out=t1[s2], in0=cv[:, cc, :], in1=mean, op=ALU.subtract),
                     reads=[("cv", cc), "mean"], writes=[("t1", s2)])
                P.op("dve", lambda e, s2=s2: e.tensor_tensor(out=t1[s2], in0=t1[s2], in1=rstd, op=ALU.mult),
                     reads=[("t1", s2), "rstd"], writes=[("t1", s2)])
                P.op("act", lambda e, s2=s2, cc=cc: e.activation(out=zT[:, cc, :], in_=t1[s2], func=AF.Silu,
                                                               scale=lng[:, cc:cc + 1], bias=lnb[:, cc:cc + 1]),
                     reads=[("t1", s2)], writes=[("zT", cc)])
            for nk in range(16):
                bank = 4 + (nk % 4)

                def pw_mm(e, nk=nk, bank=bank):
                    for kc in range(16):
                        ins = e.matmul(ps[:, bank, :], wpw[:, kc, nk * 128:(nk + 1) * 128], zT[:, kc, :],
                                       start=(kc == 0), stop=(kc == 15))
                    return ins
                P.op("pe", pw_mm, reads=[("zT", kc) for kc in range(16)], writes=[("ps", bank)])
                P.op("dve", lambda e, nk=nk, bank=bank: e.tensor_tensor(out=ygs[:, nk, :], in0=ps[:, bank, :], in1=sgt[:, nk, :], op=ALU.mult),
                     reads=[("ps", bank), "sgt"], writes=[("ygs", nk)])
            P.op("act", lambda e, tt=tt: e.dma_start(
                out=YG[2048:4096, tt * 512:(tt + 1) * 512].rearrange("(c p) t -> p c t", p=128), in_=ygs),
                reads=[("ygs", nk) for nk in range(16)], dma_key="ygsst")
        P.barrier()
        A.release()

    if has("adaln"):
        phase_adaln()
    if has("l0_norm") or has("l0_in"):
        A.mark()
        hT = A.alloc([32, T], BF16)
        if has("l0_norm"):
            phase_norm(0, x_d, hT)
            if DBG.get("dump_hT"):
                hdbg = dout("hT_dbg", [D, T], BF16)
                P.op("sp", lambda e: e.dma_start(out=hdbg.rearrange("(c p) t -> p c t", p=128), in_=hT), dma_key="hdbg")
                P.barrier()
        if has("l0_in"):
            phase_l0_in(hT)
        A.release()
    if has("l0_mix"):
        phase_l0_mix()
    if has("l0_out"):
        A.mark()
        yg = A.alloc([32, T], BF16)
        phase_out(w_out0, x_d, X1, 0, yg)
        A.release()
    if has("l1_norm") or has("l1_in"):
        A.mark()
        hT = A.alloc([32, T], BF16)
        if has("l1_norm"):
            phase_norm(1, X1, hT)
        if has("l1_in"):
            phase_l1_in(hT)
        A.release()
    if has("l1_attn"):
        phase_l1_attn()
    if has("l1_conv"):
        phase_l1_conv()
    if has("l1_out"):
        A.mark()
        yg = A.alloc([32, T], BF16)
        phase_out(w_out1, X1, y_d, 1, yg)
        A.release()
    P.final_wait("sp")
    P.emit()
    return nc


def _variant(i):
    return {0: 0, 1: 1, 14: 4, 15: 5}.get(i, 2 + (i % 2))


_VREP = {0: 0, 1: 1, 2: 2, 3: 3, 4: 14, 5: 15}


def _base(i):
    return min(max(i - 2, 0), 11)


def _const_tables(is_sample):
    Ls = 2048 if is_sample else 256
    c = np.arange(512)
    ang = 2.0 * np.pi * ((c[:, None] * c[None, :]) % 512) / 512.0
    cc = (np.cos(ang) / np.sqrt(512.0)).astype(NPBF)
    sc = (np.sin(ang) / np.sqrt(512.0)).astype(NPBF)
    l = np.arange(Ls)
    angl = 2.0 * np.pi * ((l[:, None] * l[None, :]) % Ls) / Ls
    cb = np.cos(angl) / np.sqrt(Ls)
    sb = -np.sin(angl) / np.sqrt(Ls)
    cl = np.zeros((T, T), np.float32)
    sl = np.zeros((T, T), np.float32)
    for s in range(T // Ls):
        cl[s * Ls:(s + 1) * Ls, s * Ls:(s + 1) * Ls] = cb
        sl[s * Ls:(s + 1) * Ls, s * Ls:(s + 1) * Ls] = sb
    band = np.zeros((4, 4, 6, 128, 512), np.float32)
    tl = np.arange(Ls)
    for g, win in enumerate((2, 4, 8, 16)):
        lo = np.maximum(tl - win // 2, 0)
        hi = np.minimum(tl + win // 2, Ls)
        cntv = (hi - lo).astype(np.float32)
        Mloc = np.zeros((Ls, Ls), np.float32)
        for t in range(Ls):
            Mloc[lo[t]:hi[t], t] = 1.0 / cntv[t]
            Mloc[t, t] -= 1.0
        M = np.zeros((T, T), np.float32)
        for s in range(T // Ls):
            M[s * Ls:(s + 1) * Ls, s * Ls:(s + 1) * Ls] = Mloc
        for tt in range(4):
            for j in range(6):
                lt = 4 * tt - 1 + j
                if 0 <= lt < 16:
                    band[g, tt, j] = M[lt * 128:(lt + 1) * 128, tt * 512:(tt + 1) * 512]
    mask = np.full((128, 6, 5, 128), NEG, np.float32)
    kl = np.arange(128)
    for v in range(6):
        i = _VREP[v]
        b = _base(i)
        for j in range(5):
            kt = b + j
            if is_sample:
                r = 2 * i + kl // 64
                qc = kl % 64
                kr = 2 * kt + kl // 64
                kc = kl % 64
                start = np.clip(r - 4, 0, 24)
                qs = np.clip(qc - 8, 0, 48)
                ok = ((kr[:, None] >= start[None, :]) & (kr[:, None] < start[None, :] + 8)
                      & (kc[:, None] >= qs[None, :]) & (kc[:, None] < qs[None, :] + 16))
                mask[:, v, j, :] = np.where(ok, 0.0, NEG)
            else:
                if kt // 2 == i // 2:
                    mask[:, v, j, :] = 0.0
    return dict(cc_m=cc, sc_m=sc, cl_m=cl.astype(NPBF), sl_m=sl.astype(NPBF), band_m=band.astype(NPBF),
                maskadd=np.ascontiguousarray(mask.reshape(128, 6, 640)),
                ctxbias=np.full((128, 1), 0.0 if is_sample else NEG, np.float32),
                flag=np.full((128, 1), 1.0 if is_sample else 0.0, np.float32),
                ident=np.eye(128, dtype=np.float32).astype(NPBF))


def _bias_table(rel_bias, is_sample):
    out = np.zeros((16, 128, 9, 128), np.float32)
    if is_sample:
        kl = np.arange(128)
        for o in range(-4, 5):
            dr = 2 * o + (kl[:, None] // 64) - (kl[None, :] // 64)
            dc = np.clip((kl[:, None] % 64) - (kl[None, :] % 64) + 15, 0, 30)
            okr = np.abs(dr) <= 7
            dri = np.clip(dr + 7, 0, 14)
            gathered = rel_bias[:, dri, dc]
            out[:, :, o + 4, :] = np.where(okr[None], gathered, 0.0)
    return np.ascontiguousarray(out.reshape(16, 128, 1152))


def _cols(v, n):
    return np.ascontiguousarray(np.asarray(v, np.float32).reshape(n, 128).T)


_CT_CACHE = {}


def prep_inputs(inp, core):
    is_sample = core >= 4
    if is_sample not in _CT_CACHE:
        _CT_CACHE[is_sample] = _const_tables(is_sample)
    ct = _CT_CACHE[is_sample]
    if is_sample:
        b = core - 4
        x = inp["x_sample"][b]
        cond = inp["c"][b]
        ck = inp["cache_k"][b, 0]
        cv = inp["cache_v"][b, 0]
    else:
        x = inp["x_prompt"][8 * core:8 * core + 8].reshape(T, D)
        cond = inp["c_ctx"]
        ck = np.zeros((16, 256, 128), np.float32)
        cv = np.zeros((16, 256, 128), np.float32)
    m = dict(ct)
    m.update(
        x=np.ascontiguousarray(x), cond_l=_cols(cond, 32),
        ng_l=np.stack([_cols(inp["norm_g"][0], 32), _cols(inp["norm_g"][1], 32)]),
        w_ada=inp["w_ada"], b_ada=inp["b_ada"],
        w_in_even=inp["w_in_even"][0], w_out_even=inp["w_out_even"][0],
        w_fourier=inp["w_fourier"][0], w_pool=inp["w_pool"][0], pscale_l=_cols(inp["pool_scale"][0], 16),
        w_in_odd=inp["w_in_odd"][0], w_out_odd=inp["w_out_odd"][0],
        q_norm_g=inp["q_norm_g"][0], k_norm_g=inp["k_norm_g"][0],
        bias_t=_bias_table(inp["rel_bias"][0], is_sample),
        cache_k=np.ascontiguousarray(ck), cache_v=np.ascontiguousarray(cv),
        dw_l=np.ascontiguousarray(inp["conv_dw"][0].reshape(31, 16, 128).transpose(2, 1, 0)),
        dwb_l=_cols(inp["conv_dw_b"][0], 16), lng_l=_cols(inp["conv_ln_g"][0], 16), lnb_l=_cols(inp["conv_ln_b"][0], 16),
        w_conv_pw=inp["w_conv_pw"][0],
    )
    return m


def kernel(**inputs):
    inp = {k: np.asarray(v) for k, v in inputs.items()}
    nc = build()
    in_maps = [prep_inputs(inp, c) for c in range(8)]
    res = run_bass_kernel_spmd(nc, in_maps, core_ids=list(range(8)))
    r = res.results
    y_prompt = np.concatenate([r[c]["y"].reshape(8, 256, D) for c in range(4)], axis=0)
    y_sample = np.stack([r[4 + b]["y"] for b in range(4)], axis=0)
    sk = np.concatenate([r[c]["sk"].reshape(16, 8, 256, 128).transpose(1, 0, 2, 3) for c in range(4)], axis=0)
    sv = np.concatenate([r[c]["sv"].reshape(16, 8, 256, 128).transpose(1, 0, 2, 3) for c in range(4)], axis=0)
    return (y_prompt.astype(np.float32), y_sample.astype(np.float32),
            np.ascontiguousarray(sk[:, None]).astype(np.float32), np.ascontiguousarray(sv[:, None]).astype(np.float32))
```

```python
import numpy as np
import ml_dtypes
import concourse.bass as bass
import concourse.mybir as mybir
from concourse.bass_utils import run_bass_kernel_spmd

F32 = mybir.dt.float32
BF16 = mybir.dt.bfloat16
AF = mybir.ActivationFunctionType
ALU = mybir.AluOpType
NPBF = ml_dtypes.bfloat16

T = 2048
D = 4096
NT = 16
EPS = 1e-6
NEG = -30000.0
ENG = ("sp", "act", "pool", "dve", "pe")


class Prog:
    def __init__(self, nc, eng_sems, dma_sems):
        self.nc = nc
        self.streams = {e: [] for e in ENG}
        self.eng_sem = eng_sems
        self.dma_pool = list(dma_sems)
        self.cnt = {}
        self.semobj = {}
        for s in list(eng_sems.values()) + list(dma_sems):
            self.cnt[id(s)] = 0
            self.semobj[id(s)] = s
        self.waited = {e: {} for e in ENG}
        self.last_w = {}
        self.readers = {}
        self.dma_map = {}
        self.dma_next = 0

    def _dma_sem(self, key):
        if key not in self.dma_map:
            assert self.dma_next < len(self.dma_pool), "out of dma sems"
            self.dma_map[key] = self.dma_pool[self.dma_next]
            self.dma_next += 1
        return self.dma_map[key]

    def _wait(self, eng, ev):
        sid, val = ev
        if self.waited[eng].get(sid, 0) < val:
            self.streams[eng].append(("wait", sid, val))
            self.waited[eng][sid] = val

    def op(self, eng, fn, reads=(), writes=(), dma_key=None):
        reads = list(reads)
        writes = list(writes)
        deps = set()
        for k in reads:
            if k in self.last_w:
                deps.add(self.last_w[k])
            if isinstance(k, tuple) and k[0] == "ps" and k not in writes:
                for (e2, r) in self.readers.get(k, ()):
                    if e2 != eng:
                        deps.add(r)
        for k in writes:
            if k in self.last_w:
                deps.add(self.last_w[k])
            for (e2, r) in self.readers.get(k, ()):
                deps.add(r)
        mx = {}
        for sid, val in deps:
            mx[sid] = max(mx.get(sid, 0), val)
        for sid in sorted(mx):
            self._wait(eng, (sid, mx[sid]))
        if dma_key is not None:
            sem = self._dma_sem(dma_key)
            inc = 16
        else:
            sem = self.eng_sem[eng]
            inc = 1
        self.cnt[id(sem)] += inc
        ev = (id(sem), self.cnt[id(sem)])
        self.streams[eng].append(("op", fn, id(sem), inc))
        for k in writes:
            self.last_w[k] = ev
            self.readers[k] = []
        for k in reads:
            if k not in writes:
                self.readers.setdefault(k, []).append((eng, ev))
        return ev

    def barrier(self):
        for e in ENG:
            for sid, c in self.cnt.items():
                if c > 0:
                    self._wait(e, (sid, c))
        self.last_w = {}
        self.readers = {}
        self.dma_map = {}
        self.dma_next = 0

    def final_wait(self, eng="sp"):
        for sid, c in self.cnt.items():
            if c > 0:
                self._wait(eng, (sid, c))

    def emit(self):
        nc = self.nc
        with nc.Block() as block:
            def replay(name, e):
                for it in self.streams[name]:
                    if it[0] == "wait":
                        e.wait_ge(self.semobj[it[1]], it[2])
                    else:
                        ins = it[1](e)
                        ins.then_inc(self.semobj[it[2]], it[3])

            @block.sync
            def _(e):
                replay("sp", e)

            @block.scalar
            def _(e):
                replay("act", e)

            @block.gpsimd
            def _(e):
                replay("pool", e)

            @block.vector
            def _(e):
                replay("dve", e)

            @block.tensor
            def _(e):
                replay("pe", e)


class Arena:
    def __init__(self, big, nbytes):
        self.big = big
        self.n = nbytes
        self.off = 0
        self.marks = []

    def mark(self):
        self.marks.append(self.off)

    def release(self):
        self.off = self.marks.pop()

    def alloc(self, shape_free, dtype):
        esz = 4 if dtype == F32 else 2
        n = int(np.prod(shape_free))
        nb = n * esz
        self.off = (self.off + 63) // 64 * 64
        assert self.off + nb <= self.n, f"SBUF arena overflow {self.off}+{nb}>{self.n}"
        ap = self.big[:, self.off // 2:(self.off + nb) // 2]
        self.off += nb
        if dtype == F32:
            ap = ap.bitcast(F32)
        if len(shape_free) == 2:
            ap = ap.rearrange("p (a b) -> p a b", a=shape_free[0])
        elif len(shape_free) == 3:
            ap = ap.rearrange("p (a b c) -> p a b c", a=shape_free[0], b=shape_free[1])
        elif len(shape_free) == 4:
            ap = ap.rearrange("p (a b c d) -> p a b c d", a=shape_free[0], b=shape_free[1], c=shape_free[2])
        return ap


DBG = {}
ALL_PHASES = ("adaln", "l0_norm", "l0_in", "l0_mix", "l0_out",
              "l1_norm", "l1_in", "l1_attn", "l1_conv", "l1_out")


def build(phases=ALL_PHASES, ext_in=(), ext_out=()):
    nc = bass.Bass("TRN2", target_bir_lowering=False)

    def din(name, shape, dt=F32):
        return nc.dram_tensor(name, list(shape), dt, kind="ExternalInput").ap()

    def dout(name, shape, dt=F32):
        return nc.dram_tensor(name, list(shape), dt, kind="ExternalOutput").ap()

    def dscr(name, shape, dt):
        kind = "ExternalInput" if name in ext_in else ("ExternalOutput" if name in ext_out else "Internal")
        return nc.dram_tensor(name, list(shape), dt, kind=kind).ap()

    has = lambda p: p in phases
    L0 = any(p.startswith("l0") for p in phases)
    L1 = any(p.startswith("l1") for p in phases)

    ident_d = din("ident", [128, 128], BF16)
    ng_l = din("ng_l", [2, 128, 32])
    if has("adaln"):
        cond_l = din("cond_l", [128, 32])
        w_ada = din("w_ada", [2, D, 3 * D])
        b_ada = din("b_ada", [2, 3 * D])
    if has("l0_norm") or has("l0_out"):
        x_d = din("x", [T, D])
    if has("l0_in"):
        w_in0 = din("w_in_even", [D, 2 * D])
    if has("l0_mix"):
        w_fou = din("w_fourier", [4, 512, 512])
        w_pool = din("w_pool", [4, 512, 512])
        pscale_l = din("pscale_l", [128, 16])
        cc_m = din("cc_m", [512, 512], BF16)
        sc_m = din("sc_m", [512, 512], BF16)
        cl_m = din("cl_m", [T, T], BF16)
        sl_m = din("sl_m", [T, T], BF16)
        band_m = din("band_m", [4, 4, 6, 128, 512], BF16)
    if has("l0_out"):
        w_out0 = din("w_out_even", [D, D])
    if has("l1_in"):
        w_in1 = din("w_in_odd", [D, 14336])
        qg_d = din("q_norm_g", [128])
        kg_d = din("k_norm_g", [128])
    if has("l1_attn"):
        bias_t = din("bias_t", [16, 128, 1152])
        maskadd = din("maskadd", [128, 6, 640])
        ctxbias_d = din("ctxbias", [128, 1])
        cache_k = din("cache_k", [16, 256, 128])
        cache_v = din("cache_v", [16, 256, 128])
    if has("l1_conv"):
        flag_d = din("flag", [128, 1])
        dw_l = din("dw_l", [128, 16, 31])
        dwb_l = din("dwb_l", [128, 16])
        lng_l = din("lng_l", [128, 16])
        lnb_l = din("lnb_l", [128, 16])
        w_pw = din("w_conv_pw", [2048, 2048])
    if has("l1_out"):
        w_out1 = din("w_out_odd", [D, D])

    if has("l1_out"):
        y_d = dout("y", [T, D])
    if has("l1_in"):
        sk_d = dout("sk", [16, T, 128])
        sv_d = dout("sv", [16, T, 128])

    mod_d = dscr("mod", [2, 3 * D], F32)
    if L0:
        A0 = dscr("A0", [4, 512, T], BF16)
        B0 = dscr("B0", [4, T, 512], BF16)
    SG = dscr("SG", [D, T], BF16)
    YG = dscr("YG", [D, T], BF16)
    X1 = dscr("X1", [T, D], F32)
    if L1:
        Qs = dscr("Qs", [T, 2048], BF16)
        Ks = dscr("Ks", [T, 2048], BF16)
        Vs = dscr("Vs", [T, 2048], BF16)
        Gs = dscr("Gs", [2048, T + 30], BF16)

    ARENA = 206 * 1024
    big = nc.alloc_sbuf_tensor("big", [128, ARENA // 2], BF16)
    ps = nc.alloc_psum_tensor("ps", [128, 8, 512], F32)
    sems = {e: nc.alloc_semaphore(f"s_{e}") for e in ENG}
    dsems = [nc.alloc_semaphore(f"d_{i}") for i in range(40)]
    P = Prog(nc, sems, dsems)
    A = Arena(big, ARENA)

    def psb(bank):
        return ps[:, bank, :].bitcast(BF16)

    ident = A.alloc([128], BF16)
    onesb = A.alloc([128], BF16)
    onesf = A.alloc([128], F32)
    s_bf = A.alloc([32], BF16)
    P.op("sp", lambda e: e.dma_start(out=ident, in_=ident_d), writes=["ident"], dma_key="ident")
    P.op("dve", lambda e: e.memset(onesb, 1.0), writes=["onesb"])
    P.op("dve", lambda e: e.memset(onesf, 1.0), writes=["onesf"])
    P.barrier()

    def wblock(W, col, n=256):
        return W[:, col:col + n].rearrange("(c p) n -> p c n", p=128)

    ADA_LAYERS = (0, 1) if DBG.get("ada_serial") or not has("l0_in") else (0,)

    def phase_adaln():
        A.mark()
        condt = A.alloc([32], F32)
        modrow = A.alloc([3 * D], F32)
        wb = [A.alloc([32, 512], BF16) for _ in range(3)]
        P.op("sp", lambda e: e.dma_start(out=condt, in_=cond_l), writes=["cond"], dma_key="cond")
        P.op("act", lambda e: e.activation(out=s_bf, in_=condt, func=AF.Silu), reads=["cond"], writes=["s"])
        blk = 0
        NBLK = 16 if 1 not in ADA_LAYERS else 24
        for layer in ADA_LAYERS:
            mkeys = [("modrow", nb) for nb in range(NBLK)]
            P.op("sp", lambda e, layer=layer: e.dma_start(out=modrow[0:1, 0:NBLK * 512], in_=b_ada[layer:layer + 1, 0:NBLK * 512]),
                 writes=mkeys, dma_key="modrow")
            for nb in range(NBLK):
                slot = blk % 3
                bank = blk % 8
                blk += 1
                P.op("pool", lambda e, slot=slot, layer=layer, nb=nb: e.dma_start(
                    out=wb[slot], in_=wblock(w_ada[layer], nb * 512, 512)),
                    writes=[("wb", slot)], dma_key=("wb", slot))

                def mm(e, slot=slot, bank=bank):
                    for c in range(32):
                        ins = e.matmul(ps[0:1, bank, :], s_bf[:, c:c + 1], wb[slot][:, c, :],
                                       start=(c == 0), stop=(c == 31))
                    return ins
                P.op("pe", mm, reads=[("wb", slot), "s"], writes=[("ps", bank)])
                P.op("dve", lambda e, bank=bank, nb=nb: e.tensor_tensor(
                    out=modrow[0:1, nb * 512:(nb + 1) * 512], in0=ps[0:1, bank, :],
                    in1=modrow[0:1, nb * 512:(nb + 1) * 512], op=ALU.add),
                    reads=[("ps", bank)], writes=[("modrow", nb)])
            P.op("sp", lambda e, layer=layer: e.dma_start(out=mod_d[layer:layer + 1, 0:NBLK * 512], in_=modrow[0:1, 0:NBLK * 512]),
                 reads=mkeys, dma_key="modst")
        P.barrier()
        A.release()

    def phase_norm(layer, xsrc, hT):
        A.mark()
        sc = A.alloc([32], F32)
        sh = A.alloc([32], F32)
        g = A.alloc([32], F32)
        sc1 = A.alloc([32], F32)
        xt = [A.alloc([D], F32) for _ in range(2)]
        junk = A.alloc([D], BF16)
        xn = [A.alloc([D], BF16) for _ in range(2)]
        ss = [A.alloc([1], F32) for _ in range(2)]
        rstd = [A.alloc([1], F32) for _ in range(2)]
        col = lambda off: bass.AP(mod_d.tensor, layer * 3 * D + off, [[1, 128], [128, 32]])
        P.op("sp", lambda e: e.dma_start(out=sh, in_=col(0), allow_slow_non_contiguous=True), writes=["sh"], dma_key="sh")
        P.op("sp", lambda e: e.dma_start(out=sc, in_=col(D), allow_slow_non_contiguous=True), writes=["sc"], dma_key="sc")
        P.op("sp", lambda e: e.dma_start(out=g, in_=ng_l[layer]), writes=["g"], dma_key="g")
        P.op("dve", lambda e: e.scalar_tensor_tensor(out=sc1, in0=sc, scalar=1.0, in1=g, op0=ALU.add, op1=ALU.mult),
             reads=["sc", "g"], writes=["sc1"])
        def stage_a(i):
            s = i % 2
            P.op("sp", lambda e, s=s, i=i: e.dma_start(out=xt[s], in_=xsrc[i * 128:(i + 1) * 128, :]),
                 writes=[("xt", s)], dma_key=("xt", s))
            P.op("act", lambda e, s=s: e.activation(out=junk, in_=xt[s], func=AF.Square, accum_out=ss[s]),
                 reads=[("xt", s)], writes=["junk", ("ss", s)])
            P.op("act", lambda e, s=s: e.activation(out=ss[s], in_=ss[s], func=AF.Sqrt, scale=1.0 / D, bias=EPS),
                 reads=[("ss", s)], writes=[("ss", s)])
            P.op("dve", lambda e, s=s: e.reciprocal(out=rstd[s], in_=ss[s]), reads=[("ss", s)], writes=[("rstd", s)])
            P.op("act", lambda e, s=s: e.activation(out=xn[s], in_=xt[s], func=AF.Copy, scale=rstd[s]),
                 reads=[("xt", s), ("rstd", s)], writes=[("xn", s)])

        def stage_b(i):
            s = i % 2
            for grp in range(8):
                bank = (i * 8 + grp) % 8

                def tr(e, s=s, grp=grp, bank=bank):
                    for j in range(4):
                        c = grp * 4 + j
                        ins = e.transpose(out=psb(bank)[:, j * 128:(j + 1) * 128],
                                          in_=xn[s][:, c * 128:(c + 1) * 128], identity=ident)
                    return ins
                P.op("pe", tr, reads=[("xn", s), "ident"], writes=[("ps", bank)])
                for j in range(4):
                    c = grp * 4 + j
                    P.op("dve", lambda e, bank=bank, j=j, c=c, i=i: e.tensor_scalar(
                        out=hT[:, c, i * 128:(i + 1) * 128], in0=psb(bank)[:, j * 128:(j + 1) * 128],
                        scalar1=sc1[:, c:c + 1], scalar2=sh[:, c:c + 1], op0=ALU.mult, op1=ALU.add),
                        reads=[("ps", bank), "sc1", "sh"])
        stage_a(0)
        for i in range(NT):
            if i + 1 < NT:
                stage_a(i + 1)
            stage_b(i)
        P.barrier()
        A.release()

    class G:
        bset = 0
        bank = 0

    def gemm_phase(W, jobs, hT, ncol=256, nslots=3, side=None):
        A.mark()
        wb = [A.alloc([32, ncol], BF16) for _ in range(nslots)]
        for ji, (col, mode, epi) in enumerate(jobs):
            slot = ji % nslots
            P.op("pool", lambda e, slot=slot, col=col: e.dma_start(out=wb[slot], in_=wblock(W, col, ncol)),
                 writes=[("wb", slot)], dma_key=("wb", slot))
            if mode == "fm":
                for half in range(ncol // 128):
                    banks = [G.bset * 4 + tt for tt in range(4)]
                    G.bset ^= 1

                    def mm(e, slot=slot, half=half, banks=banks):
                        for c in range(32):
                            for tt in range(4):
                                ins = e.matmul(ps[:, banks[tt], :], wb[slot][:, c, half * 128:(half + 1) * 128],
                                               hT[:, c, tt * 512:(tt + 1) * 512], start=(c == 0), stop=(c == 31))
                        return ins
                    P.op("pe", mm, reads=[("wb", slot)], writes=[("ps", b) for b in banks])
                    epi(half, banks)
                    if side is not None:
                        side(G.bset * 4 + 3)
            else:
                for i in range(NT):
                    bank = G.bank
                    G.bank = (G.bank + 1) % 8

                    def mm(e, slot=slot, i=i, bank=bank):
                        for c in range(32):
                            ins = e.matmul(ps[:, bank, 0:ncol], hT[:, c, i * 128:(i + 1) * 128], wb[slot][:, c, :],
                                           start=(c == 0), stop=(c == 31))
                        return ins
                    P.op("pe", mm, reads=[("wb", slot)], writes=[("ps", bank)])
                    epi(i, bank)
                    if side is not None and i % 8 == 7:
                        side((G.bank + 4) % 8)
        P.barrier()
        A.release()

    class St:
        pass

    def make_fm_store(stages, dst_rows, func, eng_copy="act", in1=None):
        state = {"n": 0}

        def epi(half, banks):
            s = state["n"] % len(stages)
            state["n"] += 1
            for tt in range(4):
                o = stages[s][:, tt * 512:(tt + 1) * 512]
                if in1 is not None:
                    src, key = in1(half)
                    P.op("dve", lambda e, o=o, b=banks[tt], tt=tt, src=src: e.tensor_tensor(
                        out=o, in0=ps[:, b, :], in1=src[:, tt * 512:(tt + 1) * 512], op=ALU.mult),
                        reads=[("ps", banks[tt]), key + (tt,)], writes=[("stg", id(stages), s, tt)])
                elif func is None:
                    P.op("dve", lambda e, o=o, b=banks[tt]: e.tensor_copy(out=o, in_=ps[:, b, :]),
                         reads=[("ps", banks[tt])], writes=[("stg", id(stages), s, tt)])
                else:
                    P.op("act", lambda e, o=o, b=banks[tt]: e.activation(out=o, in_=ps[:, b, :], func=func),
                         reads=[("ps", banks[tt])], writes=[("stg", id(stages), s, tt)])
            d = dst_rows(half)
            if d is not None:
                P.op("act", lambda e, s=s, d=d: e.dma_start(out=d, in_=stages[s]),
                     reads=[("stg", id(stages), s, tt) for tt in range(4)], dma_key=("stgst", id(stages), s))
        return epi

    rot = {"b": 0}

    def nbank():
        b = rot["b"]
        rot["b"] = (b + 1) % 8
        return b

    alt = {"n": 0}

    def evac(out, bank, writes, src=None):
        use_dve = src is not None
        src = ps[:, bank, :] if src is None else src
        alt["n"] += 1
        if alt["n"] % 2 or use_dve:
            P.op("dve", lambda e: e.tensor_copy(out=out, in_=src), reads=[("ps", bank)], writes=writes)
        else:
            P.op("act", lambda e: e.activation(out=out, in_=src, func=AF.Copy), reads=[("ps", bank)], writes=writes)

    def phase_l0_in(hT):
        A.mark()
        stA = [A.alloc([T], BF16) for _ in range(2)]
        stS = [A.alloc([T], BF16) for _ in range(2)]
        stB = [A.alloc([4, 256], BF16) for _ in range(2)]
        jobs = []
        for g in range(4):
            for blk in range(2):
                jobs.append((g * 512 + blk * 256, "fm", make_fm_store(
                    stA, lambda half, g=g, blk=blk: A0[g, blk * 256 + half * 128: blk * 256 + (half + 1) * 128, :], None)))
                for jj in range(2):
                    j = (g * 2 + blk) * 2 + jj
                    jobs.append((D + j * 256, "fm", make_fm_store(
                        stS, lambda half, j=j: SG[j * 256 + half * 128: j * 256 + (half + 1) * 128, :], AF.Silu)))
                stt = {"n": 0}

                def epi_b(i, bank, g=g, blk=blk, stt=stt):
                    s = (i // 4) % 2
                    P.op("dve", lambda e, s=s, i=i, bank=bank: e.tensor_copy(out=stB[s][:, i % 4, :], in_=ps[:, bank, 0:256]),
                         reads=[("ps", bank)], writes=[("stB", s, i % 4)])
                    if i % 4 == 3:
                        q = i // 4
                        d = B0[g, q * 512:(q + 1) * 512, blk * 256:(blk + 1) * 256].rearrange("(t p) c -> p t c", p=128)
                        P.op("act", lambda e, s=s, d=d: e.dma_start(out=d, in_=stB[s]),
                             reads=[("stB", s, k) for k in range(4)], dma_key=("stBst", s))
                jobs.append((2048 + g * 512 + blk * 256, "tm", epi_b))
        if DBG.get("l0_jobs"):
            jobs = [j for j in jobs if j[1] == DBG["l0_jobs"][0]][DBG["l0_jobs"][1]:DBG["l0_jobs"][2]]
        side = None
        if has("adaln") and 1 not in ADA_LAYERS:
            wa = [A.alloc([32, 256], BF16)]
            brow = [A.alloc([256], F32) for _ in range(2)]
            res = [A.alloc([256], F32) for _ in range(2)]
            sk_ = {"k": 0}
            sblocks = [(0, 32 + k) for k in range(16)] + [(1, k) for k in range(48)]

            def side(bank):
                for _ in range(1):
                    if sk_["k"] >= len(sblocks):
                        return
                    lay, k = sblocks[sk_["k"]]
                    sk_["k"] += 1
                    q = sk_["k"] % 2
                    P.op("pool", lambda e, k=k, lay=lay: e.dma_start(out=wa[0], in_=wblock(w_ada[lay], k * 256, 256)),
                         writes=[("wa", 0)], dma_key=("wa", 0))
                    P.op("sp", lambda e, q=q, k=k, lay=lay: e.dma_start(out=brow[q][0:1, :], in_=b_ada[lay:lay + 1, k * 256:(k + 1) * 256]),
                         writes=[("brow", q)], dma_key=("brow", q))

                    def mm(e, bank=bank):
                        for c in range(32):
                            ins = e.matmul(ps[0:1, bank, 0:256], s_bf[:, c:c + 1], wa[0][:, c, :], start=(c == 0), stop=(c == 31))
                        return ins
                    P.op("pe", mm, reads=[("wa", 0)], writes=[("ps", bank)])
                    P.op("dve", lambda e, q=q, bank=bank: e.tensor_tensor(out=res[q][0:1, :], in0=ps[0:1, bank, 0:256],
                                                                        in1=brow[q][0:1, :], op=ALU.add),
                         reads=[("ps", bank), ("brow", q)], writes=[("res", q)])
                    P.op("act", lambda e, q=q, k=k, lay=lay: e.dma_start(out=mod_d[lay:lay + 1, k * 256:(k + 1) * 256], in_=res[q][0:1, :]),
                         reads=[("res", q)], dma_key=("resst", q))
        gemm_phase(w_in0, jobs, hT, nslots=2 if side is not None else 3, side=side)
        A.release()

    def phase_l0_mix():
        A.mark()
        ccs = A.alloc([4, 512], BF16)
        scs = A.alloc([4, 512], BF16)
        psc = A.alloc([16], F32)
        P.op("sp", lambda e: e.dma_start(out=ccs, in_=cc_m.rearrange("(c p) n -> p c n", p=128)), writes=["ccs"], dma_key="ccs")
        P.op("sp", lambda e: e.dma_start(out=scs, in_=sc_m.rearrange("(c p) n -> p c n", p=128)), writes=["scs"], dma_key="scs")
        P.op("sp", lambda e: e.dma_start(out=psc, in_=pscale_l), writes=["psc"], dma_key="psc")
        sgt = [A.alloc([4, 512], BF16) for _ in range(2)]
        ygs = [A.alloc([4, 512], BF16) for _ in range(2)]
        wf = [A.alloc([4, 512], BF16) for _ in range(2)]
        fT = [A.alloc([4, 512], BF16) for _ in range(2)]
        A.mark()
        aT = [A.alloc([4, T], BF16) for _ in range(2)]
        ucs = A.alloc([2, 16, 512], BF16)
        clt = [A.alloc([2, 16, 512], BF16) for _ in range(2)]
        ucs_keys = [("ucs", m, lt) for m in range(2) for lt in range(16)]
        for g in range(4):
            sa = g % 2
            P.op("sp", lambda e, sa=sa, g=g: e.dma_start(out=aT[sa], in_=A0[g].rearrange("(c p) t -> p c t", p=128)),
                 writes=[("aT", sa)], dma_key=("aT", sa))
            P.op("pool", lambda e, sa=sa, g=g: e.dma_start(out=wf[sa], in_=w_fou[g].rearrange("(c p) n -> p c n", p=128)),
                 writes=[("wf", sa)], dma_key=("wf", sa))
            for lt in range(16):
                for m in range(2):
                    bank = nbank()
                    tw = ccs if m == 0 else scs

                    def mm(e, sa=sa, lt=lt, tw=tw, bank=bank):
                        for cc in range(4):
                            ins = e.matmul(ps[:, bank, :], aT[sa][:, cc, lt * 128:(lt + 1) * 128], tw[:, cc, :],
                                           start=(cc == 0), stop=(cc == 3))
                        return ins
                    P.op("pe", mm, reads=[("aT", sa), "ccs", "scs"], writes=[("ps", bank)])
                    evac(ucs[:, m, lt, :], bank, [("ucs", m, lt)])
            for tt in range(4):
                sc_ = (g * 4 + tt) % 2
                P.op("sp", lambda e, sc_=sc_, tt=tt: e.dma_start(
                    out=clt[sc_][:, 0], in_=cl_m[:, tt * 512:(tt + 1) * 512].rearrange("(c p) t -> p c t", p=128)),
                    writes=[("clt", sc_, 0)], dma_key=("clt", sc_, 0))
                P.op("sp", lambda e, sc_=sc_, tt=tt: e.dma_start(
                    out=clt[sc_][:, 1], in_=sl_m[:, tt * 512:(tt + 1) * 512].rearrange("(c p) t -> p c t", p=128)),
                    writes=[("clt", sc_, 1)], dma_key=("clt", sc_, 1))
                P.op("sp", lambda e, sc_=sc_, tt=tt, g=g: e.dma_start(
                    out=sgt[sc_], in_=SG[g * 512:(g + 1) * 512, tt * 512:(tt + 1) * 512].rearrange("(c p) t -> p c t", p=128)),
                    writes=[("sgt", sc_)], dma_key=("sgt", sc_))
                for ck in range(4):
                    bank = nbank()

                    def mm(e, sc_=sc_, ck=ck, bank=bank):
                        n = 0
                        for m in range(2):
                            for lc in range(16):
                                ins = e.matmul(ps[:, bank, :], ucs[:, m, lc, ck * 128:(ck + 1) * 128], clt[sc_][:, m, lc, :],
                                               start=(n == 0), stop=(n == 31))
                                n += 1
                        return ins
                    P.op("pe", mm, reads=ucs_keys + [("clt", sc_, 0), ("clt", sc_, 1)], writes=[("ps", bank)])
                    evac(fT[sc_][:, ck, :], bank, [("fT", sc_, ck)])
                for dk in range(4):
                    bank = nbank()

                    def mm(e, sa=sa, sc_=sc_, dk=dk, bank=bank):
                        for ck in range(4):
                            ins = e.matmul(ps[:, bank, :], wf[sa][:, ck, dk * 128:(dk + 1) * 128], fT[sc_][:, ck, :],
                                           start=(ck == 0), stop=(ck == 3))
                        return ins
                    P.op("pe", mm, reads=[("fT", sc_, ck) for ck in range(4)] + [("wf", sa)], writes=[("ps", bank)])
                    P.op("dve", lambda e, sc_=sc_, dk=dk, bank=bank: e.tensor_tensor(
                        out=ygs[sc_][:, dk, :], in0=ps[:, bank, :], in1=sgt[sc_][:, dk, :], op=ALU.mult),
                        reads=[("ps", bank), ("sgt", sc_)], writes=[("ygs", sc_, dk)])
                P.op("act", lambda e, sc_=sc_, g=g, tt=tt: e.dma_start(
                    out=YG[g * 512:(g + 1) * 512, tt * 512:(tt + 1) * 512].rearrange("(c p) t -> p c t", p=128), in_=ygs[sc_]),
                    reads=[("ygs", sc_, dk) for dk in range(4)], dma_key=("ygst", sc_))
        P.barrier()
        A.release()
        A.mark()
        bT = [A.alloc([16, 512], BF16) for _ in range(2)]
        bnd = [A.alloc([4, 6, 512], BF16) for _ in range(2)]
        for g in range(4):
            sa = g % 2
            P.op("sp", lambda e, sa=sa, g=g: e.dma_start(out=bT[sa], in_=B0[g].rearrange("(t p) c -> p t c", p=128)),
                 writes=[("bT", sa)], dma_key=("bT", sa))
            for a in range(4):
                P.op("sp", lambda e, sa=sa, g=g, a=a: e.dma_start(out=bnd[sa][:, a], in_=band_m[g, a].rearrange("j p t -> p j t")),
                     writes=[("bnd", sa, a)], dma_key=("bnd", sa, a))
            P.op("pool", lambda e, sa=sa, g=g: e.dma_start(out=wf[sa], in_=w_pool[g].rearrange("(c p) n -> p c n", p=128)),
                 writes=[("wf", sa)], dma_key=("wf", sa))
            for tt in range(4):
                sc_ = (g * 4 + tt) % 2
                P.op("sp", lambda e, sc_=sc_, tt=tt, g=g: e.dma_start(
                    out=sgt[sc_], in_=SG[2048 + g * 512:2048 + (g + 1) * 512, tt * 512:(tt + 1) * 512].rearrange("(c p) t -> p c t", p=128)),
                    writes=[("sgt", sc_)], dma_key=("sgt", sc_))
                for ck in range(4):
                    bank = nbank()

                    def mm(e, sa=sa, tt=tt, ck=ck, bank=bank):
                        for j in range(6):
                            lt = min(max(4 * tt - 1 + j, 0), 15)
                            ins = e.matmul(ps[:, bank, :], bT[sa][:, lt, ck * 128:(ck + 1) * 128], bnd[sa][:, tt, j, :],
                                           start=(j == 0), stop=(j == 5))
                        return ins
                    P.op("pe", mm, reads=[("bT", sa), ("bnd", sa, tt)], writes=[("ps", bank)])
                    evac(fT[sc_][:, ck, :], bank, [("fT", sc_, ck)])
                for dk in range(4):
                    bank = nbank()

                    def mm(e, sa=sa, sc_=sc_, dk=dk, bank=bank):
                        for ck in range(4):
                            ins = e.matmul(ps[:, bank, :], wf[sa][:, ck, dk * 128:(dk + 1) * 128], fT[sc_][:, ck, :],
                                           start=(ck == 0), stop=(ck == 3))
                        return ins
                    P.op("pe", mm, reads=[("fT", sc_, ck) for ck in range(4)] + [("wf", sa)], writes=[("ps", bank)])
                    P.op("dve", lambda e, sc_=sc_, dk=dk, bank=bank, g=g: e.scalar_tensor_tensor(
                        out=ygs[sc_][:, dk, :], in0=ps[:, bank, :], scalar=psc[:, g * 4 + dk:g * 4 + dk + 1],
                        in1=sgt[sc_][:, dk, :], op0=ALU.mult, op1=ALU.mult),
                        reads=[("ps", bank), ("sgt", sc_), "psc"], writes=[("ygs", sc_, dk)])
                P.op("act", lambda e, sc_=sc_, g=g, tt=tt: e.dma_start(
                    out=YG[2048 + g * 512:2048 + (g + 1) * 512, tt * 512:(tt + 1) * 512].rearrange("(c p) t -> p c t", p=128), in_=ygs[sc_]),
                    reads=[("ygs", sc_, dk) for dk in range(4)], dma_key=("ygst", sc_))
        P.barrier()
        A.release()
        A.release()

    def phase_out(W, xsrc, dst, layer, yg):
        A.mark()
        gate_b = A.alloc([D], F32)
        xs = [A.alloc([4, 256], F32) for _ in range(2)]
        os_ = [A.alloc([4, 256], F32) for _ in range(2)]
        for q in range(4):
            P.op("sp", lambda e, q=q: e.dma_start(
                out=yg[:, q * 8:(q + 1) * 8, :], in_=YG[q * 1024:(q + 1) * 1024, :].rearrange("(c p) t -> p c t", p=128)),
                dma_key=("ygld", q))
        P.op("sp", lambda e: e.dma_start(out=gate_b, in_=mod_d[layer, 2 * D:3 * D].partition_broadcast(128)), dma_key="gate_b")
        P.barrier()
        jobs = []
        cnt = {"n": 0}
        for j in range(16):
            col = j * 256

            def epi(i, bank, col=col):
                q = i // 4
                if i % 4 == 0:
                    cnt["n"] += 1
                s = cnt["n"] % 2
                if i % 4 == 0:
                    P.op("sp", lambda e, s=s, q=q, col=col: e.dma_start(
                        out=xs[s], in_=xsrc[q * 512:(q + 1) * 512, col:col + 256].rearrange("(t p) c -> p t c", p=128)),
                        writes=[("xs", s)], dma_key=("xs", s))
                P.op("dve", lambda e, s=s, i=i, bank=bank, col=col: e.tensor_tensor(
                    out=os_[s][:, i % 4, :], in0=ps[:, bank, 0:256], in1=gate_b[:, col:col + 256], op=ALU.mult),
                    reads=[("ps", bank)], writes=[("os", s, i % 4)])
                P.op("dve", lambda e, s=s, i=i: e.tensor_tensor(
                    out=os_[s][:, i % 4, :], in0=os_[s][:, i % 4, :], in1=xs[s][:, i % 4, :], op=ALU.add),
                    reads=[("xs", s), ("os", s, i % 4)], writes=[("os", s, i % 4)])
                if i % 4 == 3:
                    P.op("act", lambda e, s=s, q=q, col=col: e.dma_start(
                        out=dst[q * 512:(q + 1) * 512, col:col + 256].rearrange("(t p) c -> p t c", p=128), in_=os_[s]),
                        reads=[("os", s, k) for k in range(4)], dma_key=("osst", s))
            jobs.append((col, "tm", epi))
        gemm_phase(W, jobs, yg, nslots=2)
        A.release()

    def phase_l1_in(hT):
        A.mark()
        gq = A.alloc([128], F32)
        gk = A.alloc([128], F32)
        P.op("sp", lambda e: e.dma_start(out=gq, in_=qg_d.partition_broadcast(128)), dma_key="gq")
        P.op("sp", lambda e: e.dma_start(out=gk, in_=kg_d.partition_broadcast(128)), dma_key="gk")
        P.barrier()
        junkq = A.alloc([128], BF16)
        ssq = [A.alloc([2], F32) for _ in range(4)]
        rq = [A.alloc([2], F32) for _ in range(4)]
        qst = [A.alloc([4, 256], BF16) for _ in range(2)]
        kst = [A.alloc([4, 256], F32) for _ in range(2)]
        cnt = {"n": 0, "g": 0}

        def make_qk(jq, is_k):
            col_out = jq * 256
            gb = gk if is_k else gq

            def epi(i, bank):
                s2 = cnt["n"] % 4
                cnt["n"] += 1
                if i % 4 == 0:
                    cnt["g"] += 1
                s = cnt["g"] % 2
                q4 = i // 4
                for hd in range(2):
                    P.op("act", lambda e, hd=hd, s2=s2, bank=bank: e.activation(
                        out=junkq, in_=ps[:, bank, hd * 128:(hd + 1) * 128], func=AF.Square, accum_out=ssq[s2][:, hd:hd + 1]),
                        reads=[("ps", bank)], writes=["junkq", ("ssq", s2, hd)])
                P.op("act", lambda e, s2=s2: e.activation(out=ssq[s2], in_=ssq[s2], func=AF.Sqrt, scale=1.0 / 128, bias=EPS),
                     reads=[("ssq", s2, 0), ("ssq", s2, 1)], writes=[("ssq", s2, 0), ("ssq", s2, 1)])
                P.op("dve", lambda e, s2=s2: e.reciprocal(out=rq[s2], in_=ssq[s2]),
                     reads=[("ssq", s2, 0), ("ssq", s2, 1)], writes=[("rq", s2)])
                for hd in range(2):
                    if is_k:
                        o = kst[s][:, i % 4, hd * 128:(hd + 1) * 128]
                        wk = ("kst", s, i % 4, hd)
                    else:
                        o = qst[s][:, i % 4, hd * 128:(hd + 1) * 128]
                        wk = ("qst", s, i % 4, hd)
                    P.op("dve", lambda e, o=o, hd=hd, s2=s2, bank=bank: e.scalar_tensor_tensor(
                        out=o, in0=ps[:, bank, hd * 128:(hd + 1) * 128], scalar=rq[s2][:, hd:hd + 1], in1=gb,
                        op0=ALU.mult, op1=ALU.mult), reads=[("ps", bank), ("rq", s2)], writes=[wk])
                if is_k:
                    P.op("act", lambda e, s=s, i=i: e.activation(out=qst[s][:, i % 4, :], in_=kst[s][:, i % 4, :], func=AF.Copy),
                         reads=[("kst", s, i % 4, 0), ("kst", s, i % 4, 1)], writes=[("qst", s, i % 4, 0), ("qst", s, i % 4, 1)])
                if i % 4 == 3:
                    dstb = (Ks if is_k else Qs)[q4 * 512:(q4 + 1) * 512, col_out:col_out + 256].rearrange("(t p) c -> p t c", p=128)
                    P.op("act", lambda e, s=s, dstb=dstb: e.dma_start(out=dstb, in_=qst[s]),
                         reads=[("qst", s, k4, hd) for k4 in range(4) for hd in range(2)], dma_key=("qstst", s))
                    if is_k:
                        for hd in range(2):
                            h = jq * 2 + hd
                            d = sk_d[h, q4 * 512:(q4 + 1) * 512, :].rearrange("(t p) d -> p t d", p=128)
                            P.op("act", lambda e, s=s, d=d, hd=hd: e.dma_start(out=d, in_=kst[s][:, :, hd * 128:(hd + 1) * 128]),
                                 reads=[("kst", s, k4, hd) for k4 in range(4)], dma_key=("kstst", s, hd))
            return epi
        jobs = []
        PARTS = DBG.get("l1_parts", ["q", "k", "v", "glu", "gate"])
        NJ = DBG.get("l1_nj", 8)
        for jq in range(NJ if "q" in PARTS else 0):
            jobs.append((jq * 256, "tm", make_qk(jq, False)))
        for jq in range(NJ if "k" in PARTS else 0):
            jobs.append((2048 + jq * 256, "tm", make_qk(jq, True)))

        def make_v(jq):
            def epi(i, bank):
                if i % 4 == 0:
                    cnt["g"] += 1
                s = cnt["g"] % 2
                q4 = i // 4
                P.op("dve", lambda e, s=s, i=i, bank=bank: e.tensor_copy(out=kst[s][:, i % 4, :], in_=ps[:, bank, 0:256]),
                     reads=[("ps", bank)], writes=[("kst", s, i % 4, 0), ("kst", s, i % 4, 1)])
                P.op("act", lambda e, s=s, i=i, bank=bank: e.activation(out=qst[s][:, i % 4, :], in_=ps[:, bank, 0:256], func=AF.Copy),
                     reads=[("ps", bank)], writes=[("qst", s, i % 4, 0), ("qst", s, i % 4, 1)])
                if i % 4 == 3:
                    dstb = Vs[q4 * 512:(q4 + 1) * 512, jq * 256:(jq + 1) * 256].rearrange("(t p) c -> p t c", p=128)
                    P.op("act", lambda e, s=s, dstb=dstb: e.dma_start(out=dstb, in_=qst[s]),
                         reads=[("qst", s, k4, hd) for k4 in range(4) for hd in range(2)], dma_key=("qstst", s))
                    for hd in range(2):
                        h = jq * 2 + hd
                        d = sv_d[h, q4 * 512:(q4 + 1) * 512, :].rearrange("(t p) d -> p t d", p=128)
                        P.op("act", lambda e, s=s, d=d, hd=hd: e.dma_start(out=d, in_=kst[s][:, :, hd * 128:(hd + 1) * 128]),
                             reads=[("kst", s, k4, hd) for k4 in range(4)], dma_key=("kstst", s, hd))
            return epi
        for jq in range(NJ if "v" in PARTS else 0):
            jobs.append((4096 + jq * 256, "tm", make_v(jq)))
        gemm_phase(w_in1, jobs, hT)
        A.release()

        A.mark()
        zt = A.alloc([16], BF16)
        P.op("dve", lambda e: e.memset(zt, 0.0), writes=["zt"])
        for cq in range(16):
            P.op("sp", lambda e, cq=cq: e.dma_start(out=Gs[cq * 128:(cq + 1) * 128, 0:15], in_=zt[:, 0:15]),
                 reads=["zt"], dma_key="zt0")
            P.op("sp", lambda e, cq=cq: e.dma_start(out=Gs[cq * 128:(cq + 1) * 128, T + 15:T + 30], in_=zt[:, 0:15]),
                 reads=["zt"], dma_key="zt1")
        sig = [A.alloc([T], BF16) for _ in range(2)]
        stG = [A.alloc([T], BF16) for _ in range(2)]
        jobs = []
        for jb in range(NJ if "glu" in PARTS else 0):
            jobs.append((6144 + 2048 + jb * 256, "fm", make_fm_store(sig, lambda half: None, AF.Sigmoid)))
            jobs.append((6144 + jb * 256, "fm", make_fm_store(
                stG, lambda half, jb=jb: Gs[jb * 256 + half * 128: jb * 256 + (half + 1) * 128, 15:T + 15], None,
                in1=lambda half: (sig[half], ("stg", id(sig), half)))))
        gemm_phase(w_in1, jobs, hT)
        A.release()

        A.mark()
        stS = [A.alloc([T], BF16) for _ in range(3)]
        jobs = []
        for j in range(2 * NJ if "gate" in PARTS else 0):
            jobs.append((10240 + j * 256, "fm", make_fm_store(
                stS, lambda half, j=j: SG[j * 256 + half * 128: j * 256 + (half + 1) * 128, :], AF.Silu)))
        gemm_phase(w_in1, jobs, hT)
        A.release()

    def phase_l1_attn():
        A.mark()
        mk = A.alloc([6, 640], F32)
        ctxb = A.alloc([1], F32)
        P.op("sp", lambda e: e.dma_start(out=mk, in_=maskadd), dma_key="mk")
        P.op("sp", lambda e: e.dma_start(out=ctxb, in_=ctxbias_d), dma_key="ctxb")
        P.barrier()
        qtm = [A.alloc([16, 128], BF16) for _ in range(2)]
        ktm = [A.alloc([16, 128], BF16) for _ in range(2)]
        vtm = [A.alloc([16, 128], BF16) for _ in range(2)]
        kctm = [A.alloc([2, 128], BF16) for _ in range(2)]
        vctm = [A.alloc([2, 128], BF16) for _ in range(2)]
        bia = [A.alloc([1152], F32) for _ in range(2)]
        sgh = [A.alloc([T], BF16) for _ in range(2)]
        qT = [A.alloc([T], BF16) for _ in range(2)]
        kT = [A.alloc([T], BF16) for _ in range(2)]
        kcT = [A.alloc([256], BF16) for _ in range(2)]
        bm = [A.alloc([6, 640], BF16) for _ in range(2)]
        ctxm = A.alloc([256], BF16)
        pT = [A.alloc([896], BF16) for _ in range(3)]
        rs = [A.alloc([512], F32) for _ in range(2)]
        on = [A.alloc([512], F32) for _ in range(2)]
        ygst = [A.alloc([T], BF16) for _ in range(2)]
        scale = 128.0 ** -0.5
        P.op("dve", lambda e: e.tensor_scalar(out=mk, in0=mk, scalar1=1.0 / scale, scalar2=None, op0=ALU.mult), writes=["mk"])
        zc = A.alloc([256], F32)
        P.op("dve", lambda e: e.memset(zc, 0.0), writes=["zc"])
        P.op("dve", lambda e: e.tensor_scalar(out=ctxm, in0=zc, scalar1=ctxb[:, 0:1], scalar2=1.0 / scale, op0=ALU.add, op1=ALU.mult),
             reads=["zc"], writes=["ctxm"])
        P.barrier()
        cn = {"q": 0}

        def preamble(h):
            s = h % 2
            tmrows = lambda Z: Z[:, h * 128:(h + 1) * 128].rearrange("(t p) d -> p t d", p=128)
            P.op("sp", lambda e, a=tmrows(Qs): e.dma_start(out=qtm[s], in_=a), writes=[("qtm", s)], dma_key=("qtm", s))
            P.op("sp", lambda e, a=tmrows(Ks): e.dma_start(out=ktm[s], in_=a), writes=[("ktm", s)], dma_key=("ktm", s))
            P.op("sp", lambda e, a=tmrows(Vs): e.dma_start(out=vtm[s], in_=a), writes=[("vtm", s)], dma_key=("vtm", s))
            P.op("pool", lambda e: e.dma_start(out=kctm[s], in_=cache_k[h].rearrange("(t p) d -> p t d", p=128)),
                 writes=[("kctm", s)], dma_key=("kctm", s))
            P.op("pool", lambda e: e.dma_start(out=vctm[s], in_=cache_v[h].rearrange("(t p) d -> p t d", p=128)),
                 writes=[("vctm", s)], dma_key=("vctm", s))
            P.op("sp", lambda e: e.dma_start(out=bia[s], in_=bias_t[h]), writes=[("bia", s)], dma_key=("bia", s))
            P.op("sp", lambda e: e.dma_start(out=sgh[s], in_=SG[h * 128:(h + 1) * 128, :]), writes=[("sgh", s)], dma_key=("sgh", s))
            for (src, dst, nt, rk, wkk) in ((qtm, qT, 16, "qtm", "qT"), (ktm, kT, 16, "ktm", "kT"), (kctm, kcT, 2, "kctm", "kcT")):
                for g8 in range((nt + 3) // 4):
                    bank = 6 + (cn["q"] % 2)
                    cn["q"] += 1
                    n8 = min(4, nt - g8 * 4)

                    def tr(e, src=src, g8=g8, n8=n8, bank=bank):
                        for j in range(n8):
                            ins = e.transpose(out=psb(bank)[:, j * 128:(j + 1) * 128], in_=src[s][:, g8 * 4 + j, :], identity=ident)
                        return ins
                    P.op("pe", tr, reads=[(rk, s)], writes=[("ps", bank)])
                    evac(dst[s][:, g8 * 512:g8 * 512 + n8 * 128], bank, [(wkk, s, g8)], src=psb(bank)[:, 0:n8 * 128])
            for v in range(6):
                o_v = _VREP[v] - _base(_VREP[v])
                P.op("dve", lambda e, v=v, o_v=o_v: e.scalar_tensor_tensor(
                    out=bm[s][:, v, :], in0=bia[s][:, (4 - o_v) * 128:(9 - o_v) * 128], scalar=1.0 / scale, in1=mk[:, v, :],
                    op0=ALU.mult, op1=ALU.add),
                    reads=[("bia", s)], writes=[("bm", s, v)])

        def scores(n, h, i):
            s = h % 2
            b = _base(i)
            sp2 = n % 2
            bA, bB = sp2 * 2, sp2 * 2 + 1
            psA = ps[:, bA, :].rearrange("p (j q) -> p j q", j=4)
            psB = ps[:, bB, :].rearrange("p (j q) -> p j q", j=4)

            v = _variant(i)

            def sc_mm(e):
                for j in range(5):
                    o = psA[:, j, :] if j < 4 else psB[:, 0, :]
                    e.matmul(o, kT[s][:, (b + j) * 128:(b + j + 1) * 128], qT[s][:, i * 128:(i + 1) * 128],
                             start=True, stop=False)
                    ins = e.matmul(o, ident, bm[s][:, v, j * 128:(j + 1) * 128], start=False, stop=True)
                for j in range(2):
                    e.matmul(psB[:, 1 + j, :], kcT[s][:, j * 128:(j + 1) * 128], qT[s][:, i * 128:(i + 1) * 128],
                             start=True, stop=False)
                    ins = e.matmul(psB[:, 1 + j, :], ident, ctxm[:, j * 128:(j + 1) * 128], start=False, stop=True)
                return ins
            P.op("pe", sc_mm, reads=[("qT", s, g) for g in range(4)] + [("kT", s, g) for g in range(4)] + [("kcT", s, 0), ("bm", s, v)],
                 writes=[("ps", bA), ("ps", bB)])

        def rest(n, h, i):
            s = h % 2
            v = _variant(i)
            b = _base(i)
            sp2 = n % 2
            sp3 = n % 3
            bA, bB = sp2 * 2, sp2 * 2 + 1
            i4, ii = i // 4, i % 4
            so = (h * 4 + i4) % 2
            P.op("act", lambda e: e.activation(out=pT[sp3][:, 0:512], in_=ps[:, bA, :], func=AF.Exp, scale=scale),
                 reads=[("ps", bA)], writes=[("pT", sp3, 0)])
            P.op("act", lambda e: e.activation(out=pT[sp3][:, 512:896], in_=ps[:, bB, 0:384], func=AF.Exp, scale=scale),
                 reads=[("ps", bB)], writes=[("pT", sp3, 1)])

            def pv_mm(e):
                for j in range(7):
                    lhs = vtm[s][:, b + j, :] if j < 5 else vctm[s][:, j - 5, :]
                    ins = e.matmul(ps[:, 4, ii * 128:(ii + 1) * 128], lhs, pT[sp3][:, j * 128:(j + 1) * 128], start=(j == 0), stop=(j == 6))
                for j in range(7):
                    ins = e.matmul(ps[:, 5, ii * 128:(ii + 1) * 128], onesb, pT[sp3][:, j * 128:(j + 1) * 128], start=(j == 0), stop=(j == 6))
                return ins
            P.op("pe", pv_mm, reads=[("pT", sp3, 0), ("pT", sp3, 1), ("vtm", s), ("vctm", s)],
                 writes=[("ps", 4, ii), ("ps", 5, ii)])
            if ii == 3:
                okeys = [("ps", 4, k) for k in range(4)]
                skeys = [("ps", 5, k) for k in range(4)]
                P.op("dve", lambda e: e.reciprocal(out=rs[so], in_=ps[:, 5, :]), reads=skeys, writes=[("rs", so)])
                P.op("dve", lambda e: e.tensor_tensor(out=on[so], in0=ps[:, 4, :], in1=rs[so], op=ALU.mult),
                     reads=okeys + [("rs", so)], writes=[("on", so)])
                P.op("dve", lambda e: e.tensor_tensor(
                    out=ygst[s][:, i4 * 512:(i4 + 1) * 512], in0=on[so], in1=sgh[s][:, i4 * 512:(i4 + 1) * 512], op=ALU.mult),
                    reads=[("on", so), ("sgh", s)], writes=[("ygst", s, i4)])
            if i == 15:
                P.op("act", lambda e: e.dma_start(out=YG[h * 128:(h + 1) * 128, :], in_=ygst[s]),
                     reads=[("ygst", s, k) for k in range(4)], dma_key=("ygstst", s))

        tiles = [(h, i) for h in range(16) for i in range(16)]
        for n, (h, i) in enumerate(tiles):
            if i == 0:
                preamble(h)
            scores(n, h, i)
            if n >= 1:
                rest(n - 1, *tiles[n - 1])
        rest(len(tiles) - 1, *tiles[-1])
        P.barrier()
        A.release()

    def phase_l1_conv():
        A.mark()
        flag = A.alloc([1], F32)
        dwt = A.alloc([16, 31], F32)
        dwb = A.alloc([16], F32)
        lng = A.alloc([16], F32)
        lnb = A.alloc([16], F32)
        wpw = A.alloc([16, 2048], BF16)
        P.op("sp", lambda e: e.dma_start(out=flag, in_=flag_d), dma_key="c0")
        P.op("sp", lambda e: e.dma_start(out=dwt, in_=dw_l), dma_key="c1")
        P.op("sp", lambda e: e.dma_start(out=dwb, in_=dwb_l), dma_key="c2")
        P.op("sp", lambda e: e.dma_start(out=lng, in_=lng_l), dma_key="c3")
        P.op("sp", lambda e: e.dma_start(out=lnb, in_=lnb_l), dma_key="c4")
        for q in range(4):
            P.op("pool", lambda e, q=q: e.dma_start(
                out=wpw[:, q * 4:(q + 1) * 4, :], in_=w_pw[q * 512:(q + 1) * 512, :].rearrange("(c p) n -> p c n", p=128)),
                dma_key=("wpw", q))
        P.barrier()
        Gp = [A.alloc([2, 286], BF16) for _ in range(3)]
        cv = A.alloc([16, 512], F32)
        dg = [A.alloc([31, 128], BF16) for _ in range(2)]
        sq = [A.alloc([512], F32) for _ in range(2)]
        mean = A.alloc([512], F32)
        rstd = A.alloc([512], F32)
        tmp = A.alloc([512], F32)
        t1 = [A.alloc([512], F32) for _ in range(2)]
        zT = A.alloc([16, 512], BF16)
        sgt = A.alloc([16, 512], BF16)
        ygs = A.alloc([16, 512], BF16)
        W2 = T + 30
        n = 0
        for tt in range(4):
            P.op("sp", lambda e, tt=tt: e.dma_start(
                out=sgt, in_=SG[2048:4096, tt * 512:(tt + 1) * 512].rearrange("(c p) t -> p c t", p=128)),
                writes=["sgt"], dma_key="sgt")
            pend = []
            for cc in range(16):
                n += 1
                s3 = n % 3
                s2 = n % 2
                src = bass.AP(Gs.tensor, cc * 128 * W2 + tt * 512, [[W2, 128], [256, 2], [1, 286]])
                P.op("sp", lambda e, s3=s3, src=src: e.dma_start(out=Gp[s3], in_=src), writes=[("Gp", s3)], dma_key=("Gp", s3))
                P.op("dve", lambda e, s3=s3: e.tensor_scalar(out=Gp[s3][:, :, 0:15], in0=Gp[s3][:, :, 0:15],
                                                           scalar1=flag[:, 0:1], scalar2=None, op0=ALU.mult),
                     reads=[("Gp", s3)], writes=[("Gp", s3)])
                P.op("dve", lambda e, s3=s3: e.tensor_scalar(out=Gp[s3][:, :, 271:286], in0=Gp[s3][:, :, 271:286],
                                                           scalar1=flag[:, 0:1], scalar2=None, op0=ALU.mult),
                     reads=[("Gp", s3)], writes=[("Gp", s3)])
                dgs = dg[n % 2]
                in0b = bass.AP(ident.tensor, ident.offset, [list(ident.ap[0]), [0, 31], [1, 128]])
                dsl = dwt[:, cc, :]
                in1b = bass.AP(dsl.tensor, dsl.offset, [list(dsl.ap[0]), [1, 31], [0, 128]])
                P.op("dve", lambda e, dgs=dgs, in0b=in0b, in1b=in1b: e.tensor_tensor(out=dgs, in0=in0b, in1=in1b, op=ALU.mult),
                     reads=["ident"], writes=[("dg", n % 2)])
                cbank = 2 + (n % 2)

                def cv_mm(e, s3=s3, dgs=dgs, cbank=cbank):
                    for k in range(31):
                        ins = e.matmul(ps[:, cbank, :].rearrange("p (s t) -> p s t", s=2), dgs[:, k, :], Gp[s3][:, :, k:k + 256],
                                       start=(k == 0), stop=(k == 30))
                    return ins
                P.op("pe", cv_mm, reads=[("Gp", s3), ("dg", n % 2)], writes=[("ps", cbank)])
                P.op("act", lambda e, cc=cc, cbank=cbank: e.activation(out=cv[:, cc, :], in_=ps[:, cbank, :], func=AF.Identity,
                                                                     bias=dwb[:, cc:cc + 1]),
                     reads=[("ps", cbank)], writes=[("cv", cc)])
                P.op("act", lambda e, s2=s2, cc=cc: e.activation(out=sq[s2], in_=cv[:, cc, :], func=AF.Square),
                     reads=[("cv", cc)], writes=[("sq", s2)])

                def st_mm(e, cc=cc, s2=s2):
                    e.matmul(ps[:, 0, :], onesf, cv[:, cc, :], start=(cc == 0), stop=(cc == 15))
                    return e.matmul(ps[:, 1, :], onesf, sq[s2], start=(cc == 0), stop=(cc == 15))
                pend.append((st_mm, [("cv", cc), ("sq", s2)]))
                if len(pend) > 1:
                    f, rd = pend.pop(0)
                    P.op("pe", f, reads=rd, writes=[("ps", 0), ("ps", 1)])
            while pend:
                f, rd = pend.pop(0)
                P.op("pe", f, reads=rd, writes=[("ps", 0), ("ps", 1)])
            P.op("dve", lambda e: e.tensor_scalar(out=mean, in0=ps[:, 0, :], scalar1=1.0 / 2048, scalar2=None, op0=ALU.mult),
                 reads=[("ps", 0)], writes=["mean"])
            P.op("dve", lambda e: e.tensor_tensor(out=tmp, in0=mean, in1=mean, op=ALU.mult), reads=["mean"], writes=["tmp"])
            P.op("dve", lambda e: e.scalar_tensor_tensor(out=rstd, in0=ps[:, 1, :], scalar=1.0 / 2048, in1=tmp,
                                                         op0=ALU.mult, op1=ALU.subtract),
                 reads=[("ps", 1), "tmp"], writes=["rstd"])
            P.op("act", lambda e: e.activation(out=rstd, in_=rstd, func=AF.Sqrt, scale=1.0, bias=EPS), reads=["rstd"], writes=["rstd"])
            P.op("dve", lambda e: e.reciprocal(out=rstd, in_=rstd), reads=["rstd"], writes=["rstd"])
            for cc in range(16):
                s2 = cc % 2
                P.op("dve", lambda e, s2=s2, cc=cc: e.tensor_tensor(out=t1[s2], in0=cv[:, cc, :], in1=mean, op=ALU.subtract),
                     reads=[("cv", cc), "mean"], writes=[("t1", s2)])
                P.op("dve", lambda e, s2=s2: e.tensor_tensor(out=t1[s2], in0=t1[s2], in1=rstd, op=ALU.mult),
                     reads=[("t1", s2), "rstd"], writes=[("t1", s2)])
                P.op("act", lambda e, s2=s2, cc=cc: e.activation(out=zT[:, cc, :], in_=t1[s2], func=AF.Silu,
                                                               scale=lng[:, cc:cc + 1], bias=lnb[:, cc:cc + 1]),
                     reads=[("t1", s2)], writes=[("zT", cc)])
            for nk in range(16):
                bank = 4 + (nk % 4)

                def pw_mm(e, nk=nk, bank=bank):
                    for kc in range(16):
                        ins = e.matmul(ps[:, bank, :], wpw[:, kc, nk * 128:(nk + 1) * 128], zT[:, kc, :],
                                       start=(kc == 0), stop=(kc == 15))
                    return ins
                P.op("pe", pw_mm, reads=[("zT", kc) for kc in range(16)], writes=[("ps", bank)])
                P.op("dve", lambda e, nk=nk, bank=bank: e.tensor_tensor(out=ygs[:, nk, :], in0=ps[:, bank, :], in1=sgt[:, nk, :], op=ALU.mult),
                     reads=[("ps", bank), "sgt"], writes=[("ygs", nk)])
            P.op("act", lambda e, tt=tt: e.dma_start(
                out=YG[2048:4096, tt * 512:(tt + 1) * 512].rearrange("(c p) t -> p c t", p=128), in_=ygs),
                reads=[("ygs", nk) for nk in range(16)], dma_key="ygsst")
        P.barrier()
        A.release()

    if has("adaln"):
        phase_adaln()
    if has("l0_norm") or has("l0_in"):
        A.mark()
        hT = A.alloc([32, T], BF16)
        if has("l0_norm"):
            phase_norm(0, x_d, hT)
            if DBG.get("dump_hT"):
                hdbg = dout("hT_dbg", [D, T], BF16)
                P.op("sp", lambda e: e.dma_start(out=hdbg.rearrange("(c p) t -> p c t", p=128), in_=hT), dma_key="hdbg")
                P.barrier()
        if has("l0_in"):
            phase_l0_in(hT)
        A.release()
    if has("l0_mix"):
        phase_l0_mix()
    if has("l0_out"):
        A.mark()
        yg = A.alloc([32, T], BF16)
        phase_out(w_out0, x_d, X1, 0, yg)
        A.release()
    if has("l1_norm") or has("l1_in"):
        A.mark()
        hT = A.alloc([32, T], BF16)
        if has("l1_norm"):
            phase_norm(1, X1, hT)
        if has("l1_in"):
            phase_l1_in(hT)
        A.release()
    if has("l1_attn"):
        phase_l1_attn()
    if has("l1_conv"):
        phase_l1_conv()
    if has("l1_out"):
        A.mark()
        yg = A.alloc([32, T], BF16)
        phase_out(w_out1, X1, y_d, 1, yg)
        A.release()
    P.final_wait("sp")
    P.emit()
    return nc


def _variant(i):
    return {0: 0, 1: 1, 14: 4, 15: 5}.get(i, 2 + (i % 2))


_VREP = {0: 0, 1: 1, 2: 2, 3: 3, 4: 14, 5: 15}


def _base(i):
    return min(max(i - 2, 0), 11)


def _const_tables(is_sample):
    Ls = 2048 if is_sample else 256
    c = np.arange(512)
    ang = 2.0 * np.pi * ((c[:, None] * c[None, :]) % 512) / 512.0
    cc = (np.cos(ang) / np.sqrt(512.0)).astype(NPBF)
    sc = (np.sin(ang) / np.sqrt(512.0)).astype(NPBF)
    l = np.arange(Ls)
    angl = 2.0 * np.pi * ((l[:, None] * l[None, :]) % Ls) / Ls
    cb = np.cos(angl) / np.sqrt(Ls)
    sb = -np.sin(angl) / np.sqrt(Ls)
    cl = np.zeros((T, T), np.float32)
    sl = np.zeros((T, T), np.float32)
    for s in range(T // Ls):
        cl[s * Ls:(s + 1) * Ls, s * Ls:(s + 1) * Ls] = cb
        sl[s * Ls:(s + 1) * Ls, s * Ls:(s + 1) * Ls] = sb
    band = np.zeros((4, 4, 6, 128, 512), np.float32)
    tl = np.arange(Ls)
    for g, win in enumerate((2, 4, 8, 16)):
        lo = np.maximum(tl - win // 2, 0)
        hi = np.minimum(tl + win // 2, Ls)
        cntv = (hi - lo).astype(np.float32)
        Mloc = np.zeros((Ls, Ls), np.float32)
        for t in range(Ls):
            Mloc[lo[t]:hi[t], t] = 1.0 / cntv[t]
            Mloc[t, t] -= 1.0
        M = np.zeros((T, T), np.float32)
        for s in range(T // Ls):
            M[s * Ls:(s + 1) * Ls, s * Ls:(s + 1) * Ls] = Mloc
        for tt in range(4):
            for j in range(6):
                lt = 4 * tt - 1 + j
                if 0 <= lt < 16:
                    band[g, tt, j] = M[lt * 128:(lt + 1) * 128, tt * 512:(tt + 1) * 512]
    mask = np.full((128, 6, 5, 128), NEG, np.float32)
    kl = np.arange(128)
    for v in range(6):
        i = _VREP[v]
        b = _base(i)
        for j in range(5):
            kt = b + j
            if is_sample:
                r = 2 * i + kl // 64
                qc = kl % 64
                kr = 2 * kt + kl // 64
                kc = kl % 64
                start = np.clip(r - 4, 0, 24)
                qs = np.clip(qc - 8, 0, 48)
                ok = ((kr[:, None] >= start[None, :]) & (kr[:, None] < start[None, :] + 8)
                      & (kc[:, None] >= qs[None, :]) & (kc[:, None] < qs[None, :] + 16))
                mask[:, v, j, :] = np.where(ok, 0.0, NEG)
            else:
                if kt // 2 == i // 2:
                    mask[:, v, j, :] = 0.0
    return dict(cc_m=cc, sc_m=sc, cl_m=cl.astype(NPBF), sl_m=sl.astype(NPBF), band_m=band.astype(NPBF),
                maskadd=np.ascontiguousarray(mask.reshape(128, 6, 640)),
                ctxbias=np.full((128, 1), 0.0 if is_sample else NEG, np.float32),
                flag=np.full((128, 1), 1.0 if is_sample else 0.0, np.float32),
                ident=np.eye(128, dtype=np.float32).astype(NPBF))


def _bias_table(rel_bias, is_sample):
    out = np.zeros((16, 128, 9, 128), np.float32)
    if is_sample:
        kl = np.arange(128)
        for o in range(-4, 5):
            dr = 2 * o + (kl[:, None] // 64) - (kl[None, :] // 64)
            dc = np.clip((kl[:, None] % 64) - (kl[None, :] % 64) + 15, 0, 30)
            okr = np.abs(dr) <= 7
            dri = np.clip(dr + 7, 0, 14)
            gathered = rel_bias[:, dri, dc]
            out[:, :, o + 4, :] = np.where(okr[None], gathered, 0.0)
    return np.ascontiguousarray(out.reshape(16, 128, 1152))


def _cols(v, n):
    return np.ascontiguousarray(np.asarray(v, np.float32).reshape(n, 128).T)


_CT_CACHE = {}


def prep_inputs(inp, core):
    is_sample = core >= 4
    if is_sample not in _CT_CACHE:
        _CT_CACHE[is_sample] = _const_tables(is_sample)
    ct = _CT_CACHE[is_sample]
    if is_sample:
        b = core - 4
        x = inp["x_sample"][b]
        cond = inp["c"][b]
        ck = inp["cache_k"][b, 0]
        cv = inp["cache_v"][b, 0]
    else:
        x = inp["x_prompt"][8 * core:8 * core + 8].reshape(T, D)
        cond = inp["c_ctx"]
        ck = np.zeros((16, 256, 128), np.float32)
        cv = np.zeros((16, 256, 128), np.float32)
    m = dict(ct)
    m.update(
        x=np.ascontiguousarray(x), cond_l=_cols(cond, 32),
        ng_l=np.stack([_cols(inp["norm_g"][0], 32), _cols(inp["norm_g"][1], 32)]),
        w_ada=inp["w_ada"], b_ada=inp["b_ada"],
        w_in_even=inp["w_in_even"][0], w_out_even=inp["w_out_even"][0],
        w_fourier=inp["w_fourier"][0], w_pool=inp["w_pool"][0], pscale_l=_cols(inp["pool_scale"][0], 16),
        w_in_odd=inp["w_in_odd"][0], w_out_odd=inp["w_out_odd"][0],
        q_norm_g=inp["q_norm_g"][0], k_norm_g=inp["k_norm_g"][0],
        bias_t=_bias_table(inp["rel_bias"][0], is_sample),
        cache_k=np.ascontiguousarray(ck), cache_v=np.ascontiguousarray(cv),
        dw_l=np.ascontiguousarray(inp["conv_dw"][0].reshape(31, 16, 128).transpose(2, 1, 0)),
        dwb_l=_cols(inp["conv_dw_b"][0], 16), lng_l=_cols(inp["conv_ln_g"][0], 16), lnb_l=_cols(inp["conv_ln_b"][0], 16),
        w_conv_pw=inp["w_conv_pw"][0],
    )
    return m


def kernel(**inputs):
    inp = {k: np.asarray(v) for k, v in inputs.items()}
    nc = build()
    in_maps = [prep_inputs(inp, c) for c in range(8)]
    res = run_bass_kernel_spmd(nc, in_maps, core_ids=list(range(8)))
    r = res.results
    y_prompt = np.concatenate([r[c]["y"].reshape(8, 256, D) for c in range(4)], axis=0)
    y_sample = np.stack([r[4 + b]["y"] for b in range(4)], axis=0)
    sk = np.concatenate([r[c]["sk"].reshape(16, 8, 256, 128).transpose(1, 0, 2, 3) for c in range(4)], axis=0)
    sv = np.concatenate([r[c]["sv"].reshape(16, 8, 256, 128).transpose(1, 0, 2, 3) for c in range(4)], axis=0)
    return (y_prompt.astype(np.float32), y_sample.astype(np.float32),
            np.ascontiguousarray(sk[:, None]).astype(np.float32), np.ascontiguousarray(sv[:, None]).astype(np.float32))
```

```python
import numpy as np
import ml_dtypes
import concourse.bass as bass
import concourse.mybir as mybir
from concourse.bass_utils import run_bass_kernel_spmd

F32 = mybir.dt.float32
BF16 = mybir.dt.bfloat16
AF = mybir.ActivationFunctionType
ALU = mybir.AluOpType
NPBF = ml_dtypes.bfloat16

T = 2048
D = 4096
NT = 16
EPS = 1e-6
NEG = -30000.0
ENG = ("sp", "act", "pool", "dve", "pe")


class Prog:
    def __init__(self, nc, eng_sems, dma_sems):
        self.nc = nc
        self.streams = {e: [] for e in ENG}
        self.eng_sem = eng_sems
        self.dma_pool = list(dma_sems)
        self.cnt = {}
        self.semobj = {}
        for s in list(eng_sems.values()) + list(dma_sems):
            self.cnt[id(s)] = 0
            self.semobj[id(s)] = s
        self.waited = {e: {} for e in ENG}
        self.last_w = {}
        self.readers = {}
        self.dma_map = {}
        self.dma_next = 0

    def _dma_sem(self, key):
        if key not in self.dma_map:
            assert self.dma_next < len(self.dma_pool), "out of dma sems"
            self.dma_map[key] = self.dma_pool[self.dma_next]
            self.dma_next += 1
        return self.dma_map[key]

    def _wait(self, eng, ev):
        sid, val = ev
        if self.waited[eng].get(sid, 0) < val:
            self.streams[eng].append(("wait", sid, val))
            self.waited[eng][sid] = val

    def op(self, eng, fn, reads=(), writes=(), dma_key=None):
        reads = list(reads)
        writes = list(writes)
        deps = set()
        for k in reads:
            if k in self.last_w:
                deps.add(self.last_w[k])
            if isinstance(k, tuple) and k[0] == "ps" and k not in writes:
                for (e2, r) in self.readers.get(k, ()):
                    if e2 != eng:
                        deps.add(r)
        for k in writes:
            if k in self.last_w:
                deps.add(self.last_w[k])
            for (e2, r) in self.readers.get(k, ()):
                deps.add(r)
        mx = {}
        for sid, val in deps:
            mx[sid] = max(mx.get(sid, 0), val)
        for sid in sorted(mx):
            self._wait(eng, (sid, mx[sid]))
        if dma_key is not None:
            sem = self._dma_sem(dma_key)
            inc = 16
        else:
            sem = self.eng_sem[eng]
            inc = 1
        self.cnt[id(sem)] += inc
        ev = (id(sem), self.cnt[id(sem)])
        self.streams[eng].append(("op", fn, id(sem), inc))
        for k in writes:
            self.last_w[k] = ev
            self.readers[k] = []
        for k in reads:
            if k not in writes:
                self.readers.setdefault(k, []).append((eng, ev))
        return ev

    def barrier(self):
        for e in ENG:
            for sid, c in self.cnt.items():
                if c > 0:
                    self._wait(e, (sid, c))
        self.last_w = {}
        self.readers = {}
        self.dma_map = {}
        self.dma_next = 0

    def final_wait(self, eng="sp"):
        for sid, c in self.cnt.items():
            if c > 0:
                self._wait(eng, (sid, c))

    def emit(self):
        nc = self.nc
        with nc.Block() as block:
            def replay(name, e):
                for it in self.streams[name]:
                    if it[0] == "wait":
                        e.wait_ge(self.semobj[it[1]], it[2])
                    else:
                        ins = it[1](e)
                        ins.then_inc(self.semobj[it[2]], it[3])

            @block.sync
            def _(e):
                replay("sp", e)

            @block.scalar
            def _(e):
                replay("act", e)

            @block.gpsimd
            def _(e):
                replay("pool", e)

            @block.vector
            def _(e):
                replay("dve", e)

            @block.tensor
            def _(e):
                replay("pe", e)


class Arena:
    def __init__(self, big, nbytes):
        self.big = big
        self.n = nbytes
        self.off = 0
        self.marks = []

    def mark(self):
        self.marks.append(self.off)

    def release(self):
        self.off = self.marks.pop()

    def alloc(self, shape_free, dtype):
        esz = 4 if dtype == F32 else 2
        n = int(np.prod(shape_free))
        nb = n * esz
        self.off = (self.off + 63) // 64 * 64
        assert self.off + nb <= self.n, f"SBUF arena overflow {self.off}+{nb}>{self.n}"
        ap = self.big[:, self.off // 2:(self.off + nb) // 2]
        self.off += nb
        if dtype == F32:
            ap = ap.bitcast(F32)
        if len(shape_free) == 2:
            ap = ap.rearrange("p (a b) -> p a b", a=shape_free[0])
        elif len(shape_free) == 3:
            ap = ap.rearrange("p (a b c) -> p a b c", a=shape_free[0], b=shape_free[1])
        elif len(shape_free) == 4:
            ap = ap.rearrange("p (a b c d) -> p a b c d", a=shape_free[0], b=shape_free[1], c=shape_free[2])
        return ap


DBG = {}
ALL_PHASES = ("adaln", "l0_norm", "l0_in", "l0_mix", "l0_out",
              "l1_norm", "l1_in", "l1_attn", "l1_conv", "l1_out")


def build(phases=ALL_PHASES, ext_in=(), ext_out=()):
    nc = bass.Bass("TRN2", target_bir_lowering=False)

    def din(name, shape, dt=F32):
        return nc.dram_tensor(name, list(shape), dt, kind="ExternalInput").ap()

    def dout(name, shape, dt=F32):
        return nc.dram_tensor(name, list(shape), dt, kind="ExternalOutput").ap()

    def dscr(name, shape, dt):
        kind = "ExternalInput" if name in ext_in else ("ExternalOutput" if name in ext_out else "Internal")
        return nc.dram_tensor(name, list(shape), dt, kind=kind).ap()

    has = lambda p: p in phases
    L0 = any(p.startswith("l0") for p in phases)
    L1 = any(p.startswith("l1") for p in phases)

    ident_d = din("ident", [128, 128], BF16)
    ng_l = din("ng_l", [2, 128, 32])
    if has("adaln"):
        cond_l = din("cond_l", [128, 32])
        w_ada = din("w_ada", [2, D, 3 * D])
        b_ada = din("b_ada", [2, 3 * D])
    if has("l0_norm") or has("l0_out"):
        x_d = din("x", [T, D])
    if has("l0_in"):
        w_in0 = din("w_in_even", [D, 2 * D])
    if has("l0_mix"):
        w_fou = din("w_fourier", [4, 512, 512])
        w_pool = din("w_pool", [4, 512, 512])
        pscale_l = din("pscale_l", [128, 16])
        cc_m = din("cc_m", [512, 512], BF16)
        sc_m = din("sc_m", [512, 512], BF16)
        cl_m = din("cl_m", [T, T], BF16)
        sl_m = din("sl_m", [T, T], BF16)
        band_m = din("band_m", [4, 4, 6, 128, 512], BF16)
    if has("l0_out"):
        w_out0 = din("w_out_even", [D, D])
    if has("l1_in"):
        w_in1 = din("w_in_odd", [D, 14336])
        qg_d = din("q_norm_g", [128])
        kg_d = din("k_norm_g", [128])
    if has("l1_attn"):
        bias_t = din("bias_t", [16, 128, 1152])
        maskadd = din("maskadd", [128, 6, 640])
        ctxbias_d = din("ctxbias", [128, 1])
        cache_k = din("cache_k", [16, 256, 128])
        cache_v = din("cache_v", [16, 256, 128])
    if has("l1_conv"):
        flag_d = din("flag", [128, 1])
        dw_l = din("dw_l", [128, 16, 31])
        dwb_l = din("dwb_l", [128, 16])
        lng_l = din("lng_l", [128, 16])
        lnb_l = din("lnb_l", [128, 16])
        w_pw = din("w_conv_pw", [2048, 2048])
    if has("l1_out"):
        w_out1 = din("w_out_odd", [D, D])

    if has("l1_out"):
        y_d = dout("y", [T, D])
    if has("l1_in"):
        sk_d = dout("sk", [16, T, 128])
        sv_d = dout("sv", [16, T, 128])

    mod_d = dscr("mod", [2, 3 * D], F32)
    if L0:
        A0 = dscr("A0", [4, 512, T], BF16)
        B0 = dscr("B0", [4, T, 512], BF16)
    SG = dscr("SG", [D, T], BF16)
    YG = dscr("YG", [D, T], BF16)
    X1 = dscr("X1", [T, D], F32)
    if L1:
        Qs = dscr("Qs", [T, 2048], BF16)
        Ks = dscr("Ks", [T, 2048], BF16)
        Vs = dscr("Vs", [T, 2048], BF16)
        Gs = dscr("Gs", [2048, T + 30], BF16)

    ARENA = 206 * 1024
    big = nc.alloc_sbuf_tensor("big", [128, ARENA // 2], BF16)
    ps = nc.alloc_psum_tensor("ps", [128, 8, 512], F32)
    sems = {e: nc.alloc_semaphore(f"s_{e}") for e in ENG}
    dsems = [nc.alloc_semaphore(f"d_{i}") for i in range(40)]
    P = Prog(nc, sems, dsems)
    A = Arena(big, ARENA)

    def psb(bank):
        return ps[:, bank, :].bitcast(BF16)

    ident = A.alloc([128], BF16)
    onesb = A.alloc([128], BF16)
    onesf = A.alloc([128], F32)
    s_bf = A.alloc([32], BF16)
    P.op("sp", lambda e: e.dma_start(out=ident, in_=ident_d), writes=["ident"], dma_key="ident")
    P.op("dve", lambda e: e.memset(onesb, 1.0), writes=["onesb"])
    P.op("dve", lambda e: e.memset(onesf, 1.0), writes=["onesf"])
    P.barrier()

    def wblock(W, col, n=256):
        return W[:, col:col + n].rearrange("(c p) n -> p c n", p=128)

    ADA_LAYERS = (0, 1) if DBG.get("ada_serial") or not has("l0_in") else (0,)

    def phase_adaln():
        A.mark()
        condt = A.alloc([32], F32)
        modrow = A.alloc([3 * D], F32)
        wb = [A.alloc([32, 512], BF16) for _ in range(3)]
        P.op("sp", lambda e: e.dma_start(out=condt, in_=cond_l), writes=["cond"], dma_key="cond")
        P.op("act", lambda e: e.activation(out=s_bf, in_=condt, func=AF.Silu), reads=["cond"], writes=["s"])
        blk = 0
        NBLK = 16 if 1 not in ADA_LAYERS else 24
        for layer in ADA_LAYERS:
            mkeys = [("modrow", nb) for nb in range(NBLK)]
            P.op("sp", lambda e, layer=layer: e.dma_start(out=modrow[0:1, 0:NBLK * 512], in_=b_ada[layer:layer + 1, 0:NBLK * 512]),
                 writes=mkeys, dma_key="modrow")
            for nb in range(NBLK):
                slot = blk % 3
                bank = blk % 8
                blk += 1
                P.op("pool", lambda e, slot=slot, layer=layer, nb=nb: e.dma_start(
                    out=wb[slot], in_=wblock(w_ada[layer], nb * 512, 512)),
                    writes=[("wb", slot)], dma_key=("wb", slot))

                def mm(e, slot=slot, bank=bank):
                    for c in range(32):
                        ins = e.matmul(ps[0:1, bank, :], s_bf[:, c:c + 1], wb[slot][:, c, :],
                                       start=(c == 0), stop=(c == 31))
                    return ins
                P.op("pe", mm, reads=[("wb", slot), "s"], writes=[("ps", bank)])
                P.op("dve", lambda e, bank=bank, nb=nb: e.tensor_tensor(
                    out=modrow[0:1, nb * 512:(nb + 1) * 512], in0=ps[0:1, bank, :],
                    in1=modrow[0:1, nb * 512:(nb + 1) * 512], op=ALU.add),
                    reads=[("ps", bank)], writes=[("modrow", nb)])
            P.op("sp", lambda e, layer=layer: e.dma_start(out=mod_d[layer:layer + 1, 0:NBLK * 512], in_=modrow[0:1, 0:NBLK * 512]),
                 reads=mkeys, dma_key="modst")
        P.barrier()
        A.release()

    def phase_norm(layer, xsrc, hT):
        A.mark()
        sc = A.alloc([32], F32)
        sh = A.alloc([32], F32)
        g = A.alloc([32], F32)
        sc1 = A.alloc([32], F32)
        xt = [A.alloc([D], F32) for _ in range(2)]
        junk = A.alloc([D], BF16)
        xn = [A.alloc([D], BF16) for _ in range(2)]
        ss = [A.alloc([1], F32) for _ in range(2)]
        rstd = [A.alloc([1], F32) for _ in range(2)]
        col = lambda off: bass.AP(mod_d.tensor, layer * 3 * D + off, [[1, 128], [128, 32]])
        P.op("sp", lambda e: e.dma_start(out=sh, in_=col(0), allow_slow_non_contiguous=True), writes=["sh"], dma_key="sh")
        P.op("sp", lambda e: e.dma_start(out=sc, in_=col(D), allow_slow_non_contiguous=True), writes=["sc"], dma_key="sc")
        P.op("sp", lambda e: e.dma_start(out=g, in_=ng_l[layer]), writes=["g"], dma_key="g")
        P.op("dve", lambda e: e.scalar_tensor_tensor(out=sc1, in0=sc, scalar=1.0, in1=g, op0=ALU.add, op1=ALU.mult),
             reads=["sc", "g"], writes=["sc1"])
        def stage_a(i):
            s = i % 2
            P.op("sp", lambda e, s=s, i=i: e.dma_start(out=xt[s], in_=xsrc[i * 128:(i + 1) * 128, :]),
                 writes=[("xt", s)], dma_key=("xt", s))
            P.op("act", lambda e, s=s: e.activation(out=junk, in_=xt[s], func=AF.Square, accum_out=ss[s]),
                 reads=[("xt", s)], writes=["junk", ("ss", s)])
            P.op("act", lambda e, s=s: e.activation(out=ss[s], in_=ss[s], func=AF.Sqrt, scale=1.0 / D, bias=EPS),
                 reads=[("ss", s)], writes=[("ss", s)])
            P.op("dve", lambda e, s=s: e.reciprocal(out=rstd[s], in_=ss[s]), reads=[("ss", s)], writes=[("rstd", s)])
            P.op("act", lambda e, s=s: e.activation(out=xn[s], in_=xt[s], func=AF.Copy, scale=rstd[s]),
                 reads=[("xt", s), ("rstd", s)], writes=[("xn", s)])

        def stage_b(i):
            s = i % 2
            for grp in range(8):
                bank = (i * 8 + grp) % 8

                def tr(e, s=s, grp=grp, bank=bank):
                    for j in range(4):
                        c = grp * 4 + j
                        ins = e.transpose(out=psb(bank)[:, j * 128:(j + 1) * 128],
                                          in_=xn[s][:, c * 128:(c + 1) * 128], identity=ident)
                    return ins
                P.op("pe", tr, reads=[("xn", s), "ident"], writes=[("ps", bank)])
                for j in range(4):
                    c = grp * 4 + j
                    P.op("dve", lambda e, bank=bank, j=j, c=c, i=i: e.tensor_scalar(
                        out=hT[:, c, i * 128:(i + 1) * 128], in0=psb(bank)[:, j * 128:(j + 1) * 128],
                        scalar1=sc1[:, c:c + 1], scalar2=sh[:, c:c + 1], op0=ALU.mult, op1=ALU.add),
                        reads=[("ps", bank), "sc1", "sh"])
        stage_a(0)
        for i in range(NT):
            if i + 1 < NT:
                stage_a(i + 1)
            stage_b(i)
        P.barrier()
        A.release()

    class G:
        bset = 0
        bank = 0

    def gemm_phase(W, jobs, hT, ncol=256, nslots=3, side=None):
        A.mark()
        wb = [A.alloc([32, ncol], BF16) for _ in range(nslots)]
        for ji, (col, mode, epi) in enumerate(jobs):
            slot = ji % nslots
            P.op("pool", lambda e, slot=slot, col=col: e.dma_start(out=wb[slot], in_=wblock(W, col, ncol)),
                 writes=[("wb", slot)], dma_key=("wb", slot))
            if mode == "fm":
                for half in range(ncol // 128):
                    banks = [G.bset * 4 + tt for tt in range(4)]
                    G.bset ^= 1

                    def mm(e, slot=slot, half=half, banks=banks):
                        for c in range(32):
                            for tt in range(4):
                                ins = e.matmul(ps[:, banks[tt], :], wb[slot][:, c, half * 128:(half + 1) * 128],
                                               hT[:, c, tt * 512:(tt + 1) * 512], start=(c == 0), stop=(c == 31))
                        return ins
                    P.op("pe", mm, reads=[("wb", slot)], writes=[("ps", b) for b in banks])
                    epi(half, banks)
                    if side is not None:
                        side(G.bset * 4 + 3)
            else:
                for i in range(NT):
                    bank = G.bank
                    G.bank = (G.bank + 1) % 8

                    def mm(e, slot=slot, i=i, bank=bank):
                        for c in range(32):
                            ins = e.matmul(ps[:, bank, 0:ncol], hT[:, c, i * 128:(i + 1) * 128], wb[slot][:, c, :],
                                           start=(c == 0), stop=(c == 31))
                        return ins
                    P.op("pe", mm, reads=[("wb", slot)], writes=[("ps", bank)])
                    epi(i, bank)
                    if side is not None and i % 8 == 7:
                        side((G.bank + 4) % 8)
        P.barrier()
        A.release()

    class St:
        pass

    def make_fm_store(stages, dst_rows, func, eng_copy="act", in1=None):
        state = {"n": 0}

        def epi(half, banks):
            s = state["n"] % len(stages)
            state["n"] += 1
            for tt in range(4):
                o = stages[s][:, tt * 512:(tt + 1) * 512]
                if in1 is not None:
                    src, key = in1(half)
                    P.op("dve", lambda e, o=o, b=banks[tt], tt=tt, src=src: e.tensor_tensor(
                        out=o, in0=ps[:, b, :], in1=src[:, tt * 512:(tt + 1) * 512], op=ALU.mult),
                        reads=[("ps", banks[tt]), key + (tt,)], writes=[("stg", id(stages), s, tt)])
                elif func is None:
                    P.op("dve", lambda e, o=o, b=banks[tt]: e.tensor_copy(out=o, in_=ps[:, b, :]),
                         reads=[("ps", banks[tt])], writes=[("stg", id(stages), s, tt)])
                else:
                    P.op("act", lambda e, o=o, b=banks[tt]: e.activation(out=o, in_=ps[:, b, :], func=func),
                         reads=[("ps", banks[tt])], writes=[("stg", id(stages), s, tt)])
            d = dst_rows(half)
            if d is not None:
                P.op("act", lambda e, s=s, d=d: e.dma_start(out=d, in_=stages[s]),
                     reads=[("stg", id(stages), s, tt) for tt in range(4)], dma_key=("stgst", id(stages), s))
        return epi

    rot = {"b": 0}

    def nbank():
        b = rot["b"]
        rot["b"] = (b + 1) % 8
        return b

    alt = {"n": 0}

    def evac(out, bank, writes, src=None):
        use_dve = src is not None
        src = ps[:, bank, :] if src is None else src
        alt["n"] += 1
        if alt["n"] % 2 or use_dve:
            P.op("dve", lambda e: e.tensor_copy(out=out, in_=src), reads=[("ps", bank)], writes=writes)
        else:
            P.op("act", lambda e: e.activation(out=out, in_=src, func=AF.Copy), reads=[("ps", bank)], writes=writes)

    def phase_l0_in(hT):
        A.mark()
        stA = [A.alloc([T], BF16) for _ in range(2)]
        stS = [A.alloc([T], BF16) for _ in range(2)]
        stB = [A.alloc([4, 256], BF16) for _ in range(2)]
        jobs = []
        for g in range(4):
            for blk in range(2):
                jobs.append((g * 512 + blk * 256, "fm", make_fm_store(
                    stA, lambda half, g=g, blk=blk: A0[g, blk * 256 + half * 128: blk * 256 + (half + 1) * 128, :], None)))
                for jj in range(2):
                    j = (g * 2 + blk) * 2 + jj
                    jobs.append((D + j * 256, "fm", make_fm_store(
                        stS, lambda half, j=j: SG[j * 256 + half * 128: j * 256 + (half + 1) * 128, :], AF.Silu)))
                stt = {"n": 0}

                def epi_b(i, bank, g=g, blk=blk, stt=stt):
                    s = (i // 4) % 2
                    P.op("dve", lambda e, s=s, i=i, bank=bank: e.tensor_copy(out=stB[s][:, i % 4, :], in_=ps[:, bank, 0:256]),
                         reads=[("ps", bank)], writes=[("stB", s, i % 4)])
                    if i % 4 == 3:
                        q = i // 4
                        d = B0[g, q * 512:(q + 1) * 512, blk * 256:(blk + 1) * 256].rearrange("(t p) c -> p t c", p=128)
                        P.op("act", lambda e, s=s, d=d: e.dma_start(out=d, in_=stB[s]),
                             reads=[("stB", s, k) for k in range(4)], dma_key=("stBst", s))
                jobs.append((2048 + g * 512 + blk * 256, "tm", epi_b))
        if DBG.get("l0_jobs"):
            jobs = [j for j in jobs if j[1] == DBG["l0_jobs"][0]][DBG["l0_jobs"][1]:DBG["l0_jobs"][2]]
        side = None
        if has("adaln") and 1 not in ADA_LAYERS:
            wa = [A.alloc([32, 256], BF16)]
            brow = [A.alloc([256], F32) for _ in range(2)]
            res = [A.alloc([256], F32) for _ in range(2)]
            sk_ = {"k": 0}
            sblocks = [(0, 32 + k) for k in range(16)] + [(1, k) for k in range(48)]

            def side(bank):
                for _ in range(1):
                    if sk_["k"] >= len(sblocks):
                        return
                    lay, k = sblocks[sk_["k"]]
                    sk_["k"] += 1
                    q = sk_["k"] % 2
                    P.op("pool", lambda e, k=k, lay=lay: e.dma_start(out=wa[0], in_=wblock(w_ada[lay], k * 256, 256)),
                         writes=[("wa", 0)], dma_key=("wa", 0))
                    P.op("sp", lambda e, q=q, k=k, lay=lay: e.dma_start(out=brow[q][0:1, :], in_=b_ada[lay:lay + 1, k * 256:(k + 1) * 256]),
                         writes=[("brow", q)], dma_key=("brow", q))

                    def mm(e, bank=bank):
                        for c in range(32):
                            ins = e.matmul(ps[0:1, bank, 0:256], s_bf[:, c:c + 1], wa[0][:, c, :], start=(c == 0), stop=(c == 31))
                        return ins
                    P.op("pe", mm, reads=[("wa", 0)], writes=[("ps", bank)])
                    P.op("dve", lambda e, q=q, bank=bank: e.tensor_tensor(out=res[q][0:1, :], in0=ps[0:1, bank, 0:256],
                                                                        in1=brow[q][0:1, :], op=ALU.add),
                         reads=[("ps", bank), ("brow", q)], writes=[("res", q)])
                    P.op("act", lambda e, q=q, k=k, lay=lay: e.dma_start(out=mod_d[lay:lay + 1, k * 256:(k + 1) * 256], in_=res[q][0:1, :]),
                         reads=[("res", q)], dma_key=("resst", q))
        gemm_phase(w_in0, jobs, hT, nslots=2 if side is not None else 3, side=side)
        A.release()

    def phase_l0_mix():
        A.mark()
        ccs = A.alloc([4, 512], BF16)
        scs = A.alloc([4, 512], BF16)
        psc = A.alloc([16], F32)
        P.op("sp", lambda e: e.dma_start(out=ccs, in_=cc_m.rearrange("(c p) n -> p c n", p=128)), writes=["ccs"], dma_key="ccs")
        P.op("sp", lambda e: e.dma_start(out=scs, in_=sc_m.rearrange("(c p) n -> p c n", p=128)), writes=["scs"], dma_key="scs")
        P.op("sp", lambda e: e.dma_start(out=psc, in_=pscale_l), writes=["psc"], dma_key="psc")
        sgt = [A.alloc([4, 512], BF16) for _ in range(2)]
        ygs = [A.alloc([4, 512], BF16) for _ in range(2)]
        wf = [A.alloc([4, 512], BF16) for _ in range(2)]
        fT = [A.alloc([4, 512], BF16) for _ in range(2)]
        A.mark()
        aT = [A.alloc([4, T], BF16) for _ in range(2)]
        ucs = A.alloc([2, 16, 512], BF16)
        clt = [A.alloc([2, 16, 512], BF16) for _ in range(2)]
        ucs_keys = [("ucs", m, lt) for m in range(2) for lt in range(16)]
        for g in range(4):
            sa = g % 2
            P.op("sp", lambda e, sa=sa, g=g: e.dma_start(out=aT[sa], in_=A0[g].rearrange("(c p) t -> p c t", p=128)),
                 writes=[("aT", sa)], dma_key=("aT", sa))
            P.op("pool", lambda e, sa=sa, g=g: e.dma_start(out=wf[sa], in_=w_fou[g].rearrange("(c p) n -> p c n", p=128)),
                 writes=[("wf", sa)], dma_key=("wf", sa))
            for lt in range(16):
                for m in range(2):
                    bank = nbank()
                    tw = ccs if m == 0 else scs

                    def mm(e, sa=sa, lt=lt, tw=tw, bank=bank):
                        for cc in range(4):
                            ins = e.matmul(ps[:, bank, :], aT[sa][:, cc, lt * 128:(lt + 1) * 128], tw[:, cc, :],
                                           start=(cc == 0), stop=(cc == 3))
                        return ins
                    P.op("pe", mm, reads=[("aT", sa), "ccs", "scs"], writes=[("ps", bank)])
                    evac(ucs[:, m, lt, :], bank, [("ucs", m, lt)])
            for tt in range(4):
                sc_ = (g * 4 + tt) % 2
                P.op("sp", lambda e, sc_=sc_, tt=tt: e.dma_start(
                    out=clt[sc_][:, 0], in_=cl_m[:, tt * 512:(tt + 1) * 512].rearrange("(c p) t -> p c t", p=128)),
                    writes=[("clt", sc_, 0)], dma_key=("clt", sc_, 0))
                P.op("sp", lambda e, sc_=sc_, tt=tt: e.dma_start(
                    out=clt[sc_][:, 1], in_=sl_m[:, tt * 512:(tt + 1) * 512].rearrange("(c p) t -> p c t", p=128)),
                    writes=[("clt", sc_, 1)], dma_key=("clt", sc_, 1))
                P.op("sp", lambda e, sc_=sc_, tt=tt, g=g: e.dma_start(
                    out=sgt[sc_], in_=SG[g * 512:(g + 1) * 512, tt * 512:(tt + 1) * 512].rearrange("(c p) t -> p c t", p=128)),
                    writes=[("sgt", sc_)], dma_key=("sgt", sc_))
                for ck in range(4):
                    bank = nbank()

                    def mm(e, sc_=sc_, ck=ck, bank=bank):
                        n = 0
                        for m in range(2):
                            for lc in range(16):
                                ins = e.matmul(ps[:, bank, :], ucs[:, m, lc, ck * 128:(ck + 1) * 128], clt[sc_][:, m, lc, :],
                                               start=(n == 0), stop=(n == 31))
                                n += 1
                        return ins
                    P.op("pe", mm, reads=ucs_keys + [("clt", sc_, 0), ("clt", sc_, 1)], writes=[("ps", bank)])
                    evac(fT[sc_][:, ck, :], bank, [("fT", sc_, ck)])
                for dk in range(4):
                    bank = nbank()

                    def mm(e, sa=sa, sc_=sc_, dk=dk, bank=bank):
                        for ck in range(4):
                            ins = e.matmul(ps[:, bank, :], wf[sa][:, ck, dk * 128:(dk + 1) * 128], fT[sc_][:, ck, :],
                                           start=(ck == 0), stop=(ck == 3))
                        return ins
                    P.op("pe", mm, reads=[("fT", sc_, ck) for ck in range(4)] + [("wf", sa)], writes=[("ps", bank)])
                    P.op("dve", lambda e, sc_=sc_, dk=dk, bank=bank: e.tensor_tensor(
                        out=ygs[sc_][:, dk, :], in0=ps[:, bank, :], in1=sgt[sc_][:, dk, :], op=ALU.mult),
                        reads=[("ps", bank), ("sgt", sc_)], writes=[("ygs", sc_, dk)])
                P.op("act", lambda e, sc_=sc_, g=g, tt=tt: e.dma_start(
                    out=YG[g * 512:(g + 1) * 512, tt * 512:(tt + 1) * 512].rearrange("(c p) t -> p c t", p=128), in_=ygs[sc_]),
                    reads=[("ygs", sc_, dk) for dk in range(4)], dma_key=("ygst", sc_))
        P.barrier()
        A.release()
        A.mark()
        bT = [A.alloc([16, 512], BF16) for _ in range(2)]
        bnd = [A.alloc([4, 6, 512], BF16) for _ in range(2)]
        for g in range(4):
            sa = g % 2
            P.op("sp", lambda e, sa=sa, g=g: e.dma_start(out=bT[sa], in_=B0[g].rearrange("(t p) c -> p t c", p=128)),
                 writes=[("bT", sa)], dma_key=("bT", sa))
            for a in range(4):
                P.op("sp", lambda e, sa=sa, g=g, a=a: e.dma_start(out=bnd[sa][:, a], in_=band_m[g, a].rearrange("j p t -> p j t")),
                     writes=[("bnd", sa, a)], dma_key=("bnd", sa, a))
            P.op("pool", lambda e, sa=sa, g=g: e.dma_start(out=wf[sa], in_=w_pool[g].rearrange("(c p) n -> p c n", p=128)),
                 writes=[("wf", sa)], dma_key=("wf", sa))
            for tt in range(4):
                sc_ = (g * 4 + tt) % 2
                P.op("sp", lambda e, sc_=sc_, tt=tt, g=g: e.dma_start(
                    out=sgt[sc_], in_=SG[2048 + g * 512:2048 + (g + 1) * 512, tt * 512:(tt + 1) * 512].rearrange("(c p) t -> p c t", p=128)),
                    writes=[("sgt", sc_)], dma_key=("sgt", sc_))
                for ck in range(4):
                    bank = nbank()

                    def mm(e, sa=sa, tt=tt, ck=ck, bank=bank):
                        for j in range(6):
                            lt = min(max(4 * tt - 1 + j, 0), 15)
                            ins = e.matmul(ps[:, bank, :], bT[sa][:, lt, ck * 128:(ck + 1) * 128], bnd[sa][:, tt, j, :],
                                           start=(j == 0), stop=(j == 5))
                        return ins
                    P.op("pe", mm, reads=[("bT", sa), ("bnd", sa, tt)], writes=[("ps", bank)])
                    evac(fT[sc_][:, ck, :], bank, [("fT", sc_, ck)])
                for dk in range(4):
                    bank = nbank()

                    def mm(e, sa=sa, sc_=sc_, dk=dk, bank=bank):
                        for ck in range(4):
                            ins = e.matmul(ps[:, bank, :], wf[sa][:, ck, dk * 128:(dk + 1) * 128], fT[sc_][:, ck, :],
                                           start=(ck == 0), stop=(ck == 3))
                        return ins
                    P.op("pe", mm, reads=[("fT", sc_, ck) for ck in range(4)] + [("wf", sa)], writes=[("ps", bank)])
                    P.op("dve", lambda e, sc_=sc_, dk=dk, bank=bank, g=g: e.scalar_tensor_tensor(
                        out=ygs[sc_][:, dk, :], in0=ps[:, bank, :], scalar=psc[:, g * 4 + dk:g * 4 + dk + 1],
                        in1=sgt[sc_][:, dk, :], op0=ALU.mult, op1=ALU.mult),
                        reads=[("ps", bank), ("sgt", sc_), "psc"], writes=[("ygs", sc_, dk)])
                P.op("act", lambda e, sc_=sc_, g=g, tt=tt: e.dma_start(
                    out=YG[2048 + g * 512:2048 + (g + 1) * 512, tt * 512:(tt + 1) * 512].rearrange("(c p) t -> p c t", p=128), in_=ygs[sc_]),
                    reads=[("ygs", sc_, dk) for dk in range(4)], dma_key=("ygst", sc_))
        P.barrier()
        A.release()
        A.release()

    def phase_out(W, xsrc, dst, layer, yg):
        A.mark()
        gate_b = A.alloc([D], F32)
        xs = [A.alloc([4, 256], F32) for _ in range(2)]
        os_ = [A.alloc([4, 256], F32) for _ in range(2)]
        for q in range(4):
            P.op("sp", lambda e, q=q: e.dma_start(
                out=yg[:, q * 8:(q + 1) * 8, :], in_=YG[q * 1024:(q + 1) * 1024, :].rearrange("(c p) t -> p c t", p=128)),
                dma_key=("ygld", q))
        P.op("sp", lambda e: e.dma_start(out=gate_b, in_=mod_d[layer, 2 * D:3 * D].partition_broadcast(128)), dma_key="gate_b")
        P.barrier()
        jobs = []
        cnt = {"n": 0}
        for j in range(16):
            col = j * 256

            def epi(i, bank, col=col):
                q = i // 4
                if i % 4 == 0:
                    cnt["n"] += 1
                s = cnt["n"] % 2
                if i % 4 == 0:
                    P.op("sp", lambda e, s=s, q=q, col=col: e.dma_start(
                        out=xs[s], in_=xsrc[q * 512:(q + 1) * 512, col:col + 256].rearrange("(t p) c -> p t c", p=128)),
                        writes=[("xs", s)], dma_key=("xs", s))
                P.op("dve", lambda e, s=s, i=i, bank=bank, col=col: e.tensor_tensor(
                    out=os_[s][:, i % 4, :], in0=ps[:, bank, 0:256], in1=gate_b[:, col:col + 256], op=ALU.mult),
                    reads=[("ps", bank)], writes=[("os", s, i % 4)])
                P.op("dve", lambda e, s=s, i=i: e.tensor_tensor(
                    out=os_[s][:, i % 4, :], in0=os_[s][:, i % 4, :], in1=xs[s][:, i % 4, :], op=ALU.add),
                    reads=[("xs", s), ("os", s, i % 4)], writes=[("os", s, i % 4)])
                if i % 4 == 3:
                    P.op("act", lambda e, s=s, q=q, col=col: e.dma_start(
                        out=dst[q * 512:(q + 1) * 512, col:col + 256].rearrange("(t p) c -> p t c", p=128), in_=os_[s]),
                        reads=[("os", s, k) for k in range(4)], dma_key=("osst", s))
            jobs.append((col, "tm", epi))
        gemm_phase(W, jobs, yg, nslots=2)
        A.release()

    def phase_l1_in(hT):
        A.mark()
        gq = A.alloc([128], F32)
        gk = A.alloc([128], F32)
        P.op("sp", lambda e: e.dma_start(out=gq, in_=qg_d.partition_broadcast(128)), dma_key="gq")
        P.op("sp", lambda e: e.dma_start(out=gk, in_=kg_d.partition_broadcast(128)), dma_key="gk")
        P.barrier()
        junkq = A.alloc([128], BF16)
        ssq = [A.alloc([2], F32) for _ in range(4)]
        rq = [A.alloc([2], F32) for _ in range(4)]
        qst = [A.alloc([4, 256], BF16) for _ in range(2)]
        kst = [A.alloc([4, 256], F32) for _ in range(2)]
        cnt = {"n": 0, "g": 0}

        def make_qk(jq, is_k):
            col_out = jq * 256
            gb = gk if is_k else gq

            def epi(i, bank):
                s2 = cnt["n"] % 4
                cnt["n"] += 1
                if i % 4 == 0:
                    cnt["g"] += 1
                s = cnt["g"] % 2
                q4 = i // 4
                for hd in range(2):
                    P.op("act", lambda e, hd=hd, s2=s2, bank=bank: e.activation(
                        out=junkq, in_=ps[:, bank, hd * 128:(hd + 1) * 128], func=AF.Square, accum_out=ssq[s2][:, hd:hd + 1]),
                        reads=[("ps", bank)], writes=["junkq", ("ssq", s2, hd)])
                P.op("act", lambda e, s2=s2: e.activation(out=ssq[s2], in_=ssq[s2], func=AF.Sqrt, scale=1.0 / 128, bias=EPS),
                     reads=[("ssq", s2, 0), ("ssq", s2, 1)], writes=[("ssq", s2, 0), ("ssq", s2, 1)])
                P.op("dve", lambda e, s2=s2: e.reciprocal(out=rq[s2], in_=ssq[s2]),
                     reads=[("ssq", s2, 0), ("ssq", s2, 1)], writes=[("rq", s2)])
                for hd in range(2):
                    if is_k:
                        o = kst[s][:, i % 4, hd * 128:(hd + 1) * 128]
                        wk = ("kst", s, i % 4, hd)
                    else:
                        o = qst[s][:, i % 4, hd * 128:(hd + 1) * 128]
                        wk = ("qst", s, i % 4, hd)
                    P.op("dve", lambda e, o=o, hd=hd, s2=s2, bank=bank: e.scalar_tensor_tensor(
                        out=o, in0=ps[:, bank, hd * 128:(hd + 1) * 128], scalar=rq[s2][:, hd:hd + 1], in1=gb,
                        op0=ALU.mult, op1=ALU.mult), reads=[("ps", bank), ("rq", s2)], writes=[wk])
                if is_k:
                    P.op("act", lambda e, s=s, i=i: e.activation(out=qst[s][:, i % 4, :], in_=kst[s][:, i % 4, :], func=AF.Copy),
                         reads=[("kst", s, i % 4, 0), ("kst", s, i % 4, 1)], writes=[("qst", s, i % 4, 0), ("qst", s, i % 4, 1)])
                if i % 4 == 3:
                    dstb = (Ks if is_k else Qs)[q4 * 512:(q4 + 1) * 512, col_out:col_out + 256].rearrange("(t p) c -> p t c", p=128)
                    P.op("act", lambda e, s=s, dstb=dstb: e.dma_start(out=dstb, in_=qst[s]),
                         reads=[("qst", s, k4, hd) for k4 in range(4) for hd in range(2)], dma_key=("qstst", s))
                    if is_k:
                        for hd in range(2):
                            h = jq * 2 + hd
                            d = sk_d[h, q4 * 512:(q4 + 1) * 512, :].rearrange("(t p) d -> p t d", p=128)
                            P.op("act", lambda e, s=s, d=d, hd=hd: e.dma_start(out=d, in_=kst[s][:, :, hd * 128:(hd + 1) * 128]),
                                 reads=[("kst", s, k4, hd) for k4 in range(4)], dma_key=("kstst", s, hd))
            return epi
        jobs = []
        PARTS = DBG.get("l1_parts", ["q", "k", "v", "glu", "gate"])
        NJ = DBG.get("l1_nj", 8)
        for jq in range(NJ if "q" in PARTS else 0):
            jobs.append((jq * 256, "tm", make_qk(jq, False)))
        for jq in range(NJ if "k" in PARTS else 0):
            jobs.append((2048 + jq * 256, "tm", make_qk(jq, True)))

        def make_v(jq):
            def epi(i, bank):
                if i % 4 == 0:
                    cnt["g"] += 1
                s = cnt["g"] % 2
                q4 = i // 4
                P.op("dve", lambda e, s=s, i=i, bank=bank: e.tensor_copy(out=kst[s][:, i % 4, :], in_=ps[:, bank, 0:256]),
                     reads=[("ps", bank)], writes=[("kst", s, i % 4, 0), ("kst", s, i % 4, 1)])
                P.op("act", lambda e, s=s, i=i, bank=bank: e.activation(out=qst[s][:, i % 4, :], in_=ps[:, bank, 0:256], func=AF.Copy),
                     reads=[("ps", bank)], writes=[("qst", s, i % 4, 0), ("qst", s, i % 4, 1)])
                if i % 4 == 3:
                    dstb = Vs[q4 * 512:(q4 + 1) * 512, jq * 256:(jq + 1) * 256].rearrange("(t p) c -> p t c", p=128)
                    P.op("act", lambda e, s=s, dstb=dstb: e.dma_start(out=dstb, in_=qst[s]),
                         reads=[("qst", s, k4, hd) for k4 in range(4) for hd in range(2)], dma_key=("qstst", s))
                    for hd in range(2):
                        h = jq * 2 + hd
                        d = sv_d[h, q4 * 512:(q4 + 1) * 512, :].rearrange("(t p) d -> p t d", p=128)
                        P.op("act", lambda e, s=s, d=d, hd=hd: e.dma_start(out=d, in_=kst[s][:, :, hd * 128:(hd + 1) * 128]),
                             reads=[("kst", s, k4, hd) for k4 in range(4)], dma_key=("kstst", s, hd))
            return epi
        for jq in range(NJ if "v" in PARTS else 0):
            jobs.append((4096 + jq * 256, "tm", make_v(jq)))
        gemm_phase(w_in1, jobs, hT)
        A.release()

        A.mark()
        zt = A.alloc([16], BF16)
        P.op("dve", lambda e: e.memset(zt, 0.0), writes=["zt"])
        for cq in range(16):
            P.op("sp", lambda e, cq=cq: e.dma_start(out=Gs[cq * 128:(cq + 1) * 128, 0:15], in_=zt[:, 0:15]),
                 reads=["zt"], dma_key="zt0")
            P.op("sp", lambda e, cq=cq: e.dma_start(out=Gs[cq * 128:(cq + 1) * 128, T + 15:T + 30], in_=zt[:, 0:15]),
                 reads=["zt"], dma_key="zt1")
        sig = [A.alloc([T], BF16) for _ in range(2)]
        stG = [A.alloc([T], BF16) for _ in range(2)]
        jobs = []
        for jb in range(NJ if "glu" in PARTS else 0):
            jobs.append((6144 + 2048 + jb * 256, "fm", make_fm_store(sig, lambda half: None, AF.Sigmoid)))
            jobs.append((6144 + jb * 256, "fm", make_fm_store(
                stG, lambda half, jb=jb: Gs[jb * 256 + half * 128: jb * 256 + (half + 1) * 128, 15:T + 15], None,
                in1=lambda half: (sig[half], ("stg", id(sig), half)))))
        gemm_phase(w_in1, jobs, hT)
        A.release()

        A.mark()
        stS = [A.alloc([T], BF16) for _ in range(3)]
        jobs = []
        for j in range(2 * NJ if "gate" in PARTS else 0):
            jobs.append((10240 + j * 256, "fm", make_fm_store(
                stS, lambda half, j=j: SG[j * 256 + half * 128: j * 256 + (half + 1) * 128, :], AF.Silu)))
        gemm_phase(w_in1, jobs, hT)
        A.release()

    def phase_l1_attn():
        A.mark()
        mk = A.alloc([6, 640], F32)
        ctxb = A.alloc([1], F32)
        P.op("sp", lambda e: e.dma_start(out=mk, in_=maskadd), dma_key="mk")
        P.op("sp", lambda e: e.dma_start(out=ctxb, in_=ctxbias_d), dma_key="ctxb")
        P.barrier()
        qtm = [A.alloc([16, 128], BF16) for _ in range(2)]
        ktm = [A.alloc([16, 128], BF16) for _ in range(2)]
        vtm = [A.alloc([16, 128], BF16) for _ in range(2)]
        kctm = [A.alloc([2, 128], BF16) for _ in range(2)]
        vctm = [A.alloc([2, 128], BF16) for _ in range(2)]
        bia = [A.alloc([1152], F32) for _ in range(2)]
        sgh = [A.alloc([T], BF16) for _ in range(2)]
        qT = [A.alloc([T], BF16) for _ in range(2)]
        kT = [A.alloc([T], BF16) for _ in range(2)]
        kcT = [A.alloc([256], BF16) for _ in range(2)]
        bm = [A.alloc([6, 640], BF16) for _ in range(2)]
        ctxm = A.alloc([256], BF16)
        pTg = [A.alloc([7, 4, 128], BF16) for _ in range(2)]
        rs = [A.alloc([512], F32) for _ in range(2)]
        on = [A.alloc([512], F32) for _ in range(2)]
        ygst = [A.alloc([T], BF16) for _ in range(2)]
        scale = 128.0 ** -0.5
        P.op("dve", lambda e: e.tensor_scalar(out=mk, in0=mk, scalar1=1.0 / scale, scalar2=None, op0=ALU.mult), writes=["mk"])
        zc = A.alloc([256], F32)
        P.op("dve", lambda e: e.memset(zc, 0.0), writes=["zc"])
        P.op("dve", lambda e: e.tensor_scalar(out=ctxm, in0=zc, scalar1=ctxb[:, 0:1], scalar2=1.0 / scale, op0=ALU.add, op1=ALU.mult),
             reads=["zc"], writes=["ctxm"])
        P.barrier()
        cn = {"q": 0}

        def preamble(h):
            s = h % 2
            tmrows = lambda Z: Z[:, h * 128:(h + 1) * 128].rearrange("(t p) d -> p t d", p=128)
            P.op("sp", lambda e, a=tmrows(Qs): e.dma_start(out=qtm[s], in_=a), writes=[("qtm", s)], dma_key=("qtm", s))
            P.op("sp", lambda e, a=tmrows(Ks): e.dma_start(out=ktm[s], in_=a), writes=[("ktm", s)], dma_key=("ktm", s))
            P.op("sp", lambda e, a=tmrows(Vs): e.dma_start(out=vtm[s], in_=a), writes=[("vtm", s)], dma_key=("vtm", s))
            P.op("pool", lambda e: e.dma_start(out=kctm[s], in_=cache_k[h].rearrange("(t p) d -> p t d", p=128)),
                 writes=[("kctm", s)], dma_key=("kctm", s))
            P.op("pool", lambda e: e.dma_start(out=vctm[s], in_=cache_v[h].rearrange("(t p) d -> p t d", p=128)),
                 writes=[("vctm", s)], dma_key=("vctm", s))
            P.op("sp", lambda e: e.dma_start(out=bia[s], in_=bias_t[h]), writes=[("bia", s)], dma_key=("bia", s))
            P.op("sp", lambda e: e.dma_start(out=sgh[s], in_=SG[h * 128:(h + 1) * 128, :]), writes=[("sgh", s)], dma_key=("sgh", s))
            for (src, dst, nt, rk, wkk) in ((qtm, qT, 16, "qtm", "qT"), (ktm, kT, 16, "ktm", "kT"), (kctm, kcT, 2, "kctm", "kcT")):
                for g8 in range((nt + 3) // 4):
                    bank = 6 + (cn["q"] % 2)
                    cn["q"] += 1
                    n8 = min(4, nt - g8 * 4)

                    def tr(e, src=src, g8=g8, n8=n8, bank=bank):
                        for j in range(n8):
                            ins = e.transpose(out=psb(bank)[:, j * 128:(j + 1) * 128], in_=src[s][:, g8 * 4 + j, :], identity=ident)
                        return ins
                    P.op("pe", tr, reads=[(rk, s)], writes=[("ps", bank)])
                    evac(dst[s][:, g8 * 512:g8 * 512 + n8 * 128], bank, [(wkk, s, g8)], src=psb(bank)[:, 0:n8 * 128])
            for v in range(6):
                o_v = _VREP[v] - _base(_VREP[v])
                P.op("dve", lambda e, v=v, o_v=o_v: e.scalar_tensor_tensor(
                    out=bm[s][:, v, :], in0=bia[s][:, (4 - o_v) * 128:(9 - o_v) * 128], scalar=1.0 / scale, in1=mk[:, v, :],
                    op0=ALU.mult, op1=ALU.add),
                    reads=[("bia", s)], writes=[("bm", s, v)])

        def scores(n, h, i):
            s = h % 2
            b = _base(i)
            sp2 = n % 2
            bA, bB = sp2 * 2, sp2 * 2 + 1
            psA = ps[:, bA, :].rearrange("p (j q) -> p j q", j=4)
            psB = ps[:, bB, :].rearrange("p (j q) -> p j q", j=4)

            v = _variant(i)

            def sc_mm(e):
                for j in range(5):
                    o = psA[:, j, :] if j < 4 else psB[:, 0, :]
                    e.matmul(o, kT[s][:, (b + j) * 128:(b + j + 1) * 128], qT[s][:, i * 128:(i + 1) * 128],
                             start=True, stop=False)
                    ins = e.matmul(o, ident, bm[s][:, v, j * 128:(j + 1) * 128], start=False, stop=True)
                for j in range(2):
                    e.matmul(psB[:, 1 + j, :], kcT[s][:, j * 128:(j + 1) * 128], qT[s][:, i * 128:(i + 1) * 128],
                             start=True, stop=False)
                    ins = e.matmul(psB[:, 1 + j, :], ident, ctxm[:, j * 128:(j + 1) * 128], start=False, stop=True)
                return ins
            P.op("pe", sc_mm, reads=[("qT", s, g) for g in range(4)] + [("kT", s, g) for g in range(4)] + [("kcT", s, 0), ("bm", s, v)],
                 writes=[("ps", bA), ("ps", bB)])

        def rest(n, h, i):
            s = h % 2
            v = _variant(i)
            b = _base(i)
            sp2 = n % 2
            sp3 = n % 3
            bA, bB = sp2 * 2, sp2 * 2 + 1
            i4, ii = i // 4, i % 4
            so = (h * 4 + i4) % 2
            P.op("act", lambda e: e.activation(out=pTg[so][:, 0:4, ii, :], in_=ps[:, bA, :].rearrange("p (j q) -> p j q", j=4),
                                               func=AF.Exp, scale=scale),
                 reads=[("ps", bA)], writes=[("pT", so, ii, 0)])
            P.op("act", lambda e: e.activation(out=pTg[so][:, 4:7, ii, :], in_=ps[:, bB, 0:384].rearrange("p (j q) -> p j q", j=3),
                                               func=AF.Exp, scale=scale),
                 reads=[("ps", bB)], writes=[("pT", so, ii, 1)])

            def pv_mm(e):
                for j in range(7):
                    lhs = vtm[s][:, b + j, :] if j < 5 else vctm[s][:, j - 5, :]
                    ins = e.matmul(ps[:, 4, ii * 128:(ii + 1) * 128], lhs, pTg[so][:, j, ii, :], start=(j == 0), stop=(j == 6))
                return ins
            P.op("pe", pv_mm, reads=[("pT", so, ii, 0), ("pT", so, ii, 1), ("vtm", s), ("vctm", s)],
                 writes=[("ps", 4, ii)])
            if ii == 3:
                def sum_mm(e):
                    for j in range(7):
                        ins = e.matmul(ps[:, 5, :], onesb, pTg[so][:, j, :, :].rearrange("p a q -> p (a q)"), start=(j == 0), stop=(j == 6))
                    return ins
                P.op("pe", sum_mm, reads=[("pT", so, k, hf) for k in range(4) for hf in range(2)],
                     writes=[("ps", 5, k) for k in range(4)])
            if ii == 3:
                okeys = [("ps", 4, k) for k in range(4)]
                skeys = [("ps", 5, k) for k in range(4)]
                P.op("dve", lambda e: e.reciprocal(out=rs[so], in_=ps[:, 5, :]), reads=skeys, writes=[("rs", so)])
                P.op("dve", lambda e: e.tensor_tensor(out=on[so], in0=ps[:, 4, :], in1=rs[so], op=ALU.mult),
                     reads=okeys + [("rs", so)], writes=[("on", so)])
                P.op("dve", lambda e: e.tensor_tensor(
                    out=ygst[s][:, i4 * 512:(i4 + 1) * 512], in0=on[so], in1=sgh[s][:, i4 * 512:(i4 + 1) * 512], op=ALU.mult),
                    reads=[("on", so), ("sgh", s)], writes=[("ygst", s, i4)])
            if i == 15:
                P.op("act", lambda e: e.dma_start(out=YG[h * 128:(h + 1) * 128, :], in_=ygst[s]),
                     reads=[("ygst", s, k) for k in range(4)], dma_key=("ygstst", s))

        tiles = [(h, i) for h in range(16) for i in range(16)]
        for n, (h, i) in enumerate(tiles):
            if i == 0:
                preamble(h)
            scores(n, h, i)
            if n >= 1:
                rest(n - 1, *tiles[n - 1])
        rest(len(tiles) - 1, *tiles[-1])
        P.barrier()
        A.release()

    def phase_l1_conv():
        A.mark()
        flag = A.alloc([1], F32)
        dwt = A.alloc([16, 31], F32)
        dwb = A.alloc([16], F32)
        lng = A.alloc([16], F32)
        lnb = A.alloc([16], F32)
        wpw = A.alloc([16, 2048], BF16)
        P.op("sp", lambda e: e.dma_start(out=flag, in_=flag_d), dma_key="c0")
        P.op("sp", lambda e: e.dma_start(out=dwt, in_=dw_l), dma_key="c1")
        P.op("sp", lambda e: e.dma_start(out=dwb, in_=dwb_l), dma_key="c2")
        P.op("sp", lambda e: e.dma_start(out=lng, in_=lng_l), dma_key="c3")
        P.op("sp", lambda e: e.dma_start(out=lnb, in_=lnb_l), dma_key="c4")
        for q in range(4):
            P.op("pool", lambda e, q=q: e.dma_start(
                out=wpw[:, q * 4:(q + 1) * 4, :], in_=w_pw[q * 512:(q + 1) * 512, :].rearrange("(c p) n -> p c n", p=128)),
                dma_key=("wpw", q))
        P.barrier()
        Gp = [A.alloc([2, 286], BF16) for _ in range(3)]
        cv = A.alloc([16, 512], F32)
        dg = [A.alloc([31, 128], BF16) for _ in range(2)]
        sq = [A.alloc([512], F32) for _ in range(2)]
        mean = A.alloc([512], F32)
        rstd = A.alloc([512], F32)
        tmp = A.alloc([512], F32)
        t1 = [A.alloc([512], F32) for _ in range(2)]
        zT = A.alloc([16, 512], BF16)
        sgt = A.alloc([16, 512], BF16)
        ygs = A.alloc([16, 512], BF16)
        W2 = T + 30
        n = 0
        for tt in range(4):
            P.op("sp", lambda e, tt=tt: e.dma_start(
                out=sgt, in_=SG[2048:4096, tt * 512:(tt + 1) * 512].rearrange("(c p) t -> p c t", p=128)),
                writes=["sgt"], dma_key="sgt")
            pend = []
            for cc in range(16):
                n += 1
                s3 = n % 3
                s2 = n % 2
                src = bass.AP(Gs.tensor, cc * 128 * W2 + tt * 512, [[W2, 128], [256, 2], [1, 286]])
                P.op("sp", lambda e, s3=s3, src=src: e.dma_start(out=Gp[s3], in_=src), writes=[("Gp", s3)], dma_key=("Gp", s3))
                P.op("dve", lambda e, s3=s3: e.tensor_scalar(out=Gp[s3][:, :, 0:15], in0=Gp[s3][:, :, 0:15],
                                                           scalar1=flag[:, 0:1], scalar2=None, op0=ALU.mult),
                     reads=[("Gp", s3)], writes=[("Gp", s3)])
                P.op("dve", lambda e, s3=s3: e.tensor_scalar(out=Gp[s3][:, :, 271:286], in0=Gp[s3][:, :, 271:286],
                                                           scalar1=flag[:, 0:1], scalar2=None, op0=ALU.mult),
                     reads=[("Gp", s3)], writes=[("Gp", s3)])
                dgs = dg[n % 2]
                in0b = bass.AP(ident.tensor, ident.offset, [list(ident.ap[0]), [0, 31], [1, 128]])
                dsl = dwt[:, cc, :]
                in1b = bass.AP(dsl.tensor, dsl.offset, [list(dsl.ap[0]), [1, 31], [0, 128]])
                P.op("dve", lambda e, dgs=dgs, in0b=in0b, in1b=in1b: e.tensor_tensor(out=dgs, in0=in0b, in1=in1b, op=ALU.mult),
                     reads=["ident"], writes=[("dg", n % 2)])
                cbank = 2 + (n % 2)

                def cv_mm(e, s3=s3, dgs=dgs, cbank=cbank):
                    for k in range(31):
                        ins = e.matmul(ps[:, cbank, :].rearrange("p (s t) -> p s t", s=2), dgs[:, k, :], Gp[s3][:, :, k:k + 256],
                                       start=(k == 0), stop=(k == 30))
                    return ins
                P.op("pe", cv_mm, reads=[("Gp", s3), ("dg", n % 2)], writes=[("ps", cbank)])
                P.op("act", lambda e, cc=cc, cbank=cbank: e.activation(out=cv[:, cc, :], in_=ps[:, cbank, :], func=AF.Identity,
                                                                     bias=dwb[:, cc:cc + 1]),
                     reads=[("ps", cbank)], writes=[("cv", cc)])
                P.op("act", lambda e, s2=s2, cc=cc: e.activation(out=sq[s2], in_=cv[:, cc, :], func=AF.Square),
                     reads=[("cv", cc)], writes=[("sq", s2)])

                def st_mm(e, cc=cc, s2=s2):
                    e.matmul(ps[:, 0, :], onesf, cv[:, cc, :], start=(cc == 0), stop=(cc == 15))
                    return e.matmul(ps[:, 1, :], onesf, sq[s2], start=(cc == 0), stop=(cc == 15))
                pend.append((st_mm, [("cv", cc), ("sq", s2)]))
                if len(pend) > 1:
                    f, rd = pend.pop(0)
                    P.op("pe", f, reads=rd, writes=[("ps", 0), ("ps", 1)])
            while pend:
                f, rd = pend.pop(0)
                P.op("pe", f, reads=rd, writes=[("ps", 0), ("ps", 1)])
            P.op("dve", lambda e: e.tensor_scalar(out=mean, in0=ps[:, 0, :], scalar1=1.0 / 2048, scalar2=None, op0=ALU.mult),
                 reads=[("ps", 0)], writes=["mean"])
            P.op("dve", lambda e: e.tensor_tensor(out=tmp, in0=mean, in1=mean, op=ALU.mult), reads=["mean"], writes=["tmp"])
            P.op("dve", lambda e: e.scalar_tensor_tensor(out=rstd, in0=ps[:, 1, :], scalar=1.0 / 2048, in1=tmp,
                                                         op0=ALU.mult, op1=ALU.subtract),
                 reads=[("ps", 1), "tmp"], writes=["rstd"])
            P.op("act", lambda e: e.activation(out=rstd, in_=rstd, func=AF.Sqrt, scale=1.0, bias=EPS), reads=["rstd"], writes=["rstd"])
            P.op("dve", lambda e: e.reciprocal(out=rstd, in_=rstd), reads=["rstd"], writes=["rstd"])
            for cc in range(16):
                s2 = cc % 2
                P.op("dve", lambda e, s2=s2, cc=cc: e.tensor_tensor(out=t1[s2], in0=cv[:, cc, :], in1=mean, op=ALU.subtract),
                     reads=[("cv", cc), "mean"], writes=[("t1", s2)])
                P.op("dve", lambda e, s2=s2: e.tensor_tensor(out=t1[s2], in0=t1[s2], in1=rstd, op=ALU.mult),
                     reads=[("t1", s2), "rstd"], writes=[("t1", s2)])
                P.op("act", lambda e, s2=s2, cc=cc: e.activation(out=zT[:, cc, :], in_=t1[s2], func=AF.Silu,
                                                               scale=lng[:, cc:cc + 1], bias=lnb[:, cc:cc + 1]),
                     reads=[("t1", s2)], writes=[("zT", cc)])
            for nk in range(16):
                bank = 4 + (nk % 4)

                def pw_mm(e, nk=nk, bank=bank):
                    for kc in range(16):
                        ins = e.matmul(ps[:, bank, :], wpw[:, kc, nk * 128:(nk + 1) * 128], zT[:, kc, :],
                                       start=(kc == 0), stop=(kc == 15))
                    return ins
                P.op("pe", pw_mm, reads=[("zT", kc) for kc in range(16)], writes=[("ps", bank)])
                P.op("dve", lambda e, nk=nk, bank=bank: e.tensor_tensor(out=ygs[:, nk, :], in0=ps[:, bank, :], in1=sgt[:, nk, :], op=ALU.mult),
                     reads=[("ps", bank), "sgt"], writes=[("ygs", nk)])
            P.op("act", lambda e, tt=tt: e.dma_start(
                out=YG[2048:4096, tt * 512:(tt + 1) * 512].rearrange("(c p) t -> p c t", p=128), in_=ygs),
                reads=[("ygs", nk) for nk in range(16)], dma_key="ygsst")
        P.barrier()
        A.release()

    if has("adaln"):
        phase_adaln()
    if has("l0_norm") or has("l0_in"):
        A.mark()
        hT = A.alloc([32, T], BF16)
        if has("l0_norm"):
            phase_norm(0, x_d, hT)
            if DBG.get("dump_hT"):
                hdbg = dout("hT_dbg", [D, T], BF16)
                P.op("sp", lambda e: e.dma_start(out=hdbg.rearrange("(c p) t -> p c t", p=128), in_=hT), dma_key="hdbg")
                P.barrier()
        if has("l0_in"):
            phase_l0_in(hT)
        A.release()
    if has("l0_mix"):
        phase_l0_mix()
    if has("l0_out"):
        A.mark()
        yg = A.alloc([32, T], BF16)
        phase_out(w_out0, x_d, X1, 0, yg)
        A.release()
    if has("l1_norm") or has("l1_in"):
        A.mark()
        hT = A.alloc([32, T], BF16)
        if has("l1_norm"):
            phase_norm(1, X1, hT)
        if has("l1_in"):
            phase_l1_in(hT)
        A.release()
    if has("l1_attn"):
        phase_l1_attn()
    if has("l1_conv"):
        phase_l1_conv()
    if has("l1_out"):
        A.mark()
        yg = A.alloc([32, T], BF16)
        phase_out(w_out1, X1, y_d, 1, yg)
        A.release()
    P.final_wait("sp")
    P.emit()
    return nc


def _variant(i):
    return {0: 0, 1: 1, 14: 4, 15: 5}.get(i, 2 + (i % 2))


_VREP = {0: 0, 1: 1, 2: 2, 3: 3, 4: 14, 5: 15}


def _base(i):
    return min(max(i - 2, 0), 11)


def _const_tables(is_sample):
    Ls = 2048 if is_sample else 256
    c = np.arange(512)
    ang = 2.0 * np.pi * ((c[:, None] * c[None, :]) % 512) / 512.0
    cc = (np.cos(ang) / np.sqrt(512.0)).astype(NPBF)
    sc = (np.sin(ang) / np.sqrt(512.0)).astype(NPBF)
    l = np.arange(Ls)
    angl = 2.0 * np.pi * ((l[:, None] * l[None, :]) % Ls) / Ls
    cb = np.cos(angl) / np.sqrt(Ls)
    sb = -np.sin(angl) / np.sqrt(Ls)
    cl = np.zeros((T, T), np.float32)
    sl = np.zeros((T, T), np.float32)
    for s in range(T // Ls):
        cl[s * Ls:(s + 1) * Ls, s * Ls:(s + 1) * Ls] = cb
        sl[s * Ls:(s + 1) * Ls, s * Ls:(s + 1) * Ls] = sb
    band = np.zeros((4, 4, 6, 128, 512), np.float32)
    tl = np.arange(Ls)
    for g, win in enumerate((2, 4, 8, 16)):
        lo = np.maximum(tl - win // 2, 0)
        hi = np.minimum(tl + win // 2, Ls)
        cntv = (hi - lo).astype(np.float32)
        Mloc = np.zeros((Ls, Ls), np.float32)
        for t in range(Ls):
            Mloc[lo[t]:hi[t], t] = 1.0 / cntv[t]
            Mloc[t, t] -= 1.0
        M = np.zeros((T, T), np.float32)
        for s in range(T // Ls):
            M[s * Ls:(s + 1) * Ls, s * Ls:(s + 1) * Ls] = Mloc
        for tt in range(4):
            for j in range(6):
                lt = 4 * tt - 1 + j
                if 0 <= lt < 16:
                    band[g, tt, j] = M[lt * 128:(lt + 1) * 128, tt * 512:(tt + 1) * 512]
    mask = np.full((128, 6, 5, 128), NEG, np.float32)
    kl = np.arange(128)
    for v in range(6):
        i = _VREP[v]
        b = _base(i)
        for j in range(5):
            kt = b + j
            if is_sample:
                r = 2 * i + kl // 64
                qc = kl % 64
                kr = 2 * kt + kl // 64
                kc = kl % 64
                start = np.clip(r - 4, 0, 24)
                qs = np.clip(qc - 8, 0, 48)
                ok = ((kr[:, None] >= start[None, :]) & (kr[:, None] < start[None, :] + 8)
                      & (kc[:, None] >= qs[None, :]) & (kc[:, None] < qs[None, :] + 16))
                mask[:, v, j, :] = np.where(ok, 0.0, NEG)
            else:
                if kt // 2 == i // 2:
                    mask[:, v, j, :] = 0.0
    return dict(cc_m=cc, sc_m=sc, cl_m=cl.astype(NPBF), sl_m=sl.astype(NPBF), band_m=band.astype(NPBF),
                maskadd=np.ascontiguousarray(mask.reshape(128, 6, 640)),
                ctxbias=np.full((128, 1), 0.0 if is_sample else NEG, np.float32),
                flag=np.full((128, 1), 1.0 if is_sample else 0.0, np.float32),
                ident=np.eye(128, dtype=np.float32).astype(NPBF))


def _bias_table(rel_bias, is_sample):
    out = np.zeros((16, 128, 9, 128), np.float32)
    if is_sample:
        kl = np.arange(128)
        for o in range(-4, 5):
            dr = 2 * o + (kl[:, None] // 64) - (kl[None, :] // 64)
            dc = np.clip((kl[:, None] % 64) - (kl[None, :] % 64) + 15, 0, 30)
            okr = np.abs(dr) <= 7
            dri = np.clip(dr + 7, 0, 14)
            gathered = rel_bias[:, dri, dc]
            out[:, :, o + 4, :] = np.where(okr[None], gathered, 0.0)
    return np.ascontiguousarray(out.reshape(16, 128, 1152))


def _cols(v, n):
    return np.ascontiguousarray(np.asarray(v, np.float32).reshape(n, 128).T)


_CT_CACHE = {}


def prep_inputs(inp, core):
    is_sample = core >= 4
    if is_sample not in _CT_CACHE:
        _CT_CACHE[is_sample] = _const_tables(is_sample)
    ct = _CT_CACHE[is_sample]
    if is_sample:
        b = core - 4
        x = inp["x_sample"][b]
        cond = inp["c"][b]
        ck = inp["cache_k"][b, 0]
        cv = inp["cache_v"][b, 0]
    else:
        x = inp["x_prompt"][8 * core:8 * core + 8].reshape(T, D)
        cond = inp["c_ctx"]
        ck = np.zeros((16, 256, 128), np.float32)
        cv = np.zeros((16, 256, 128), np.float32)
    m = dict(ct)
    m.update(
        x=np.ascontiguousarray(x), cond_l=_cols(cond, 32),
        ng_l=np.stack([_cols(inp["norm_g"][0], 32), _cols(inp["norm_g"][1], 32)]),
        w_ada=inp["w_ada"], b_ada=inp["b_ada"],
        w_in_even=inp["w_in_even"][0], w_out_even=inp["w_out_even"][0],
        w_fourier=inp["w_fourier"][0], w_pool=inp["w_pool"][0], pscale_l=_cols(inp["pool_scale"][0], 16),
        w_in_odd=inp["w_in_odd"][0], w_out_odd=inp["w_out_odd"][0],
        q_norm_g=inp["q_norm_g"][0], k_norm_g=inp["k_norm_g"][0],
        bias_t=_bias_table(inp["rel_bias"][0], is_sample),
        cache_k=np.ascontiguousarray(ck), cache_v=np.ascontiguousarray(cv),
        dw_l=np.ascontiguousarray(inp["conv_dw"][0].reshape(31, 16, 128).transpose(2, 1, 0)),
        dwb_l=_cols(inp["conv_dw_b"][0], 16), lng_l=_cols(inp["conv_ln_g"][0], 16), lnb_l=_cols(inp["conv_ln_b"][0], 16),
        w_conv_pw=inp["w_conv_pw"][0],
    )
    return m


def kernel(**inputs):
    inp = {k: np.asarray(v) for k, v in inputs.items()}
    nc = build()
    in_maps = [prep_inputs(inp, c) for c in range(8)]
    res = run_bass_kernel_spmd(nc, in_maps, core_ids=list(range(8)))
    r = res.results
    y_prompt = np.concatenate([r[c]["y"].reshape(8, 256, D) for c in range(4)], axis=0)
    y_sample = np.stack([r[4 + b]["y"] for b in range(4)], axis=0)
    sk = np.concatenate([r[c]["sk"].reshape(16, 8, 256, 128).transpose(1, 0, 2, 3) for c in range(4)], axis=0)
    sv = np.concatenate([r[c]["sv"].reshape(16, 8, 256, 128).transpose(1, 0, 2, 3) for c in range(4)], axis=0)
    return (y_prompt.astype(np.float32), y_sample.astype(np.float32),
            np.ascontiguousarray(sk[:, None]).astype(np.float32), np.ascontiguousarray(sv[:, None]).astype(np.float32))
```
